# Optimizing a Trainium2 kernel written in Bass

```python
import jax, jax.numpy as jnp
from jax import lax
import numpy as np

D_MODEL = 1024
BATCH = 4
SEQ = 4096
DEPTH = 1

CTX_LEN = 256
GRID_W = 64
EPS = 1e-6
N_MOD = 9
D_FF = 2816
DN_HEADS = 4
DN_DK = 128
DN_DV = 128
DN_WIDTH = DN_HEADS * DN_DV
DN_CONV = 5
DN_CHUNK = 64
AT_HEADS = 4
AT_KV_HEADS = 2
AT_GROUP = AT_HEADS // AT_KV_HEADS
AT_HD = 128
AT_WIDTH = AT_HEADS * AT_HD
ATT_SCALE = AT_HD ** -0.5
Q_BLOCK = 128
ROPE_AXIS_DIM = AT_HD // 2
ROPE_THETA = 10000.0
D_MIX = DN_WIDTH + AT_WIDTH
LEN_DN_QKV = 3 * DN_WIDTH
LEN_DN_Z = DN_WIDTH
LEN_DN_B = 2 * DN_HEADS
LEN_DN_A = 2 * DN_HEADS
LEN_AT_Q = AT_WIDTH
LEN_AT_K = AT_KV_HEADS * AT_HD
LEN_AT_V = AT_KV_HEADS * AT_HD
OFF_DN_QKV = 0
OFF_DN_Z = OFF_DN_QKV + LEN_DN_QKV
OFF_DN_B = OFF_DN_Z + LEN_DN_Z
OFF_DN_A = OFF_DN_B + LEN_DN_B
OFF_AT_Q = OFF_DN_A + LEN_DN_A
OFF_AT_K = OFF_AT_Q + LEN_AT_Q
OFF_AT_V = OFF_AT_K + LEN_AT_K
P_IN = OFF_AT_V + LEN_AT_V

kernel_name = "hybrid_deltanet_gqa_macaron_prefix_block"


def _rmsnorm(x, gain):
    x32 = x.astype(jnp.float32)
    y = x32 * lax.rsqrt(jnp.mean(x32 * x32, axis=-1, keepdims=True) + EPS)
    return (y * gain.astype(jnp.float32)).astype(x.dtype)


def _l2norm(x):
    x32 = x.astype(jnp.float32)
    return x32 * lax.rsqrt(jnp.sum(x32 * x32, axis=-1, keepdims=True) + EPS)


def _modulate(x, gain, shift, scale):
    return _rmsnorm(x, gain) * (1 + scale) + shift


def _swiglu(h, w1, w3, w2):
    return (jax.nn.silu(h @ w1) * (h @ w3)) @ w2


def _centred_dwconv(x, w):
    k, ch = w.shape
    pad = (k - 1) // 2
    return lax.conv_general_dilated(
        x, w[:, None, :].astype(x.dtype), window_strides=(1,),
        padding=[(pad, k - 1 - pad)], dimension_numbers=('NWC', 'WIO', 'NWC'),
        feature_group_count=ch)


def _axial_rope_angles(n):
    rows = n // GRID_W
    row = jnp.repeat(jnp.arange(rows, dtype=jnp.int32), GRID_W).astype(jnp.float32)
    col = jnp.tile(jnp.arange(GRID_W, dtype=jnp.int32), rows).astype(jnp.float32)
    freqs = 1.0 / (ROPE_THETA ** (jnp.arange(0, ROPE_AXIS_DIM, 2, dtype=jnp.float32) / ROPE_AXIS_DIM))
    ang = jnp.concatenate([row[:, None] * freqs, col[:, None] * freqs], axis=-1)
    return jnp.cos(ang), jnp.sin(ang)


def _apply_rope(x, cos, sin):
    x32 = x.astype(jnp.float32)
    xp = x32.reshape(*x.shape[:-1], -1, 2)
    x0, x1 = xp[..., 0], xp[..., 1]
    cc = cos[None, :, None, :]
    ss = sin[None, :, None, :]
    out = jnp.stack([x0 * cc - x1 * ss, x0 * ss + x1 * cc], axis=-1).reshape(x.shape)
    return out.astype(x.dtype)


def _gated_delta_chunked(q, k, v, g, beta, s0):
    bsz, t, h, dk = q.shape
    dv = v.shape[-1]
    n = t // DN_CHUNK

    def chunks(a):
        a = jnp.moveaxis(a, 2, 1)
        return a.reshape(bsz, h, n, DN_CHUNK, *a.shape[3:])

    q = chunks(q) * (dk ** -0.5)
    k = chunks(k)
    v = chunks(v)
    g = chunks(g)
    beta = chunks(beta)
    gcum = jnp.cumsum(g, axis=-1)
    idx = jnp.arange(DN_CHUNK)
    lower = idx[:, None] >= idx[None, :]
    strict = idx[:, None] > idx[None, :]
    decay = jnp.exp(jnp.where(lower, gcum[..., :, None] - gcum[..., None, :], -jnp.inf))
    kb = k * beta[..., None]
    a_mat = jnp.where(strict, jnp.einsum('bhnid,bhnjd->bhnij', kb, k) * decay, 0.0)
    l_mat = a_mat + jnp.eye(DN_CHUNK, dtype=jnp.float32)
    u = lax.linalg.triangular_solve(l_mat, v * beta[..., None], left_side=True, lower=True, unit_diagonal=True)
    w = lax.linalg.triangular_solve(l_mat, kb * jnp.exp(gcum)[..., None], left_side=True, lower=True, unit_diagonal=True)
    attn = jnp.einsum('bhnid,bhnjd->bhnij', q, k) * decay
    g_last = gcum[..., -1]
    k_tail = k * jnp.exp(g_last[..., None] - gcum)[..., None]
    q_head = q * jnp.exp(gcum)[..., None]
    xs = tuple(jnp.moveaxis(a, 2, 0) for a in (q_head, k_tail, u, w, attn, g_last))

    def step(s, inp):
        qh, kt, ui, wi, ai, gl = inp
        v_new = ui - jnp.einsum('bhcd,bhde->bhce', wi, s)
        o = jnp.einsum('bhcd,bhde->bhce', qh, s) + jnp.einsum('bhcj,bhje->bhce', ai, v_new)
        s = s * jnp.exp(gl)[..., None, None] + jnp.einsum('bhcd,bhce->bhde', kt, v_new)
        return s, o

    s_fin, o = lax.scan(step, s0, xs)
    o = jnp.moveaxis(o, 0, 2).reshape(bsz, h, t, dv)
    return jnp.moveaxis(o, 1, 2), s_fin


def _delta_inputs(p, conv_w, a_log, dt_bias):
    bsz, t, _ = p.shape
    qkv = jax.nn.silu(_centred_dwconv(p[..., OFF_DN_QKV:OFF_DN_QKV + LEN_DN_QKV], conv_w))
    q, k, v = jnp.split(qkv, 3, axis=-1)
    q = _l2norm(q.reshape(bsz, t, DN_HEADS, DN_DK))
    k = _l2norm(k.reshape(bsz, t, DN_HEADS, DN_DK))
    v = v.reshape(bsz, t, DN_HEADS, DN_DV).astype(jnp.float32)
    beta = jax.nn.sigmoid(p[..., OFF_DN_B:OFF_DN_B + LEN_DN_B].astype(jnp.float32)).reshape(bsz, t, 2, DN_HEADS)
    a_raw = p[..., OFF_DN_A:OFF_DN_A + LEN_DN_A].astype(jnp.float32).reshape(bsz, t, 2, DN_HEADS)
    g = -jnp.exp(a_log.astype(jnp.float32)) * jax.nn.softplus(a_raw + dt_bias.astype(jnp.float32))
    z = p[..., OFF_DN_Z:OFF_DN_Z + LEN_DN_Z].reshape(bsz, t, DN_HEADS, DN_DV)
    return q, k, v, g, beta, z


def _flip_t(a, d):
    return jnp.flip(a, axis=1) if d == 1 else a


def _gated_out(o, z, gain, dtype):
    y = _rmsnorm(o, gain) * jax.nn.silu(z.astype(jnp.float32))
    return y.reshape(o.shape[0], o.shape[1], DN_WIDTH).astype(dtype)


def _attend(q, k, v):
    bsz, t = q.shape[:2]
    nb = t // Q_BLOCK
    qb = q.reshape(bsz, nb, Q_BLOCK, AT_KV_HEADS, AT_GROUP, AT_HD).transpose(1, 0, 2, 3, 4, 5)

    def one(qi):
        s = jnp.einsum('bqhgd,bkhd->bhgqk', qi, k, preferred_element_type=jnp.float32) * ATT_SCALE
        pr = jax.nn.softmax(s, axis=-1).astype(v.dtype)
        return jnp.einsum('bhgqk,bkhd->bqhgd', pr, v)

    o = lax.map(one, qb)
    return o.transpose(1, 0, 2, 3, 4, 5).reshape(bsz, t, AT_WIDTH)


def _mixer(h_lat, h_ctx, w_in, dn_conv, dn_a_log, dn_dt_bias, dn_norm, q_norm, k_norm, w_out,
           cos, sin, need_ctx):
    bsz = h_lat.shape[0]
    dtype = h_lat.dtype
    p_lat = h_lat @ w_in
    p_ctx = h_ctx @ w_in

    ql, kl, vl, gl, bl, zl = _delta_inputs(p_lat, dn_conv, dn_a_log, dn_dt_bias)
    qc, kc, vc, gc, bc, zc = _delta_inputs(p_ctx, dn_conv, dn_a_log, dn_dt_bias)
    o_lat = jnp.zeros(vl.shape, jnp.float32)
    o_ctx = jnp.zeros(vc.shape, jnp.float32)
    for d in range(2):
        s0 = jnp.zeros((bsz, DN_HEADS, DN_DK, DN_DV), jnp.float32)
        oc, s_ctx = _gated_delta_chunked(_flip_t(qc, d), _flip_t(kc, d), _flip_t(vc, d),
                                         _flip_t(gc[:, :, d], d), _flip_t(bc[:, :, d], d), s0)
        ol, _ = _gated_delta_chunked(_flip_t(ql, d), _flip_t(kl, d), _flip_t(vl, d),
                                     _flip_t(gl[:, :, d], d), _flip_t(bl[:, :, d], d), s_ctx)
        o_lat = o_lat + _flip_t(ol, d)
        o_ctx = o_ctx + _flip_t(oc, d)
    dn_lat = _gated_out(o_lat, zl, dn_norm, dtype)

    def qkv_at(p):
        b_, t_ = p.shape[:2]
        q = _rmsnorm(p[..., OFF_AT_Q:OFF_AT_Q + LEN_AT_Q].reshape(b_, t_, AT_HEADS, AT_HD), q_norm)
        k = _rmsnorm(p[..., OFF_AT_K:OFF_AT_K + LEN_AT_K].reshape(b_, t_, AT_KV_HEADS, AT_HD), k_norm)
        v = p[..., OFF_AT_V:OFF_AT_V + LEN_AT_V].reshape(b_, t_, AT_KV_HEADS, AT_HD)
        return q, k, v

    aq_l, ak_l, av_l = qkv_at(p_lat)
    aq_c, ak_c, av_c = qkv_at(p_ctx)
    aq_l = _apply_rope(aq_l, cos, sin)
    ak_l = _apply_rope(ak_l, cos, sin)
    k_all = jnp.concatenate([ak_l, ak_c], axis=1)
    v_all = jnp.concatenate([av_l, av_c], axis=1)
    at_lat = _attend(aq_l, k_all, v_all)

    out_lat = jnp.concatenate([dn_lat, at_lat], axis=-1) @ w_out
    if not need_ctx:
        return out_lat, None
    dn_ctx = _gated_out(o_ctx, zc, dn_norm, dtype)
    at_ctx = _attend(aq_c, ak_c, av_c)
    out_ctx = jnp.concatenate([dn_ctx, at_ctx], axis=-1) @ w_out
    return out_lat, out_ctx


def setup_inputs(seed: int = 0) -> dict:
    key = jax.random.key(seed)
    ks = jax.random.split(key, 24)
    f32 = jnp.float32

    def nrm(k, shape, scale):
        return jax.random.normal(k, shape, f32) * scale

    def gain(k, shape):
        return 1.0 + 0.02 * jax.random.normal(k, shape, f32)

    dt = jnp.exp(jax.random.uniform(ks[14], (DEPTH, 2, DN_HEADS), f32, np.log(1e-3), np.log(1e-1)))
    return {
        "x": nrm(ks[0], (BATCH, SEQ, D_MODEL), 1.0),
        "c": nrm(ks[1], (BATCH, D_MODEL), 1.0),
        "ctx": nrm(ks[2], (BATCH, CTX_LEN, D_MODEL), 1.0),
        "c_ctx": nrm(ks[3], (D_MODEL,), 1.0),
        "w_mod": nrm(ks[4], (DEPTH, D_MODEL, N_MOD * D_MODEL), D_MODEL ** -0.5),
        "b_mod": nrm(ks[5], (DEPTH, N_MOD * D_MODEL), 0.01),
        "g_ffn1": gain(ks[6], (DEPTH, D_MODEL)),
        "ffn1_w1": nrm(ks[7], (DEPTH, D_MODEL, D_FF), D_MODEL ** -0.5),
        "ffn1_w3": nrm(ks[8], (DEPTH, D_MODEL, D_FF), D_MODEL ** -0.5),
        "ffn1_w2": nrm(ks[9], (DEPTH, D_FF, D_MODEL), D_FF ** -0.5),
        "g_mix": gain(ks[10], (DEPTH, D_MODEL)),
        "w_in": nrm(ks[11], (DEPTH, D_MODEL, P_IN), D_MODEL ** -0.5),
        "dn_conv": nrm(ks[12], (DEPTH, DN_CONV, LEN_DN_QKV), DN_CONV ** -0.5),
        "dn_a_log": jnp.log(jax.random.uniform(ks[13], (DEPTH, 2, DN_HEADS), f32, 1.0, 16.0)),
        "dn_dt_bias": dt + jnp.log(-jnp.expm1(-dt)),
        "dn_norm": gain(ks[15], (DEPTH, DN_DV)),
        "q_norm": gain(ks[16], (DEPTH, AT_HD)),
        "k_norm": gain(ks[17], (DEPTH, AT_HD)),
        "w_out": nrm(ks[18], (DEPTH, D_MIX, D_MODEL), D_MIX ** -0.5),
        "g_ffn2": gain(ks[19], (DEPTH, D_MODEL)),
        "ffn2_w1": nrm(ks[20], (DEPTH, D_MODEL, D_FF), D_MODEL ** -0.5),
        "ffn2_w3": nrm(ks[21], (DEPTH, D_MODEL, D_FF), D_MODEL ** -0.5),
        "ffn2_w2": nrm(ks[22], (DEPTH, D_FF, D_MODEL), D_FF ** -0.5),
        "g_final": gain(ks[23], (D_MODEL,)),
    }


def reference(x, c, ctx, c_ctx, w_mod, b_mod, g_ffn1, ffn1_w1, ffn1_w3, ffn1_w2, g_mix, w_in,
              dn_conv, dn_a_log, dn_dt_bias, dn_norm, q_norm, k_norm, w_out, g_ffn2,
              ffn2_w1, ffn2_w3, ffn2_w2, g_final):
    cos, sin = _axial_rope_angles(x.shape[1])
    h_ctx = ctx
    for i in range(DEPTH):
        last = i == DEPTH - 1
        ml = jnp.split((jax.nn.silu(c) @ w_mod[i] + b_mod[i])[:, None, :], N_MOD, axis=-1)
        mc = jnp.split((jax.nn.silu(c_ctx) @ w_mod[i] + b_mod[i])[None, None, :], N_MOD, axis=-1)
        x = x + 0.5 * ml[2] * _swiglu(_modulate(x, g_ffn1[i], ml[0], ml[1]), ffn1_w1[i], ffn1_w3[i], ffn1_w2[i])
        h_ctx = h_ctx + 0.5 * mc[2] * _swiglu(_modulate(h_ctx, g_ffn1[i], mc[0], mc[1]), ffn1_w1[i], ffn1_w3[i], ffn1_w2[i])
        mix_l, mix_c = _mixer(_modulate(x, g_mix[i], ml[3], ml[4]), _modulate(h_ctx, g_mix[i], mc[3], mc[4]),
                              w_in[i], dn_conv[i], dn_a_log[i], dn_dt_bias[i], dn_norm[i], q_norm[i], k_norm[i],
                              w_out[i], cos, sin, not last)
        x = x + ml[5] * mix_l
        x = x + 0.5 * ml[8] * _swiglu(_modulate(x, g_ffn2[i], ml[6], ml[7]), ffn2_w1[i], ffn2_w3[i], ffn2_w2[i])
        if not last:
            h_ctx = h_ctx + mc[5] * mix_c
            h_ctx = h_ctx + 0.5 * mc[8] * _swiglu(_modulate(h_ctx, g_ffn2[i], mc[6], mc[7]), ffn2_w1[i], ffn2_w3[i], ffn2_w2[i])
    return _rmsnorm(x, g_final)
```

```python
import contextlib
import numpy as np
import concourse.bass as bass
import concourse.mybir as mybir
from concourse.bass_utils import run_bass_kernel_spmd

F32 = mybir.dt.float32
BF16 = mybir.dt.bfloat16
AF = mybir.ActivationFunctionType
ALU = mybir.AluOpType

D = 1024
T = 4096
TC = 256
NT = T + TC
FF = 2816
NFF = FF // 128
PIN = 3088
EPS = 1e-6
ENG = ("pe", "act", "dve", "pool", "sp")
import os as _os
DMAQ = {"pool": _os.environ.get("POOLQ", "pool")}


class _Op:
    __slots__ = ("eng", "fn", "deps", "signal", "count", "is_dma", "key")

    def __init__(self, eng, fn, is_dma, key):
        self.eng = eng
        self.fn = fn
        self.deps = set()
        self.signal = False
        self.count = 0
        self.is_dma = is_dma
        self.key = key


class Prog:
    def __init__(self, nc):
        self.nc = nc
        self.ops = []
        self.last_w = {}
        self.readers = {}
        self.key_last = {}
        self.key_n = {}
        self.eng_last = {}

    def _add(self, eng, fn, reads, writes, is_dma=False, key=None):
        op = _Op(eng, fn, is_dma, key)
        deps = op.deps
        for r in reads:
            w = self.last_w.get(r)
            if w is not None:
                deps.add(w)
        for w_ in writes:
            w = self.last_w.get(w_)
            if w is not None:
                deps.add(w)
            for rd in self.readers.get(w_, ()):
                deps.add(rd)
        if is_dma:
            prev = self.key_last.get(key)
            if prev is not None:
                deps.add(prev)
            self.key_last[key] = op
            self.key_n[key] = self.key_n.get(key, 0) + 1
            op.count = 16 * self.key_n[key]
        else:
            self.eng_last[eng] = op
        deps.discard(op)
        for r in reads:
            self.readers.setdefault(r, []).append(op)
        for w_ in writes:
            self.last_w[w_] = op
            self.readers[w_] = []
        self.ops.append(op)
        return op

    def op(self, eng, fn, reads=(), writes=()):
        writes = tuple(writes) + tuple(r for r in reads if isinstance(r, str) and r.startswith("bank") and r not in writes)
        return self._add(eng, fn, tuple(reads), writes)

    def dma(self, q, key, out, in_, reads=(), writes=()):
        q = DMAQ.get(q, q)

        def fn(e):
            return e.dma_start(out=out, in_=in_)
        return self._add(q, fn, tuple(reads), tuple(writes), True, key)

    def barrier(self):
        deps = set(self.eng_last.values()) | set(self.key_last.values())
        for e in ENG:
            op = _Op(e, None, False, None)
            op.deps = set(deps)
            self.ops.append(op)
        self.last_w = {}
        self.readers = {}

    def emit(self):
        nc = self.nc
        ops = self.ops
        for o in ops:
            for d in o.deps:
                if d.is_dma:
                    continue
                if d.eng == "pe" and o.eng == "pe" and not o.is_dma and o.fn is not None:
                    continue
                d.signal = True
        cnt = {e: 0 for e in ENG}
        for o in ops:
            if o.is_dma or o.fn is None:
                continue
            if o.signal:
                cnt[o.eng] += 1
                o.count = cnt[o.eng]
        keys = list(self.key_n.keys())
        with contextlib.ExitStack() as es:
            esem = {e: es.enter_context(nc.semaphore("s_" + e)) for e in ENG}
            ksem = {k: es.enter_context(nc.semaphore("k%d" % i)) for i, k in enumerate(keys)}
            block = es.enter_context(nc.Block())
            streams = {e: [o for o in ops if o.eng == e] for e in ENG}

            def run(e, engobj):
                seen = {}
                for o in streams[e]:
                    need = {}
                    for d in o.deps:
                        if d.is_dma:
                            s = ksem[d.key]
                        else:
                            if d.eng == "pe" and e == "pe" and not o.is_dma and o.fn is not None:
                                continue
                            s = esem[d.eng]
                        v = d.count
                        if need.get(s, 0) < v:
                            need[s] = v
                    for s, v in need.items():
                        if seen.get(s, 0) >= v:
                            continue
                        seen[s] = v
                        engobj.wait_ge(s, v)
                    if o.fn is None:
                        continue
                    ins = o.fn(engobj)
                    if o.is_dma:
                        ins.then_inc(ksem[o.key], 16)
                    elif o.signal:
                        ins.then_inc(esem[e], 1)
                if e == "sp":
                    for k in keys:
                        v = 16 * self.key_n[k]
                        if seen.get(ksem[k], 0) < v:
                            engobj.wait_ge(ksem[k], v)

            @block.sync
            def _(eng):
                run("sp", eng)

            @block.scalar
            def _(eng):
                run("act", eng)

            @block.vector
            def _(eng):
                run("dve", eng)

            @block.gpsimd
            def _(eng):
                run("pool", eng)

            @block.tensor
            def _(eng):
                run("pe", eng)


SB_LO = 16512
SB_HI = 229376


class Builder:
    def __init__(self, taps=()):
        self.nc = bass.Bass("TRN2", target_bir_lowering=False)
        self.P = Prog(self.nc)
        self.off = SB_LO
        self.nalloc = 0
        self.taps = set(taps)
        self.tap_out = {}
        self.es = contextlib.ExitStack()
        self.rot = {}

    def sb(self, name, shape, dt):
        n = int(np.prod(shape[1:])) * (4 if dt == F32 else 2)
        n = (n + 63) // 64 * 64
        assert self.off + n <= SB_HI, ("SBUF overflow", name, self.off, n)
        self.nalloc += 1
        t = self.nc.alloc_sbuf_tensor_at("%s_%d" % (name, self.nalloc), list(shape), dt, offset=self.off)
        self.off += n
        return t

    def mark(self):
        return self.off

    def release(self, m):
        self.P.barrier()
        self.off = m

    def din(self, name, shape, dt=F32):
        return self.nc.dram_tensor(name, list(shape), dt, kind="ExternalInput").ap()

    def dscr(self, name, shape, dt=F32):
        if name in self.taps:
            t = self.nc.dram_tensor(name, list(shape), dt, kind="ExternalOutput").ap()
            self.tap_out[name] = t
            return t
        return self.nc.dram_tensor(name, list(shape), dt).ap()

    def tap(self, name, ap, shape, reads, dt=F32):
        if name not in self.taps:
            return
        t = self.nc.dram_tensor(name, list(shape), dt, kind="ExternalOutput").ap()
        self.tap_out[name] = t
        self.P.dma("sp", "tap_" + name, t, ap, reads=reads)

    def nxt(self, name, n):
        i = self.rot.get(name, 0)
        self.rot[name] = i + 1
        return i % n

    def build(self, stop_after=None):
        nc, P = self.nc, self.P
        es = self.es
        x_in = self.din("x", [T, D])
        ctx_in = self.din("ctx", [TC, D])
        ccT_in = self.din("ccT", [128, 8, 2])
        w_mod = self.din("w_mod", [D, 9 * D])
        b_modT = self.din("b_modT", [128, 72])
        gvec = self.din("gvec", [128, 4, 8])
        f1w1 = self.din("ffn1_w1", [D, FF]); f1w3 = self.din("ffn1_w3", [D, FF]); f1w2 = self.din("ffn1_w2", [FF, D])
        f2w1 = self.din("ffn2_w1", [D, FF]); f2w3 = self.din("ffn2_w3", [D, FF]); f2w2 = self.din("ffn2_w2", [FF, D])
        w_in = self.din("w_in", [D, PIN]); w_out = self.din("w_out", [D, D])
        consts = self.din("consts", [128, 7, 128]); rows = self.din("rows", [128, 400])
        self.convT_in = self.din("convT", [128, 12, 5])
        self.lvmask_in = self.din("lvmask", [128, 7, 2, 128])
        ropeC = self.din("ropeC", [128, T]); ropeS = self.din("ropeS", [128, T])
        out = self.nc.dram_tensor("out", [T, D], F32, kind="ExternalOutput").ap()
        xT = self.dscr("xT", [D, NT])
        self.xT = xT

        self.banks = [es.enter_context(nc.psum_tensor("bank%d" % i, [128, 512], F32))[:, :] for i in range(8)]
        banks = self.banks

        identf = self.sb("identf", [128, 128], F32)
        onesb = self.sb("onesb", [128, 128], BF16)
        epsc = self.sb("epsc", [128, 1], F32)
        self.identf, self.onesb, self.epsc = identf, onesb, epsc
        P.op("pool", lambda e: e.memset(identf[:], 0.0), writes=["identf"])
        P.op("pool", lambda e: e.affine_select(out=identf[:], in_=identf[:], pattern=[[-1, 128]],
                                               compare_op=ALU.not_equal, fill=1.0, base=0, channel_multiplier=1),
             reads=["identf"], writes=["identf"])
        P.op("pool", lambda e: e.memset(onesb[:], 1.0), writes=["onesb"])
        P.op("pool", lambda e: e.memset(epsc[:], EPS), writes=["epsc"])

        modT = self.sb("modT", [128, 72, 2], F32)
        gv = self.sb("gv", [128, 4, 8], F32)
        vec = self.sb("vec", [128, 3, 3, 2, 8], F32)
        self.vec = vec
        m0 = self.mark()
        ccT = self.sb("ccT", [128, 8, 2], F32)
        bmT = self.sb("bmT", [128, 72], F32)
        wm = [self.sb("wm%d" % i, [128, 4608], F32) for i in range(2)]
        P.dma("sp", "ccT", ccT[:], ccT_in[:, :, :], writes=["ccT"])
        P.dma("sp", "bmT", bmT[:], b_modT[:, :], writes=["bmT"])
        P.dma("sp", "gv", gv[:], gvec[:, :, :], writes=["gv"])
        P.op("act", lambda e: e.activation(out=ccT[:], in_=ccT[:], func=AF.Silu), reads=["ccT"], writes=["ccT"])
        mb = banks[0]
        for k in range(8):
            for h in range(2):
                s = self.nxt("wm", 2)
                P.dma("sp", "wm%d" % s, wm[s][:], w_mod[k * 128:(k + 1) * 128, h * 4608:(h + 1) * 4608],
                      writes=["wm%d" % s])
                for n in range(36):
                    col = (h * 36 + n) * 2
                    P.op("pe", lambda e, s=s, n=n, col=col, k=k, h=h: e.matmul(
                        out=mb[:, col:col + 2], lhsT=wm[s][:, n * 128:(n + 1) * 128], rhs=ccT[:, k, :],
                        start=(k == 0 and h == 0 and n == 0), stop=(k == 7), skip_group_check=True),
                        reads=["wm%d" % s, "ccT"], writes=["bank0"])
        P.op("dve", lambda e: e.tensor_tensor(out=modT[:], in0=mb[:, 0:144].rearrange("p (n r) -> p n r", r=2),
                                              in1=bmT[:].unsqueeze(2).to_broadcast([128, 72, 2]), op=ALU.add),
             reads=["bank0", "bmT"], writes=["modT"])
        for wi, (jshift, jscale, jgate, gsc) in enumerate([(0, 1, 2, 0.5), (3, 4, 5, 1.0), (6, 7, 8, 0.5)]):
            for r in range(2):
                P.op("dve", lambda e, wi=wi, r=r, jscale=jscale: e.scalar_tensor_tensor(
                    out=vec[:, wi, 0, r, :], in0=modT[:, jscale * 8:(jscale + 1) * 8, r], scalar=1.0,
                    in1=gv[:, wi, :], op0=ALU.add, op1=ALU.mult), reads=["modT", "gv"], writes=["vec"])
                P.op("dve", lambda e, wi=wi, r=r, jshift=jshift: e.tensor_copy(
                    out=vec[:, wi, 1, r, :], in_=modT[:, jshift * 8:(jshift + 1) * 8, r]),
                    reads=["modT"], writes=["vec"])
                P.op("dve", lambda e, wi=wi, r=r, jgate=jgate, gsc=gsc: e.tensor_scalar(
                    out=vec[:, wi, 2, r, :], in0=modT[:, jgate * 8:(jgate + 1) * 8, r], scalar1=gsc, scalar2=None,
                    op0=ALU.mult), reads=["modT"], writes=["vec"])
        if "vec" in self.taps:
            tv = self.nc.dram_tensor("vec", [128, 144], F32, kind="ExternalOutput").ap()
            self.tap_out["vec"] = tv
            P.dma("sp", "tapvec", tv[:, :], vec[:].rearrange("p a b c d -> p (a b c d)"), reads=["vec"])
        self.release(m0)

        m0 = self.mark()
        xin = [self.sb("xin%d" % i, [128, 4, D], F32) for i in range(2)]
        xst = [self.sb("xst%d" % i, [128, 8, 512], F32) for i in range(2)]
        for g in range(9):
            n = 512 if g < 8 else 256
            nt = n // 128
            s = self.nxt("xin", 2)
            src = x_in[g * 512:(g + 1) * 512, :] if g < 8 else ctx_in[:, :]
            P.dma("sp", "xin%d" % s, xin[s][:, 0:nt, :], src.rearrange("(a p) d -> p a d", p=128), writes=["xin%d" % s])
            so = self.nxt("xst", 2)
            for c in range(8):
                bi = self.nxt("t0bank", 4)
                bk = banks[bi]
                for a in range(nt):
                    P.op("pe", lambda e, s=s, a=a, c=c, bk=bk: e.transpose(
                        out=bk[:, a * 128:(a + 1) * 128], in_=xin[s][:, a, c * 128:(c + 1) * 128], identity=identf[:]),
                        reads=["xin%d" % s, "identf"], writes=["bank%d" % bi])
                eng = "act" if c % 2 == 0 else "dve"
                if eng == "act":
                    P.op("act", lambda e, so=so, c=c, bk=bk, n=n: e.copy(out=xst[so][:, c, 0:n], in_=bk[:, 0:n]),
                         reads=["bank%d" % bi], writes=["xst%d" % so])
                else:
                    P.op("dve", lambda e, so=so, c=c, bk=bk, n=n: e.tensor_copy(out=xst[so][:, c, 0:n], in_=bk[:, 0:n]),
                         reads=["bank%d" % bi], writes=["xst%d" % so])
            P.dma("pool", "xst%d" % so, xT[:, g * 512:g * 512 + n].rearrange("(c p) t -> p c t", p=128), xst[so][:, :, 0:n],
                  reads=["xst%d" % so], writes=[("xT", g)])
        self.release(m0)
        if stop_after == "t0":
            return self.finish()

        groups = [(g * 512, 512, 0, g) for g in range(8)] + [(T, TC, 1, 8)]
        self.ffn(f1w1, f1w3, f1w2, 0, groups)
        if stop_after == "ffn1":
            return self.finish()
        self.mixer(w_in, w_out, consts, rows, ropeC, ropeS, stop_after=stop_after)
        if stop_after in ("atproj", "dnproj", "attn", "dn", "mixer"):
            return self.finish()
        lat_groups = groups[:8]
        self.ffn(f2w1, f2w3, f2w2, 2, lat_groups)
        self.final_norm(out, gv)
        return self.finish()

    def build_attn_only(self):
        nc, P, es = self.nc, self.P, self.es
        self.banks = [es.enter_context(nc.psum_tensor("bank%d" % i, [128, 512], F32))[:, :] for i in range(8)]
        at_qT = self.din("at_qT", [4, 128, T], BF16)
        at_kT = self.din("at_kT", [2, 128, NT], BF16)
        at_v = self.din("at_v", [NT, 256], BF16)
        mixT = self.nc.dram_tensor("mixT", [D, T], BF16, kind="ExternalOutput").ap()
        self.onesb = self.sb("onesb", [128, 128], BF16)
        negm = self.sb("negm", [128, 1], F32)
        P.op("pool", lambda e: e.memset(self.onesb[:], 1.0), writes=["onesb"])
        P.op("pool", lambda e: e.memset(negm[:], -12.623587), writes=["negm"])
        import os
        if not os.environ.get("SKIP_ATT"):
            self.attention(at_qT, at_kT, at_v, mixT, negm)
        self.dn_main(mixT)
        if stop_after == "dn":
            return
        return self.finish()

    def finish(self):
        self.P.emit()
        self.es.close()
        return self.nc

    def rms_stats(self, xg, n, sq, bank_key, bk, rs):
        P = self.P
        onesb, epsc = self.onesb, self.epsc
        kx, ksq, krs = xg[1], sq[1], rs[1]
        xg, sq, rs = xg[0], sq[0], rs[0]
        P.op("act", lambda e: e.activation(out=sq[:, :, 0:n], in_=xg[:, :, 0:n], func=AF.Square), reads=[kx], writes=[ksq])
        for c in range(8):
            P.op("pe", lambda e, c=c: e.matmul(out=bk[:, 0:n], lhsT=onesb[:], rhs=sq[:, c, 0:n], start=(c == 0), stop=(c == 7)),
                 reads=[ksq, "onesb"], writes=[bank_key])
        P.op("act", lambda e: e.activation(out=rs[:, 0:n], in_=bk[:, 0:n], func=AF.Sqrt, bias=epsc[:, 0:1], scale=1.0 / D),
             reads=[bank_key, "epsc"], writes=[krs])
        P.op("dve", lambda e: e.reciprocal(out=rs[:, 0:n], in_=rs[:, 0:n]), reads=[krs], writes=[krs])


    def norm_mod(self, wi, groups, loc, hT, xg, sq, rs):
        P, banks, xT, vec = self.P, self.banks, self.xT, self.vec
        for gi, (c0, n, kind, gid) in enumerate(groups):
            s = self.nxt("xg", 2)
            P.dma("sp", "xg%d" % s, xg[s][:, :, 0:n], xT[:, c0:c0 + n].rearrange("(c p) t -> p c t", p=128),
                  reads=[("xT", gid)], writes=["xg%d" % s])
            bi = 4 + self.nxt("stbank", 2)
            r = self.nxt("rs", 2)
            self.rms_stats((xg[s], "xg%d" % s), n, (sq, "sq"), "bank%d" % bi, banks[bi], (rs[r], "rs%d" % r))
            P.op("dve", lambda e, s=s, r=r, n=n: e.tensor_tensor(
                out=xg[s][:, :, 0:n], in0=xg[s][:, :, 0:n], in1=rs[r][:, 0:n].unsqueeze(1).to_broadcast([128, 8, n]),
                op=ALU.mult), reads=["xg%d" % s, "rs%d" % r], writes=["xg%d" % s])
            for c in range(8):
                P.op("act", lambda e, s=s, c=c, n=n, kind=kind, lo=loc[gi]: e.activation(
                    out=hT[:, c, lo:lo + n], in_=xg[s][:, c, 0:n], func=AF.Identity,
                    scale=vec[:, wi, 0, kind, c:c + 1], bias=vec[:, wi, 1, kind, c:c + 1]),
                    reads=["xg%d" % s, "vec"], writes=[("hT", gi)])

    def ffn(self, w1, w3, w2, wi, groups):
        nc, P, banks, xT, vec = self.nc, self.P, self.banks, self.xT, self.vec
        m0 = self.mark()
        ntok = sum(g[1] for g in groups)
        hT = self.sb("hT", [128, 8, ntok], BF16)
        parts = [(0, 6), (6, 12), (12, 17), (17, 22)]
        maxp = 6
        gT = self.sb("gT", [128, maxp, ntok], BF16)
        w2b = self.sb("w2b", [128, maxp, D], BF16)
        wst = [self.sb("wst%d" % i, [128, 8, 128], F32) for i in range(4)]
        w13b = [self.sb("w13b%d" % i, [128, 8, 128], BF16) for i in range(4)]
        xg = [self.sb("xg%d" % i, [128, 8, 512], F32) for i in range(2)]
        sq = self.sb("sq", [128, 8, 512], BF16)
        rs = [self.sb("rs%d" % i, [128, 512], F32) for i in range(2)]
        sa = [self.sb("sa%d" % i, [128, 512], F32) for i in range(2)]
        loc = []
        o = 0
        for (c0, n, kind, gid) in groups:
            loc.append(o)
            o += n
        self.norm_mod(wi, groups, loc, hT, xg, sq, rs)
        for (fa, fb) in parts:
            for f in range(fa, fb):
                wb = []
                for wsrc in (w1, w3):
                    s = self.nxt("wst", 4)
                    P.dma("sp", "wst%d" % s, wst[s][:], wsrc[:, f * 128:(f + 1) * 128].rearrange("(c p) n -> p c n", p=128),
                          writes=["wst%d" % s])
                    b = self.nxt("w13b", 4)
                    P.op("pool", lambda e, s=s, b=b: e.tensor_copy(out=w13b[b][:], in_=wst[s][:]),
                         reads=["wst%d" % s], writes=["w13b%d" % b])
                    wb.append(b)
                for gi, (c0, n, kind, gid) in enumerate(groups):
                    lo = loc[gi]
                    bp = self.nxt("upbank", 2)
                    ba, bb = banks[2 * bp], banks[2 * bp + 1]
                    for (bk, bkey, b) in ((ba, "bank%d" % (2 * bp), wb[0]), (bb, "bank%d" % (2 * bp + 1), wb[1])):
                        for c in range(8):
                            P.op("pe", lambda e, bk=bk, b=b, c=c, lo=lo, n=n: e.matmul(
                                out=bk[:, 0:n], lhsT=w13b[b][:, c, :], rhs=hT[:, c, lo:lo + n], start=(c == 0), stop=(c == 7)),
                                reads=["w13b%d" % b, ("hT", gi)], writes=[bkey])
                    si = self.nxt("sa", 2)
                    P.op("act", lambda e, si=si, ba=ba, n=n: e.activation(out=sa[si][:, 0:n], in_=ba[:, 0:n], func=AF.Silu),
                         reads=["bank%d" % (2 * bp)], writes=["sa%d" % si])
                    P.op("dve", lambda e, si=si, bb=bb, n=n, lo=lo, f=f, fa=fa: e.tensor_tensor(
                        out=gT[:, f - fa, lo:lo + n], in0=sa[si][:, 0:n], in1=bb[:, 0:n], op=ALU.mult),
                        reads=["sa%d" % si, "bank%d" % (2 * bp + 1)], writes=[("gT", f - fa, gi)])
            for f in range(fa, fb):
                for hh in range(2):
                    s = self.nxt("wst", 4)
                    P.dma("sp", "wst%d" % s, wst[s][:].rearrange("p c n -> p (c n)")[:, 0:512],
                          w2[f * 128:(f + 1) * 128, hh * 512:(hh + 1) * 512], writes=["wst%d" % s])
                    P.op("pool", lambda e, s=s, f=f, fa=fa, hh=hh: e.tensor_copy(
                        out=w2b[:, f - fa, hh * 512:(hh + 1) * 512], in_=wst[s][:].rearrange("p c n -> p (c n)")[:, 0:512]),
                        reads=["wst%d" % s], writes=[("w2b", f - fa)])
            for gi, (c0, n, kind, gid) in enumerate(groups):
                lo = loc[gi]
                s = self.nxt("xg", 2)
                P.dma("sp", "xg%d" % s, xg[s][:, :, 0:n], xT[:, c0:c0 + n].rearrange("(c p) t -> p c t", p=128),
                      reads=[("xT", gid)], writes=["xg%d" % s])
                for dc in range(8):
                    bi = 4 + self.nxt("dnbank", 4)
                    bk = banks[bi]
                    for f in range(fa, fb):
                        P.op("pe", lambda e, bk=bk, f=f, fa=fa, dc=dc, lo=lo, n=n: e.matmul(
                            out=bk[:, 0:n], lhsT=w2b[:, f - fa, dc * 128:(dc + 1) * 128], rhs=gT[:, f - fa, lo:lo + n],
                            start=(f == fa), stop=(f == fb - 1)),
                            reads=[("w2b", f - fa), ("gT", f - fa, gi)], writes=["bank%d" % bi])
                    P.op("dve", lambda e, bk=bk, s=s, dc=dc, n=n, kind=kind: e.scalar_tensor_tensor(
                        out=xg[s][:, dc, 0:n], in0=bk[:, 0:n], scalar=vec[:, wi, 2, kind, dc:dc + 1], in1=xg[s][:, dc, 0:n],
                        op0=ALU.mult, op1=ALU.add), reads=["bank%d" % bi, "xg%d" % s, "vec"], writes=["xg%d" % s])
                P.dma("pool", "xgo%d" % s, xT[:, c0:c0 + n].rearrange("(c p) t -> p c t", p=128), xg[s][:, :, 0:n],
                      reads=["xg%d" % s], writes=[("xT", gid)])
        self.release(m0)


    def load_w_chunk(self, src_ap, wst, w13b, nst=4, nb=None, bkey="w13b", bidx=None):
        P = self.P
        s = self.nxt("wst", nst)
        P.dma("sp", "wst%d" % s, wst[s][:], src_ap.rearrange("(c p) n -> p c n", p=128), writes=["wst%d" % s])
        if bidx is None:
            bidx = self.nxt(bkey, nb)
        P.op("pool", lambda e, s=s, b=bidx: e.tensor_copy(out=w13b[b][:], in_=wst[s][:]),
             reads=["wst%d" % s], writes=["%s%d" % (bkey, bidx)])
        return bidx

    def mixer(self, w_in, w_out, consts, rows, ropeC, ropeS, stop_after=None):
        nc, P, banks, xT, vec = self.nc, self.P, self.banks, self.xT, self.vec
        groups = [(g * 512, 512, 0, g) for g in range(8)] + [(T, TC, 1, 8)]
        loc = [g[0] for g in groups]
        at_qT = self.dscr("at_qT", [4, 128, T], BF16)
        at_kT = self.dscr("at_kT", [2, 128, NT], BF16)
        at_v = self.dscr("at_v", [NT, 256], BF16)
        mixT = self.dscr("mixT", [D, T], BF16)
        self.mixT = mixT
        m_top = self.mark()
        cst = self.sb("cst", [128, 7, 128], F32)
        rws = self.sb("rws", [128, 400], F32)
        negm = self.sb("negm", [128, 1], F32)
        identb = self.sb("identb", [128, 128], BF16)
        self.cst, self.rws, self.identb = cst, rws, identb
        self.dnsc = {k_: self.sb("dn_" + k_, [128, 34, 8], F32) for k_ in ("beta", "gc", "ngc", "gcl", "et", "bet", "egl")}
        onesf = self.sb("onesf", [128, 128], F32)
        self.onesf = onesf
        P.op("pool", lambda e: e.memset(onesf[:], 1.0), writes=["onesf"])
        P.dma("sp", "cst", cst[:], consts[:, :, :], writes=["cst"])
        P.dma("sp", "rws", rws[:], rows[:, :], writes=["rws"])
        P.op("dve", lambda e: e.tensor_copy(out=identb[:], in_=self.identf[:]), reads=["identf"], writes=["identb"])
        m_all = self.mark()
        hT = self.sb("mhT", [128, 8, NT], BF16)
        m1 = self.mark()
        xg = [self.sb("mxg%d" % i, [128, 8, 512], F32) for i in range(2)]
        sq = self.sb("msq", [128, 8, 512], BF16)
        rs = [self.sb("mrs%d" % i, [128, 512], F32) for i in range(2)]
        self.norm_mod(1, groups, loc, hT, xg, sq, rs)
        self.release(m1)
        m1 = self.mark()
        wst = [self.sb("awst%d" % i, [128, 8, 128], F32) for i in range(2)]
        wq = [self.sb("awq%d" % i, [128, 8, 128], BF16) for i in range(6)]
        wvs = self.sb("awvs", [128, 8, 256], F32)
        wv = self.sb("awv", [128, 8, 256], BF16)
        gcol = self.sb("agcol", [128, 2], F32)
        tabC = [self.sb("atabC%d" % i, [128, 512], F32) for i in range(2)]
        tabS = [self.sb("atabS%d" % i, [128, 512], F32) for i in range(2)]
        sqb = [self.sb("asqb%d" % i, [128, 512], BF16) for i in range(2)]
        rsb = [self.sb("arsb%d" % i, [128, 512], F32) for i in range(2)]
        qn = [self.sb("aqn%d" % i, [128, 512], F32) for i in range(2)]
        t1 = [self.sb("at1%d" % i, [128, 512], F32) for i in range(2)]
        t2 = [self.sb("at2%d" % i, [128, 512], F32) for i in range(2)]
        qst = [self.sb("aqst%d" % i, [128, 512], BF16) for i in range(2)]
        vst = [self.sb("avst%d" % i, [128, 4, 256], BF16) for i in range(2)]
        for j in range(2):
            P.op("dve", lambda e, j=j: e.tensor_tensor(out=t1[0][:, 0:128], in0=rws[:, j * 128:(j + 1) * 128], in1=self.identf[:],
                                                       op=ALU.mult), reads=["rws", "identf", "at10"], writes=["at10"])
            P.op("dve", lambda e, j=j: e.tensor_reduce(out=gcol[:, j:j + 1], in_=t1[0][:, 0:128], axis=mybir.AxisListType.X,
                                                       op=ALU.add), reads=["at10"], writes=["agcol"])
        P.op("dve", lambda e: e.scalar_tensor_tensor(out=t1[1][:, 0:256], in0=rws[:, 0:256], scalar=-1.0, in1=rws[:, 0:256],
                                                     op0=ALU.mult, op1=ALU.max), reads=["rws", "at11"], writes=["at11"])
        P.op("dve", lambda e: e.tensor_reduce(out=t2[0][:, 0:2], in_=t1[1][:, 0:256].rearrange("p (a b) -> p a b", a=2),
                                              axis=mybir.AxisListType.X, op=ALU.max), reads=["at11", "at20"], writes=["at20"])
        P.op("dve", lambda e: e.scalar_tensor_tensor(out=negm[:], in0=t2[0][:, 0:1], scalar=-float(np.sqrt(128.0)),
                                                     in1=t2[0][:, 1:2], op0=ALU.mult, op1=ALU.mult),
             reads=["at20"], writes=["negm"])
        self.tap("negm", negm[:], [128, 1], ["negm"])
        self.tap("gcol", gcol[:], [128, 2], ["agcol"])
        colq = [2064 + h * 128 for h in range(4)] + [2576 + g * 128 for g in range(2)]
        for ch in range(6):
            self.load_w_chunk(w_in[:, colq[ch]:colq[ch] + 128], wst, wq, nst=2, bkey="awq", bidx=ch)
        P.dma("sp", "awvs", wvs[:], w_in[:, 2832:3088].rearrange("(c p) n -> p c n", p=128), writes=["awvs"])
        P.op("pool", lambda e: e.tensor_copy(out=wv[:], in_=wvs[:]), reads=["awvs"], writes=["awv"])
        for gi, (c0, n, kind, gid) in enumerate(groups):
            lat = kind == 0
            if lat:
                tb = self.nxt("atab", 2)
                P.dma("sp", "atabC%d" % tb, tabC[tb][:], ropeC[:, c0:c0 + n], writes=["atabC%d" % tb])
                P.dma("sp", "atabS%d" % tb, tabS[tb][:], ropeS[:, c0:c0 + n], writes=["atabS%d" % tb])
            for ch in range(6):
                if ch < 4 and not lat:
                    continue
                gj = 0 if ch < 4 else 1
                bi = self.nxt("apbank", 2)
                bk = banks[bi]
                for c in range(8):
                    P.op("pe", lambda e, bk=bk, ch=ch, c=c, c0=c0, n=n: e.matmul(
                        out=bk[:, 0:n], lhsT=wq[ch][:, c, :], rhs=hT[:, c, c0:c0 + n], start=(c == 0), stop=(c == 7)),
                        reads=["awq%d" % ch, ("hT", gi)], writes=["bank%d" % bi])
                i2 = self.nxt("asq", 2)
                P.op("act", lambda e, i2=i2, bk=bk, n=n: e.activation(out=sqb[i2][:, 0:n], in_=bk[:, 0:n], func=AF.Square),
                     reads=["bank%d" % bi], writes=["asqb%d" % i2])
                bs = 2 + self.nxt("asbank", 2)
                P.op("pe", lambda e, bs=bs, i2=i2, n=n: e.matmul(out=banks[bs][:, 0:n], lhsT=self.onesb[:], rhs=sqb[i2][:, 0:n],
                                                                 start=True, stop=True),
                     reads=["asqb%d" % i2, "onesb"], writes=["bank%d" % bs])
                P.op("act", lambda e, bs=bs, i2=i2, n=n: e.activation(out=rsb[i2][:, 0:n], in_=banks[bs][:, 0:n], func=AF.Sqrt,
                                                                      bias=self.epsc[:, 0:1], scale=1.0 / 128),
                     reads=["bank%d" % bs, "epsc"], writes=["arsb%d" % i2])
                P.op("dve", lambda e, i2=i2, n=n: e.reciprocal(out=rsb[i2][:, 0:n], in_=rsb[i2][:, 0:n]),
                     reads=["arsb%d" % i2], writes=["arsb%d" % i2])
                P.op("dve", lambda e, i2=i2, bk=bk, gj=gj, n=n: e.scalar_tensor_tensor(
                    out=qn[i2][:, 0:n], in0=bk[:, 0:n], scalar=gcol[:, gj:gj + 1], in1=rsb[i2][:, 0:n],
                    op0=ALU.mult, op1=ALU.mult), reads=["bank%d" % bi, "agcol", "arsb%d" % i2], writes=["aqn%d" % i2])
                so = self.nxt("aqst", 2)
                if lat:
                    br = 4 + self.nxt("arbank", 2)
                    P.op("pe", lambda e, br=br, i2=i2, n=n: e.matmul(out=banks[br][:, 0:n], lhsT=cst[:, 0, :], rhs=qn[i2][:, 0:n],
                                                                     start=True, stop=True),
                         reads=["cst", "aqn%d" % i2], writes=["bank%d" % br])
                    P.op("pool", lambda e, i2=i2, tb=tb, n=n: e.tensor_tensor(out=t1[i2][:, 0:n], in0=qn[i2][:, 0:n],
                                                                             in1=tabC[tb][:, 0:n], op=ALU.mult),
                         reads=["aqn%d" % i2, "atabC%d" % tb], writes=["at1%d" % i2])
                    P.op("dve", lambda e, i2=i2, tb=tb, br=br, n=n: e.tensor_tensor(out=t2[i2][:, 0:n], in0=banks[br][:, 0:n],
                                                                                    in1=tabS[tb][:, 0:n], op=ALU.mult),
                         reads=["bank%d" % br, "atabS%d" % tb], writes=["at2%d" % i2])
                    P.op("pool", lambda e, i2=i2, so=so, n=n: e.tensor_tensor(out=qst[so][:, 0:n], in0=t1[i2][:, 0:n],
                                                                             in1=t2[i2][:, 0:n], op=ALU.add),
                         reads=["at1%d" % i2, "at2%d" % i2], writes=["aqst%d" % so])
                else:
                    P.op("pool", lambda e, i2=i2, so=so, n=n: e.tensor_copy(out=qst[so][:, 0:n], in_=qn[i2][:, 0:n]),
                         reads=["aqn%d" % i2], writes=["aqst%d" % so])
                dst = at_qT[ch, :, c0:c0 + n] if ch < 4 else at_kT[ch - 4, :, c0:c0 + n]
                P.dma("pool", "aqst%d" % so, dst, qst[so][:, 0:n], reads=["aqst%d" % so],
                      writes=[("at_q", ch, gi)])
            nt = n // 128
            sv = self.nxt("avst", 2)
            for a in range(nt):
                bv = 6 + (a // 2) % 2
                for c in range(8):
                    P.op("pe", lambda e, bv=bv, a=a, c=c, c0=c0: e.matmul(
                        out=banks[bv][:, (a % 2) * 256:(a % 2) * 256 + 256], lhsT=hT[:, c, c0 + a * 128:c0 + (a + 1) * 128],
                        rhs=wv[:, c, :], start=(c == 0), stop=(c == 7)),
                        reads=["awv", ("hT", gi)], writes=["bank%d" % bv])
                if a % 2 == 1 or a == nt - 1:
                    a0 = a - (a % 2)
                    na = a - a0 + 1
                    P.op("act", lambda e, bv=bv, sv=sv, a0=a0, na=na: e.copy(
                        out=vst[sv][:, a0:a0 + na, :], in_=banks[bv][:, 0:na * 256].rearrange("p (a e) -> p a e", e=256)),
                        reads=["bank%d" % bv], writes=["avst%d" % sv])
            P.dma("pool", "avst%d" % sv, at_v[c0:c0 + n, :].rearrange("(a p) e -> p a e", p=128), vst[sv][:, 0:nt, :],
                  reads=["avst%d" % sv], writes=[("at_v", gi)])
        self.release(m1)
        if stop_after == "atproj":
            self.release(m_all)
            return
        self.dn_proj(w_in, hT, groups)
        self.release(m_all)
        if stop_after == "dnproj":
            return

        import os
        if not os.environ.get("SKIP_ATT"):
            self.attention(at_qT, at_kT, at_v, mixT, negm)
        self.dn_main(mixT)
        if stop_after == "dn":
            return
        if stop_after == "attn":
            return

        m1 = self.mark()
        wos = [self.sb("owos%d" % i, [128, D], F32) for i in range(2)]
        wo = self.sb("owo", [128, 8, D], BF16)
        mt = [self.sb("omt%d" % i, [128, 8, 512], BF16) for i in range(2)]
        xg = [self.sb("oxg%d" % i, [128, 8, 512], F32) for i in range(2)]
        for mc in range(8):
            s_ = self.nxt("owos", 2)
            P.dma("sp", "owos%d" % s_, wos[s_][:], w_out[mc * 128:(mc + 1) * 128, :], writes=["owos%d" % s_])
            P.op("pool", lambda e, s_=s_, mc=mc: e.tensor_copy(out=wo[:, mc, :], in_=wos[s_][:]),
                 reads=["owos%d" % s_], writes=["owo"])
        for g in range(8):
            im = self.nxt("omt", 2)
            P.dma("sp", "omt%d" % im, mt[im][:], mixT[:, g * 512:(g + 1) * 512].rearrange("(c p) t -> p c t", p=128),
                  reads=[("mixT", r_, g) for r_ in range(8)], writes=["omt%d" % im])
            ix = self.nxt("oxg", 2)
            P.dma("sp", "oxg%d" % ix, xg[ix][:], xT[:, g * 512:(g + 1) * 512].rearrange("(c p) t -> p c t", p=128),
                  reads=[("xT", g)], writes=["oxg%d" % ix])
            for dc in range(8):
                bi = self.nxt("obank", 4)
                for mc in range(8):
                    P.op("pe", lambda e, bi=bi, mc=mc, dc=dc, im=im: e.matmul(
                        out=banks[bi][:, :], lhsT=wo[:, mc, dc * 128:(dc + 1) * 128], rhs=mt[im][:, mc, :],
                        start=(mc == 0), stop=(mc == 7)), reads=["owo", "omt%d" % im], writes=["bank%d" % bi])
                P.op("dve", lambda e, bi=bi, ix=ix, dc=dc: e.scalar_tensor_tensor(
                    out=xg[ix][:, dc, :], in0=banks[bi][:, :], scalar=vec[:, 1, 2, 0, dc:dc + 1], in1=xg[ix][:, dc, :],
                    op0=ALU.mult, op1=ALU.add), reads=["bank%d" % bi, "oxg%d" % ix, "vec"], writes=["oxg%d" % ix])
            P.dma("pool", "oxgo%d" % ix, xT[:, g * 512:(g + 1) * 512].rearrange("(c p) t -> p c t", p=128), xg[ix][:],
                  reads=["oxg%d" % ix], writes=[("xT", g)])
        self.release(m1)
        self.release(m_top)


    def dn_proj(self, w_in, hT, groups):
        nc, P, banks = self.nc, self.P, self.banks
        cst, rws, identb, identf = self.cst, self.rws, self.identb, self.identf
        dn_T = self.dscr("dn_T", [12, 128, NT], BF16)
        dn_zs = self.dscr("dn_zs", [T, 512], F32)
        self.dn_T, self.dn_zs = dn_T, dn_zs
        m1 = self.mark()
        wst = [self.sb("dwst%d" % i, [128, 8, 128], F32) for i in range(2)]
        wq = [self.sb("dwq%d" % i, [128, 8, 128], BF16) for i in range(2)]
        cw = self.sb("dcw", [128, 12, 5], F32)
        dg = [self.sb("ddg%d" % i, [128, 5, 128], BF16) for i in range(2)]
        pre = [self.sb("dpre%d" % i, [128, NT + 6], BF16) for i in range(2)]
        yb = [self.sb("dyb%d" % i, [128, 512], F32) for i in range(2)]
        sqb = [self.sb("dsqb%d" % i, [128, 512], BF16) for i in range(2)]
        rsb = [self.sb("drsb%d" % i, [128, 512], F32) for i in range(2)]
        yst = [self.sb("dyst%d" % i, [128, 512], BF16) for i in range(2)]
        P.dma("sp", "dcw", cw[:], self.convT_in[:, :, :], writes=["dcw"])
        for i in range(2):
            P.op("pool", lambda e, i=i: e.memset(pre[i][:], 0.0), writes=["dpre%d" % i])
        pos = lambda c0: c0 + 2 if c0 < T else c0 + 4
        import os
        dnp = os.environ.get("DNP", "conv,z,sc")
        for j in range(12 if "conv" in dnp else 0):
            wb = self.load_w_chunk(w_in[:, j * 128:(j + 1) * 128], wst, wq, nst=2, nb=2, bkey="dwq")
            di = self.nxt("ddg", 2)
            for k in range(5):
                P.op("dve", lambda e, di=di, k=k, j=j: e.tensor_scalar(out=dg[di][:, k, :], in0=identf[:], scalar1=cw[:, j, k:k + 1],
                                                                     scalar2=None, op0=ALU.mult),
                     reads=["identf", "dcw"], writes=["ddg%d" % di])
            pi = self.nxt("dpre", 2)
            for gi, (c0, n, kind, gid) in enumerate(groups):
                bi = self.nxt("dpbank", 2)
                for c in range(8):
                    P.op("pe", lambda e, bi=bi, wb=wb, c=c, c0=c0, n=n: e.matmul(
                        out=banks[bi][:, 0:n], lhsT=wq[wb][:, c, :], rhs=hT[:, c, c0:c0 + n], start=(c == 0), stop=(c == 7)),
                        reads=["dwq%d" % wb, ("hT", gi)], writes=["bank%d" % bi])
                P.op("act", lambda e, bi=bi, pi=pi, c0=c0, n=n: e.copy(out=pre[pi][:, pos(c0):pos(c0) + n], in_=banks[bi][:, 0:n]),
                     reads=["bank%d" % bi], writes=["dpre%d" % pi])
            for gi, (c0, n, kind, gid) in enumerate(groups):
                bi = 2 + self.nxt("dcbank", 2)
                for k in range(5):
                    P.op("pe", lambda e, bi=bi, di=di, k=k, pi=pi, c0=c0, n=n: e.matmul(
                        out=banks[bi][:, 0:n], lhsT=dg[di][:, k, :], rhs=pre[pi][:, pos(c0) + k - 2:pos(c0) + k - 2 + n],
                        start=(k == 0), stop=(k == 4)), reads=["ddg%d" % di, "dpre%d" % pi], writes=["bank%d" % bi])
                yi = self.nxt("dyb", 2)
                P.op("act", lambda e, bi=bi, yi=yi, n=n: e.activation(out=yb[yi][:, 0:n], in_=banks[bi][:, 0:n], func=AF.Silu),
                     reads=["bank%d" % bi], writes=["dyb%d" % yi])
                so = self.nxt("dyst", 2)
                if j < 8:
                    P.op("pool", lambda e, yi=yi, n=n: e.tensor_tensor(out=sqb[yi][:, 0:n], in0=yb[yi][:, 0:n], in1=yb[yi][:, 0:n],
                                                                      op=ALU.mult), reads=["dyb%d" % yi], writes=["dsqb%d" % yi])
                    bs = 4 + self.nxt("dsbank", 2)
                    P.op("pe", lambda e, bs=bs, yi=yi, n=n: e.matmul(out=banks[bs][:, 0:n], lhsT=self.onesb[:], rhs=sqb[yi][:, 0:n],
                                                                     start=True, stop=True),
                         reads=["dsqb%d" % yi, "onesb"], writes=["bank%d" % bs])
                    P.op("act", lambda e, bs=bs, yi=yi, n=n: e.activation(out=rsb[yi][:, 0:n], in_=banks[bs][:, 0:n], func=AF.Sqrt,
                                                                          bias=self.epsc[:, 0:1], scale=1.0),
                         reads=["bank%d" % bs, "epsc"], writes=["drsb%d" % yi])
                    P.op("dve", lambda e, yi=yi, n=n: e.reciprocal(out=rsb[yi][:, 0:n], in_=rsb[yi][:, 0:n]),
                         reads=["drsb%d" % yi], writes=["drsb%d" % yi])
                    qsc = float(128 ** -0.5) if j < 4 else 1.0
                    P.op("dve", lambda e, yi=yi, so=so, n=n, qsc=qsc: e.scalar_tensor_tensor(
                        out=yst[so][:, 0:n], in0=yb[yi][:, 0:n], scalar=qsc, in1=rsb[yi][:, 0:n], op0=ALU.mult, op1=ALU.mult),
                        reads=["dyb%d" % yi, "drsb%d" % yi], writes=["dyst%d" % so])
                else:
                    P.op("pool", lambda e, yi=yi, so=so, n=n: e.tensor_copy(out=yst[so][:, 0:n], in_=yb[yi][:, 0:n]),
                         reads=["dyb%d" % yi], writes=["dyst%d" % so])
                P.dma("pool", "dyst%d" % so, dn_T[j, :, c0:c0 + n], yst[so][:, 0:n], reads=["dyst%d" % so], writes=[("dn_T", j, gi)])
        self.release(m1)
        m1 = self.mark()
        wzs = self.sb("dwzs", [128, 8, 512], F32)
        wz = self.sb("dwz", [128, 8, 512], BF16)
        wbas = self.sb("dwbas", [128, 8, 16], F32)
        wba = self.sb("dwba", [128, 8, 16], BF16)
        zst = [self.sb("dzst%d" % i, [128, 512], F32) for i in range(2)]
        ba = self.sb("dba", [128, 34, 16], F32)
        tmp = {k_: self.sb("dt_" + k_, [128, 34, 8], F32) for k_ in ("x", "ax", "e", "l", "g", "lnb", "gl")}
        for c in range(8):
            P.dma("sp", "dwzs", wzs[:, c, :], w_in[c * 128:(c + 1) * 128, 1536:2048], writes=["dwzs"])
        P.op("pool", lambda e: e.tensor_copy(out=wz[:], in_=wzs[:]), reads=["dwzs"], writes=["dwz"])
        P.dma("sp", "dwbas", wbas[:], w_in[:, 2048:2064].rearrange("(c p) n -> p c n", p=128), writes=["dwbas"])
        P.op("pool", lambda e: e.tensor_copy(out=wba[:], in_=wbas[:]), reads=["dwbas"], writes=["dwba"])
        for tt in range(34 if "z" in dnp else 0):
            c0 = tt * 128
            gi = min(tt // 4, 8)
            if tt < 32:
                bi = self.nxt("dzbank", 2)
                for c in range(8):
                    P.op("pe", lambda e, bi=bi, c=c, c0=c0: e.matmul(out=banks[bi][:, :], lhsT=hT[:, c, c0:c0 + 128], rhs=wz[:, c, :],
                                                                     start=(c == 0), stop=(c == 7)),
                         reads=["dwz", ("hT", gi)], writes=["bank%d" % bi])
                zi = self.nxt("dzst", 2)
                P.op("act", lambda e, bi=bi, zi=zi: e.activation(out=zst[zi][:], in_=banks[bi][:, :], func=AF.Silu),
                     reads=["bank%d" % bi], writes=["dzst%d" % zi])
                P.dma("pool", "dzst%d" % zi, dn_zs[c0:c0 + 128, :], zst[zi][:], reads=["dzst%d" % zi], writes=[("dn_zs", tt)])
            bb = 2 if tt < 32 else 3
            col = (tt % 32) * 16
            for c in range(8):
                P.op("pe", lambda e, bb=bb, col=col, c=c, c0=c0: e.matmul(out=banks[bb][:, col:col + 16], lhsT=hT[:, c, c0:c0 + 128],
                                                                        rhs=wba[:, c, :], start=(c == 0), stop=(c == 7)),
                     reads=["dwba", ("hT", gi)], writes=["bank%d" % bb])
        P.op("dve", lambda e: e.tensor_copy(out=ba[:, 0:32, :], in_=banks[2][:, :].rearrange("p (t n) -> p t n", n=16)),
             reads=["bank2"], writes=["dba"])
        P.op("dve", lambda e: e.tensor_copy(out=ba[:, 32:34, :], in_=banks[3][:, 0:32].rearrange("p (t n) -> p t n", n=16)),
             reads=["bank3"], writes=["dba"])
        if "sc" not in dnp:
            self.release(m1)
            return
        self.tap("ba", ba[:].rearrange("p t n -> p (t n)"), [128, 544], ["dba"])
        sc = self.dnsc
        rowb = lambda lo: rws[:, lo:lo + 8].unsqueeze(1).to_broadcast([128, 34, 8])
        P.op("act", lambda e: e.activation(out=sc["beta"][:], in_=ba[:, :, 0:8], func=AF.Sigmoid), reads=["dba"], writes=["s_beta"])
        P.op("act", lambda e: e.activation(out=tmp["lnb"][:], in_=sc["beta"][:], func=AF.Ln), reads=["s_beta"], writes=["t_lnb"])
        P.op("dve", lambda e: e.tensor_tensor(out=tmp["x"][:], in0=ba[:, :, 8:16], in1=rowb(264), op=ALU.add),
             reads=["dba", "rws"], writes=["t_x"])
        P.op("dve", lambda e: e.scalar_tensor_tensor(out=tmp["ax"][:], in0=tmp["x"][:], scalar=-1.0, in1=tmp["x"][:],
                                                     op0=ALU.mult, op1=ALU.max), reads=["t_x"], writes=["t_ax"])
        P.op("act", lambda e: e.activation(out=tmp["e"][:], in_=tmp["ax"][:], func=AF.Exp, scale=-1.0), reads=["t_ax"], writes=["t_e"])
        P.op("act", lambda e: e.activation(out=tmp["l"][:], in_=tmp["e"][:], func=AF.Ln, bias=self.onesf[:, 0:1], scale=1.0),
             reads=["t_e", "onesf"], writes=["t_l"])
        P.op("dve", lambda e: e.scalar_tensor_tensor(out=tmp["l"][:], in0=tmp["x"][:], scalar=0.0, in1=tmp["l"][:],
                                                     op0=ALU.max, op1=ALU.add), reads=["t_x", "t_l"], writes=["t_l"])
        P.op("act", lambda e: e.activation(out=tmp["e"][:, 0, :], in_=rws[:, 256:264], func=AF.Exp), reads=["rws", "t_e"], writes=["t_e"])
        P.op("dve", lambda e: e.scalar_tensor_tensor(out=tmp["g"][:], in0=tmp["l"][:], scalar=-1.0,
                                                     in1=tmp["e"][:, 0:1, :].to_broadcast([128, 34, 8]), op0=ALU.mult, op1=ALU.mult),
             reads=["t_l", "t_e"], writes=["t_g"])
        gsp = self.sb("dgsp", [128, 2, 34, 4], F32)
        for d_ in range(2):
            P.op("dve", lambda e, d_=d_: e.tensor_copy(out=gsp[:, d_, :, :], in_=tmp["g"][:, :, d_ * 4:(d_ + 1) * 4]),
                 reads=["t_g"], writes=["dgsp"])
        P.op("pe", lambda e: e.matmul(out=banks[4][:, 0:136], lhsT=cst[:, 1, :], rhs=gsp[:, 0, :, :].rearrange("p t n -> p (t n)"),
                                      start=True, stop=True), reads=["cst", "dgsp"], writes=["bank4"])
        P.op("pe", lambda e: e.matmul(out=banks[5][:, 0:136], lhsT=cst[:, 2, :], rhs=gsp[:, 1, :, :].rearrange("p t n -> p (t n)"),
                                      start=True, stop=True), reads=["cst", "dgsp"], writes=["bank5"])
        P.op("pe", lambda e: e.matmul(out=banks[6][:, 0:272], lhsT=self.onesf[:], rhs=tmp["g"][:].rearrange("p t n -> p (t n)"),
                                      start=True, stop=True), reads=["onesf", "t_g"], writes=["bank6"])
        P.op("dve", lambda e: e.tensor_copy(out=sc["gc"][:, :, 0:4], in_=banks[4][:, 0:136].rearrange("p (t n) -> p t n", n=4)),
             reads=["bank4"], writes=["s_gc"])
        P.op("dve", lambda e: e.tensor_copy(out=sc["gc"][:, :, 4:8], in_=banks[5][:, 0:136].rearrange("p (t n) -> p t n", n=4)),
             reads=["bank5"], writes=["s_gc"])
        P.op("dve", lambda e: e.tensor_copy(out=tmp["gl"][:], in_=banks[6][:, 0:272].rearrange("p (t n) -> p t n", n=8)),
             reads=["bank6"], writes=["t_gl"])
        P.op("dve", lambda e: e.tensor_scalar(out=sc["ngc"][:], in0=sc["gc"][:], scalar1=-1.0, scalar2=None, op0=ALU.mult),
             reads=["s_gc"], writes=["s_ngc"])
        P.op("dve", lambda e: e.tensor_tensor(out=sc["gcl"][:], in0=sc["gc"][:], in1=tmp["lnb"][:], op=ALU.add),
             reads=["s_gc", "t_lnb"], writes=["s_gcl"])
        P.op("act", lambda e: e.activation(out=tmp["e"][:], in_=sc["gc"][:], func=AF.Exp), reads=["s_gc", "t_e"], writes=["t_e"])
        P.op("dve", lambda e: e.tensor_tensor(out=sc["bet"][:], in0=sc["beta"][:], in1=tmp["e"][:], op=ALU.mult),
             reads=["s_beta", "t_e"], writes=["s_bet"])
        P.op("dve", lambda e: e.tensor_tensor(out=tmp["x"][:], in0=tmp["gl"][:], in1=sc["gc"][:], op=ALU.subtract),
             reads=["t_gl", "s_gc", "t_x"], writes=["t_x"])
        P.op("act", lambda e: e.activation(out=sc["et"][:], in_=tmp["x"][:], func=AF.Exp), reads=["t_x"], writes=["s_et"])
        P.op("act", lambda e: e.activation(out=sc["egl"][:], in_=tmp["gl"][:], func=AF.Exp), reads=["t_gl"], writes=["s_egl"])
        for k_ in ("beta", "gc", "et", "bet", "egl", "gcl"):
            self.tap("sc_" + k_, sc[k_][:].rearrange("p t n -> p (t n)"), [128, 272], ["s_" + k_])
        self.release(m1)


    def dn_main(self, mixT):
        nc, P, banks = self.nc, self.P, self.banks
        cst, rws, identb, identf, sc = self.cst, self.rws, self.identb, self.identf, self.dnsc
        dn_T, dn_zs = self.dn_T, self.dn_zs
        m1 = self.mark()
        o_sb = self.sb("n_o", [128, 32, 4, 128], F32)
        S = self.sb("n_S", [128, 8, 128], F32)
        Sb = self.sb("n_Sb", [128, 8, 128], BF16)
        dnt = [self.sb("n_dnt%d" % i, [128, 12, 128], BF16) for i in range(4)]
        NS = 2
        tl = []
        for sl in range(NS):
            t = {}
            for nm in ("E", "ET", "ER", "Dm", "Ym", "u"):
                t[nm] = self.sb("n_%s%d" % (nm, sl), [128, 128], F32)
            for nm in ("attnT", "qhT", "ktail", "kbe", "bv", "wTn", "vnew"):
                t[nm] = self.sb("n_%s%d" % (nm, sl), [128, 128], BF16)
            t["XY0"] = self.sb("n_XY0%d" % sl, [128, 2, 128], BF16)
            t["W"] = self.sb("n_W%d" % sl, [128, 2, 128], BF16)
            t["UD"] = self.sb("n_UD%d" % sl, [128, 2, 128], BF16)
            tl.append(t)
        lvm = self.sb("n_lvm", [128, 7, 2, 128], F32)
        self.lvm = lvm
        identb2 = self.sb("n_idb2", [128, 2, 128], BF16)
        self.identb2 = identb2
        P.dma("sp", "lvm", lvm[:], self.lvmask_in[:, :, :, :], writes=["lvm"])
        for a_ in range(2):
            P.op("dve", lambda e, a_=a_: e.tensor_copy(out=identb2[:, a_, :], in_=identf[:]), reads=["identf"], writes=["identb2"])
        P.op("pool", lambda e: e.memset(S[:], 0.0), writes=["n_S%d" % c for c in range(8)])
        P.op("pool", lambda e: e.memset(Sb[:], 0.0), writes=["n_Sb%d" % c for c in range(8)])
        order_f = [32, 33] + list(range(32))
        order_b = [33, 32] + list(range(31, -1, -1))
        touched = set()
        import os
        nsteps = int(os.environ.get("DN_STEPS", "34"))
        dn_stage = int(os.environ.get("DN_STAGE", "99"))
        for m in range(nsteps):
            for d in range(2):
                tt = (order_f if d == 0 else order_b)[m]
                lat = tt < 32
                ib = self.nxt("n_dnt", 4)
                kd = "n_dnt%d" % ib
                for j0 in range(0, 12, 4):
                    P.dma("sp", kd, dnt[ib][:, j0:j0 + 4, :], dn_T[j0:j0 + 4, :, tt * 128:(tt + 1) * 128].rearrange("j p t -> p j t"),
                          writes=[kd])
                for h in range(4):
                    c = d * 4 + h
                    sl = self.nxt("n_slot", NS)
                    t = tl[sl]
                    K_ = lambda nm, sl=sl: "s%d_%s" % (sl, nm)
                    bA, bB, bC, bD = (banks[4 * sl + i] for i in range(4))
                    kA, kB, kC, kD = ("bank%d" % (4 * sl + i) for i in range(4))
                    qT, kT, vT = dnt[ib][:, h, :], dnt[ib][:, 4 + h, :], dnt[ib][:, 8 + h, :]
                    scol = lambda nm, tt=tt, c=c: sc[nm][:, tt, c:c + 1]
                    maski = cst[:, 3 + d, :]
                    nmask = cst[:, 5 + d, :]
                    P.op("pe", lambda e, bA=bA, kT=kT: e.matmul(out=bA[:, 0:128], lhsT=kT, rhs=kT, start=True, stop=True),
                         reads=[kd], writes=[kA])
                    P.op("pe", lambda e, bA=bA, kT=kT, qT=qT: e.matmul(out=bA[:, 128:256], lhsT=kT, rhs=qT, start=True, stop=True),
                         reads=[kd], writes=[kA])
                    P.op("pe", lambda e, bB=bB, g_=scol("gc"): e.matmul(out=bB[:, 0:128], lhsT=g_.to_broadcast([128, 128]), rhs=identf[:],
                                                                       start=True, stop=True), reads=["s_gc", "identf"], writes=[kB])
                    P.op("pe", lambda e, bA=bA, kT=kT: e.matmul(out=bA[:, 256:384], lhsT=kT, rhs=identb[:], start=True, stop=True),
                         reads=[kd, "identb"], writes=[kA])
                    P.op("pe", lambda e, bA=bA, vT=vT: e.matmul(out=bA[:, 384:512], lhsT=vT, rhs=identb[:], start=True, stop=True),
                         reads=[kd, "identb"], writes=[kA])
                    if dn_stage < 1:
                        continue
                    P.op("act", lambda e, t=t, bB=bB, b_=scol("ngc"): e.activation(out=t["E"][:], in_=bB[:, 0:128], func=AF.Exp,
                                                                                   bias=b_, scale=1.0),
                         reads=[kB, "s_ngc"], writes=[K_("E")])
                    P.op("act", lambda e, t=t, bB=bB, b_=scol("gcl"): e.activation(out=t["ET"][:], in_=bB[:, 0:128], func=AF.Exp,
                                                                                   bias=b_, scale=-1.0),
                         reads=[kB, "s_gcl"], writes=[K_("ET")])
                    P.op("act", lambda e, t=t, bB=bB: e.activation(out=t["ER"][:], in_=bB[:, 0:128], func=AF.Exp),
                         reads=[kB], writes=[K_("ER")])
                    if dn_stage < 2:
                        continue
                    P.op("dve", lambda e, t=t, maski=maski: e.scalar_tensor_tensor(out=t["Dm"][:], in0=t["E"][:], scalar=1.0, in1=maski,
                                                                                  op0=ALU.min, op1=ALU.mult),
                         reads=[K_("E"), "cst"], writes=[K_("Dm")])
                    P.op("dve", lambda e, t=t, bA=bA: e.tensor_tensor(out=t["attnT"][:], in0=bA[:, 128:256], in1=t["Dm"][:], op=ALU.mult),
                         reads=[kA, K_("Dm")], writes=[K_("attnT")])
                    P.op("dve", lambda e, t=t, nmask=nmask: e.scalar_tensor_tensor(out=t["Ym"][:], in0=t["ET"][:], scalar=1.0, in1=nmask,
                                                                                  op0=ALU.min, op1=ALU.mult),
                         reads=[K_("ET"), "cst"], writes=[K_("Ym")])
                    P.op("dve", lambda e, t=t, bA=bA: e.tensor_tensor(out=t["XY0"][:, 1, :], in0=bA[:, 0:128], in1=t["Ym"][:], op=ALU.mult),
                         reads=[kA, K_("Ym")], writes=[K_("XY0")])
                    P.op("pool", lambda e, t=t, qT=qT: e.tensor_tensor(out=t["qhT"][:], in0=qT, in1=t["ER"][:], op=ALU.mult),
                         reads=[kd, K_("ER")], writes=[K_("qhT")])
                    P.op("dve", lambda e, t=t, bA=bA, s_=scol("et"): e.tensor_scalar(out=t["ktail"][:], in0=bA[:, 256:384], scalar1=s_,
                                                                                    scalar2=None, op0=ALU.mult),
                         reads=[kA, "s_et"], writes=[K_("ktail")])
                    P.op("dve", lambda e, t=t, bA=bA, s_=scol("bet"): e.tensor_scalar(out=t["kbe"][:], in0=bA[:, 256:384], scalar1=s_,
                                                                                     scalar2=None, op0=ALU.mult),
                         reads=[kA, "s_bet"], writes=[K_("kbe")])
                    P.op("dve", lambda e, t=t, bA=bA, s_=scol("beta"): e.tensor_scalar(out=t["bv"][:], in0=bA[:, 384:512], scalar1=s_,
                                                                                      scalar2=None, op0=ALU.mult),
                         reads=[kA, "s_beta"], writes=[K_("bv")])
                    if dn_stage < 3:
                        continue
                    P.op("pe", lambda e, t=t, bB=bB: e.matmul(out=bB[:, 128:256], lhsT=t["XY0"][:, 1, :], rhs=identb[:], start=True, stop=True),
                         reads=[K_("XY0"), "identb"], writes=[kB])
                    P.op("dve", lambda e, t=t, bB=bB: e.tensor_copy(out=t["XY0"][:, 0, :], in_=bB[:, 128:256]), reads=[kB], writes=[K_("XY0")])
                    if dn_stage < 4:
                        continue
                    lv = self.lvm
                    mk = lambda l, d=d: (lv[:, l, :, :] if d == 0 else None)
                    P.op("pool", lambda e, t=t, d=d: e.tensor_tensor(out=t["W"][:, 0, :], in0=t["XY0"][:, 0, :], in1=lv[:, 0, d, :], op=ALU.mult),
                         reads=[K_("XY0"), "lvm"], writes=[K_("W")])
                    P.op("pool", lambda e, t=t, d=d: e.tensor_tensor(out=t["W"][:, 1, :], in0=t["XY0"][:, 1, :], in1=lv[:, 0, 1 - d, :], op=ALU.mult),
                         reads=[K_("XY0"), "lvm"], writes=[K_("W")])
                    P.op("pool", lambda e, t=t: e.tensor_tensor(out=t["UD"][:], in0=t["W"][:], in1=self.identb2[:], op=ALU.add),
                         reads=[K_("W"), "identb2"], writes=[K_("UD")])
                    for l in range(1, 7):
                        P.op("pe", lambda e, t=t, bC=bC: e.matmul(out=bC[:, 0:128], lhsT=t["XY0"][:, 1, :], rhs=t["UD"][:, 0, :], start=True, stop=True),
                             reads=[K_("XY0"), K_("UD")], writes=[kC])
                        P.op("pe", lambda e, t=t, bC=bC: e.matmul(out=bC[:, 128:256], lhsT=t["XY0"][:, 0, :], rhs=t["UD"][:, 1, :], start=True, stop=True),
                             reads=[K_("XY0"), K_("UD")], writes=[kC])
                        P.op("dve", lambda e, t=t, bC=bC, l=l, d=d: e.tensor_tensor(out=t["W"][:, 0, :], in0=bC[:, 0:128], in1=lv[:, l, d, :], op=ALU.mult),
                             reads=[kC, "lvm"], writes=[K_("W")])
                        P.op("dve", lambda e, t=t, bC=bC, l=l, d=d: e.tensor_tensor(out=t["W"][:, 1, :], in0=bC[:, 128:256], in1=lv[:, l, 1 - d, :], op=ALU.mult),
                             reads=[kC, "lvm"], writes=[K_("W")])
                        P.op("pe", lambda e, t=t, bD=bD: e.matmul(out=bD[:, 0:128], lhsT=t["UD"][:, 1, :], rhs=t["W"][:, 0, :], start=True, stop=True),
                             reads=[K_("UD"), K_("W")], writes=[kD])
                        P.op("pe", lambda e, t=t, bD=bD: e.matmul(out=bD[:, 128:256], lhsT=t["UD"][:, 0, :], rhs=t["W"][:, 1, :], start=True, stop=True),
                             reads=[K_("UD"), K_("W")], writes=[kD])
                        P.op("dve", lambda e, t=t, bD=bD: e.tensor_tensor(out=t["UD"][:], in0=bD[:, 0:256].rearrange("p (a b) -> p a b", a=2),
                                                                         in1=t["UD"][:], op=ALU.add), reads=[kD, K_("UD")], writes=[K_("UD")])
                    if m == 0 and d == 0 and h == 0:
                        self.tap("c0_UD", t["UD"][:].rearrange("p a b -> p (a b)"), [128, 256], [K_("UD")], BF16)
                    if dn_stage < 5:
                        continue
                    P.op("pe", lambda e, t=t, bD=bD: e.matmul(out=bD[:, 256:384], lhsT=t["UD"][:, 0, :], rhs=t["bv"][:], start=True, stop=True),
                         reads=[K_("UD"), K_("bv")], writes=[kD])
                    P.op("pe", lambda e, t=t, bC=bC: e.matmul(out=bC[:, 256:384], lhsT=t["kbe"][:], rhs=t["UD"][:, 0, :], start=True, stop=True),
                         reads=[K_("UD"), K_("kbe")], writes=[kC])
                    P.op("dve", lambda e, t=t, bD=bD: e.tensor_copy(out=t["u"][:], in_=bD[:, 256:384]), reads=[kD], writes=[K_("u")])
                    P.op("dve", lambda e, t=t, bC=bC: e.tensor_scalar(out=t["wTn"][:], in0=bC[:, 256:384], scalar1=-1.0, scalar2=None, op0=ALU.mult),
                         reads=[kC], writes=[K_("wTn")])
                    if dn_stage < 6:
                        continue
                    kS, kSb = "n_S%d" % c, "n_Sb%d" % c
                    P.op("pe", lambda e, t=t, bD=bD, c=c: e.matmul(out=bD[:, 384:512], lhsT=t["wTn"][:], rhs=Sb[:, c, :], start=True, stop=True),
                         reads=[K_("wTn"), kSb], writes=[kD])
                    P.op("dve", lambda e, t=t, bD=bD: e.tensor_tensor(out=t["vnew"][:], in0=bD[:, 384:512], in1=t["u"][:], op=ALU.add),
                         reads=[kD, K_("u")], writes=[K_("vnew")])
                    P.op("pe", lambda e, t=t, bB=bB: e.matmul(out=bB[:, 384:512], lhsT=t["ktail"][:], rhs=t["vnew"][:], start=True, stop=True),
                         reads=[K_("ktail"), K_("vnew")], writes=[kB])
                    if lat:
                        P.op("pe", lambda e, t=t, bB=bB, c=c: e.matmul(out=bB[:, 256:384], lhsT=t["qhT"][:], rhs=Sb[:, c, :], start=True, stop=False),
                             reads=[K_("qhT"), kSb], writes=[kB])
                        P.op("pe", lambda e, t=t, bB=bB: e.matmul(out=bB[:, 256:384], lhsT=t["attnT"][:], rhs=t["vnew"][:], start=False, stop=True),
                             reads=[K_("attnT"), K_("vnew")], writes=[kB])
                        ko = ("n_o", tt, h)
                        if ko not in touched:
                            touched.add(ko)
                            P.op("dve", lambda e, bB=bB, tt=tt, h=h: e.tensor_copy(out=o_sb[:, tt, h, :], in_=bB[:, 256:384]), reads=[kB], writes=[ko])
                        else:
                            P.op("dve", lambda e, bB=bB, tt=tt, h=h: e.tensor_tensor(out=o_sb[:, tt, h, :], in0=bB[:, 256:384],
                                                                                    in1=o_sb[:, tt, h, :], op=ALU.add),
                                 reads=[kB, ko], writes=[ko])
                    P.op("dve", lambda e, bB=bB, c=c, s_=scol("egl"): e.scalar_tensor_tensor(
                        out=S[:, c, :], in0=S[:, c, :], scalar=s_, in1=bB[:, 384:512], op0=ALU.mult, op1=ALU.add),
                        reads=[kS, kB, "s_egl"], writes=[kS])
                    P.op("act", lambda e, c=c: e.copy(out=Sb[:, c, :], in_=S[:, c, :]), reads=[kS], writes=[kSb])
                    if m == 0 and d == 0 and h == 0:
                        for nm in ("E", "ET", "ER", "Dm", "Ym", "u"):
                            self.tap("c0_" + nm, t[nm][:], [128, 128], [K_(nm)])
                        for nm in ("attnT", "qhT", "ktail", "kbe", "bv", "wTn", "vnew"):
                            self.tap("c0_" + nm, t[nm][:], [128, 128], [K_(nm)], BF16)
                        self.tap("c0_XY0", t["XY0"][:].rearrange("p a b -> p (a b)"), [128, 256], [K_("XY0")], BF16)
        if "dn_o" in self.taps:
            t_o = self.nc.dram_tensor("dn_o", [128, 32 * 512], F32, kind="ExternalOutput").ap()
            for a0 in range(0, 32, 4):
                P.dma("sp", "tap_dn_o", t_o[:, a0 * 512:(a0 + 4) * 512], o_sb[:, a0:a0 + 4, :, :].rearrange("p a h e -> p (a h e)"),
                      reads=[("n_o", tt_, h_) for tt_ in range(a0, a0 + 4) for h_ in range(4)])
        self.tap("dn_S", S[:].rearrange("p c e -> p (c e)"), [128, 1024], ["n_S%d" % c for c in range(8)])
        ss = self.sb("n_ss", [128, 32, 4], F32)
        sqt = [self.sb("n_sqt%d" % i, [128, 4, 128], F32) for i in range(2)]
        zt = [self.sb("n_zt%d" % i, [128, 4, 128], F32) for i in range(2)]
        yb = [self.sb("n_yb%d" % i, [128, 4, 128], BF16) for i in range(2)]
        yo = [self.sb("n_yo%d" % i, [128, 4, 128], BF16) for i in range(2)]
        okeys = lambda tt: [("n_o", tt, h_) for h_ in range(4)]
        for tt in range(32):
            i2 = self.nxt("n_sqt", 2)
            P.op("pool", lambda e, i2=i2, tt=tt: e.tensor_tensor(out=sqt[i2][:], in0=o_sb[:, tt, :, :], in1=o_sb[:, tt, :, :], op=ALU.mult),
                 reads=okeys(tt), writes=["n_sqt%d" % i2])
            P.op("dve", lambda e, i2=i2, tt=tt: e.tensor_reduce(out=ss[:, tt, :], in_=sqt[i2][:], axis=mybir.AxisListType.X, op=ALU.add),
                 reads=["n_sqt%d" % i2], writes=["n_ss"])
        P.op("act", lambda e: e.activation(out=ss[:], in_=ss[:], func=AF.Sqrt, bias=self.epsc[:, 0:1], scale=1.0 / 128),
             reads=["n_ss", "epsc"], writes=["n_ss"])
        P.op("dve", lambda e: e.reciprocal(out=ss[:], in_=ss[:]), reads=["n_ss"], writes=["n_ss"])
        for tt in range(32):
            iz = self.nxt("n_zt", 2)
            P.dma("sp", "n_zt%d" % iz, zt[iz][:], dn_zs[tt * 128:(tt + 1) * 128, :].rearrange("p (h e) -> p h e", h=4), writes=["n_zt%d" % iz])
            i2 = self.nxt("n_sqt", 2)
            P.op("dve", lambda e, i2=i2, tt=tt: e.tensor_tensor(out=sqt[i2][:], in0=o_sb[:, tt, :, :],
                                                               in1=ss[:, tt, :].unsqueeze(2).to_broadcast([128, 4, 128]), op=ALU.mult),
                 reads=okeys(tt) + ["n_ss"], writes=["n_sqt%d" % i2])
            P.op("pool", lambda e, i2=i2: e.tensor_tensor(out=sqt[i2][:], in0=sqt[i2][:],
                                                         in1=rws[:, 272:400].unsqueeze(1).to_broadcast([128, 4, 128]), op=ALU.mult),
                 reads=["n_sqt%d" % i2, "rws"], writes=["n_sqt%d" % i2])
            iy = self.nxt("n_yb", 2)
            P.op("dve", lambda e, i2=i2, iz=iz, iy=iy: e.tensor_tensor(out=yb[iy][:], in0=sqt[i2][:], in1=zt[iz][:], op=ALU.mult),
                 reads=["n_sqt%d" % i2, "n_zt%d" % iz], writes=["n_yb%d" % iy])
            bi = self.nxt("n_tbank", 2)
            for h in range(4):
                P.op("pe", lambda e, bi=bi, iy=iy, h=h: e.matmul(out=banks[bi][:, h * 128:(h + 1) * 128], lhsT=yb[iy][:, h, :], rhs=identb[:],
                                                                 start=True, stop=True), reads=["n_yb%d" % iy, "identb"], writes=["bank%d" % bi])
            io = self.nxt("n_yo", 2)
            P.op("act", lambda e, bi=bi, io=io: e.copy(out=yo[io][:], in_=banks[bi][:, :].rearrange("p (h t) -> p h t", h=4)),
                 reads=["bank%d" % bi], writes=["n_yo%d" % io])
            P.dma("pool", "n_yo%d" % io, mixT[0:512, tt * 128:(tt + 1) * 128].rearrange("(h p) t -> p h t", p=128), yo[io][:],
                  reads=["n_yo%d" % io], writes=[("mixT", h_, tt // 4) for h_ in range(4)])
        self.release(m1)

    def attention(self, at_qT, at_kT, at_v, mixT, negm):
        nc, P, banks = self.nc, self.P, self.banks
        m1 = self.mark()
        kT = self.sb("kkT", [128, 2, NT], BF16)
        vt = self.sb("kvt", [128, 34, 256], BF16)
        qs = [self.sb("kqs%d" % i, [128, 512], BF16) for i in range(2)]
        pT = [self.sb("kpT%d" % i, [128, 512], BF16) for i in range(3)]
        rl = [self.sb("krl%d" % i, [128, 512], F32) for i in range(2)]
        ot = [self.sb("kot%d" % i, [128, 512], BF16) for i in range(2)]
        P.dma("sp", "kkT", kT[:], at_kT.rearrange("g p t -> p g t"), writes=["kkT"])
        for a0 in range(0, 34, 6):
            a1 = min(34, a0 + 6)
            P.dma("sp", "kvt", vt[:, a0:a1, :], at_v[a0 * 128:a1 * 128, :].rearrange("(a p) e -> p a e", p=128), writes=["kvt"])
        scale = float(128 ** -0.5)
        import os
        adbg = bool(os.environ.get("ATT_DEBUG"))
        aiters = int(os.environ.get("ATT_ITERS", "1000"))
        acount = 0
        for g in range(2):
            for hh in range(2):
                h = 2 * g + hh
                for qg in range(8):
                    acount += 1
                    if acount > aiters:
                        continue
                    iq = self.nxt("kqs", 2)
                    P.dma("sp", "kqs%d" % iq, qs[iq][:], at_qT[h, :, qg * 512:(qg + 1) * 512], writes=["kqs%d" % iq])
                    par = self.nxt("kacc", 2)
                    OA, LA = banks[6 + par], banks[4 + par]
                    ko, kl = "bank%d" % (6 + par), "bank%d" % (4 + par)

                    def smm(kt, iq=iq, g=g):
                        sbk = kt % 4
                        P.op("pe", lambda e: e.matmul(out=banks[sbk][:, :], lhsT=kT[:, g, kt * 128:(kt + 1) * 128], rhs=qs[iq][:],
                                                      start=True, stop=True),
                             reads=["kkT", "kqs%d" % iq], writes=["bank%d" % sbk])
                    smm(0)
                    for kt in range(34):
                        if kt + 1 < 34:
                            smm(kt + 1)
                        sbk = kt % 4
                        ip = self.nxt("kpT", 3)
                        P.op("act", lambda e, sbk=sbk, ip=ip: e.activation(out=pT[ip][:], in_=banks[sbk][:, :], func=AF.Exp,
                                                                           scale=scale, bias=negm[:, 0:1]),
                             reads=["bank%d" % sbk, "negm"], writes=["kpT%d" % ip])
                        if adbg and kt in (0, 5) and acount == 1:
                            self.tap("pT%d" % kt, pT[ip][:], [128, 512], ["kpT%d" % ip], BF16)
                        P.op("pe", lambda e, kt=kt, ip=ip, OA=OA, g=g: e.matmul(
                            out=OA[:, :], lhsT=vt[:, kt, g * 128:(g + 1) * 128], rhs=pT[ip][:], start=(kt == 0), stop=(kt == 33)),
                            reads=["kvt", "kpT%d" % ip], writes=[ko])
                        P.op("pe", lambda e, kt=kt, ip=ip, LA=LA: e.matmul(
                            out=LA[:, :], lhsT=self.onesb[:], rhs=pT[ip][:], start=(kt == 0), stop=(kt == 33)),
                            reads=["onesb", "kpT%d" % ip], writes=[kl])
                    ir = self.nxt("krl", 2)
                    P.op("dve", lambda e, ir=ir, LA=LA: e.reciprocal(out=rl[ir][:], in_=LA[:, :]), reads=[kl], writes=["krl%d" % ir])
                    if adbg and acount == 1:
                        self.tap("rl", rl[ir][:], [128, 512], ["krl%d" % ir])
                    P.op("dve", lambda e, ir=ir, OA=OA: e.tensor_tensor(out=ot[ir][:], in0=OA[:, :], in1=rl[ir][:], op=ALU.mult),
                         reads=[ko, "krl%d" % ir], writes=["kot%d" % ir])
                    P.dma("pool", "kot%d" % ir, mixT[(4 + h) * 128:(5 + h) * 128, qg * 512:(qg + 1) * 512], ot[ir][:],
                          reads=["kot%d" % ir], writes=[("mixT", 4 + h, qg)])
        self.release(m1)

    def final_norm(self, out, gv):
        nc, P, banks, xT = self.nc, self.P, self.banks, self.xT
        identf = self.identf
        m0 = self.mark()
        xg = [self.sb("fxg%d" % i, [128, 8, 512], F32) for i in range(2)]
        sq = self.sb("fsq", [128, 8, 512], BF16)
        rs = [self.sb("frs%d" % i, [128, 512], F32) for i in range(2)]
        yo = [self.sb("fyo%d" % i, [128, 4, D], F32) for i in range(2)]
        for g in range(8):
            n = 512
            s = self.nxt("fxg", 2)
            P.dma("sp", "fxg%d" % s, xg[s][:], xT[:, g * 512:(g + 1) * 512].rearrange("(c p) t -> p c t", p=128),
                  reads=[("xT", g)], writes=["fxg%d" % s])
            bi = self.nxt("fstbank", 2)
            r = self.nxt("frs", 2)
            self.rms_stats((xg[s], "fxg%d" % s), n, (sq, "fsq"), "bank%d" % bi, banks[bi], (rs[r], "frs%d" % r))
            for c in range(8):
                P.op("dve", lambda e, s=s, r=r, c=c: e.scalar_tensor_tensor(
                    out=xg[s][:, c, :], in0=xg[s][:, c, :], scalar=gv[:, 3, c:c + 1], in1=rs[r][:, :],
                    op0=ALU.mult, op1=ALU.mult), reads=["fxg%d" % s, "frs%d" % r, "gv"], writes=["fxg%d" % s])
            so = self.nxt("fyo", 2)
            for a in range(4):
                for half in range(2):
                    bi2 = 2 + self.nxt("fobank", 4)
                    bk = banks[bi2]
                    for cc in range(4):
                        c = half * 4 + cc
                        P.op("pe", lambda e, s=s, a=a, c=c, cc=cc, bk=bk: e.transpose(
                            out=bk[:, cc * 128:(cc + 1) * 128], in_=xg[s][:, c, a * 128:(a + 1) * 128], identity=identf[:]),
                            reads=["fxg%d" % s, "identf"], writes=["bank%d" % bi2])
                    if (a + half) % 2 == 0:
                        P.op("act", lambda e, so=so, a=a, half=half, bk=bk: e.copy(
                            out=yo[so][:, a, half * 512:(half + 1) * 512], in_=bk[:, :]),
                            reads=["bank%d" % bi2], writes=["fyo%d" % so])
                    else:
                        P.op("dve", lambda e, so=so, a=a, half=half, bk=bk: e.tensor_copy(
                            out=yo[so][:, a, half * 512:(half + 1) * 512], in_=bk[:, :]),
                            reads=["bank%d" % bi2], writes=["fyo%d" % so])
            P.dma("pool", "fyo%d" % so, out[g * 512:(g + 1) * 512, :].rearrange("(a p) d -> p a d", p=128), yo[so][:],
                  reads=["fyo%d" % so], writes=[("out", g)])
        self.release(m0)


def _col(v):
    v = np.asarray(v, np.float32).reshape(-1, 128)
    return np.ascontiguousarray(v.T)


def make_inputs(inputs, core):
    b = core % 4
    f = lambda a: np.ascontiguousarray(np.asarray(a, np.float32))
    cc = np.stack([np.asarray(inputs["c"])[b], np.asarray(inputs["c_ctx"])], axis=1).astype(np.float32)
    ccT = np.ascontiguousarray(cc.reshape(8, 128, 2).transpose(1, 0, 2))
    gvec = np.stack([_col(inputs["g_ffn1"][0]), _col(inputs["g_mix"][0]), _col(inputs["g_ffn2"][0]),
                     _col(inputs["g_final"])], axis=1)
    m = {
        "x": f(inputs["x"][b]),
        "ctx": f(inputs["ctx"][b]),
        "ccT": ccT,
        "w_mod": f(inputs["w_mod"][0]),
        "b_modT": _col(inputs["b_mod"][0]),
        "gvec": np.ascontiguousarray(gvec),
        "ffn1_w1": f(inputs["ffn1_w1"][0]), "ffn1_w3": f(inputs["ffn1_w3"][0]), "ffn1_w2": f(inputs["ffn1_w2"][0]),
        "w_in": f(inputs["w_in"][0]), "w_out": f(inputs["w_out"][0]),
        "consts": _CACHE.setdefault("consts", _consts()),
        "lvmask": _CACHE.setdefault("lvmask", _lvmask()),
        "rows": np.ascontiguousarray(np.tile(np.concatenate([
            np.asarray(inputs["q_norm"][0], np.float32), np.asarray(inputs["k_norm"][0], np.float32),
            np.asarray(inputs["dn_a_log"][0], np.float32).reshape(-1),
            np.asarray(inputs["dn_dt_bias"][0], np.float32).reshape(-1),
            np.asarray(inputs["dn_norm"][0], np.float32)])[None, :], (128, 1))),
        "convT": np.ascontiguousarray(np.asarray(inputs["dn_conv"][0], np.float32).reshape(5, 12, 128).transpose(2, 1, 0)),
        "ropeC": _CACHE.setdefault("rope", _rope_tables())[0], "ropeS": _CACHE.setdefault("rope", _rope_tables())[1],
        "ffn2_w1": f(inputs["ffn2_w1"][0]), "ffn2_w3": f(inputs["ffn2_w3"][0]), "ffn2_w2": f(inputs["ffn2_w2"][0]),
    }
    return m


def _consts():
    i = np.arange(128)
    pm = (i[:, None] == (i[None, :] ^ 1)).astype(np.float32)
    tri_le = (i[:, None] <= i[None, :]).astype(np.float32)
    tri_ge = (i[:, None] >= i[None, :]).astype(np.float32)
    maski_f = (i[None, :] >= i[:, None]).astype(np.float32)
    maski_b = (i[None, :] <= i[:, None]).astype(np.float32)
    nmasks_f = -(i[:, None] > i[None, :]).astype(np.float32)
    nmasks_b = -(i[:, None] < i[None, :]).astype(np.float32)
    return np.ascontiguousarray(np.stack([pm, tri_le, tri_ge, maski_f, maski_b, nmasks_f, nmasks_b], axis=1))


def _lvmask():
    i = np.arange(128)
    out = np.zeros((128, 7, 2, 128), np.float32)
    for l in range(7):
        sz = 1 << l
        same = (i[:, None] // (2 * sz)) == (i[None, :] // (2 * sz))
        r2 = ((i[:, None] // sz) % 2) == 1
        c1 = ((i[None, :] // sz) % 2) == 0
        mlow = (same & r2 & c1).astype(np.float32)
        out[:, l, 1, :] = mlow
        out[:, l, 0, :] = mlow.T
    return out


def _rope_tables():
    t = np.arange(T)
    row = (t // 64).astype(np.float32)
    col = (t % 64).astype(np.float32)
    freqs = (1.0 / (np.float32(10000.0) ** (np.arange(0, 64, 2, dtype=np.float32) / np.float32(64)))).astype(np.float32)
    ang = np.concatenate([row[:, None] * freqs[None, :], col[:, None] * freqs[None, :]], axis=1).astype(np.float32)
    cos = np.cos(ang).astype(np.float32)
    sin = np.sin(ang).astype(np.float32)
    C = np.repeat(cos, 2, axis=1).T
    S = np.repeat(sin, 2, axis=1).T.copy()
    S[0::2, :] *= -1.0
    return np.ascontiguousarray(C), np.ascontiguousarray(S)


_CACHE = {}


def kernel(**inputs):
    if "nc" not in _CACHE:
        _CACHE["nc"] = Builder().build()
    nc = _CACHE["nc"]
    in_maps = [make_inputs(inputs, c) for c in range(8)]
    res = run_bass_kernel_spmd(nc, in_maps, core_ids=list(range(8)))
    outs = [res.results[c]["out"] for c in range(4)]
    return np.stack(outs, axis=0).astype(np.float32)
```

```python
import contextlib
import numpy as np
import concourse.bass as bass
import concourse.mybir as mybir
from concourse.bass_utils import run_bass_kernel_spmd

F32 = mybir.dt.float32
BF16 = mybir.dt.bfloat16
AF = mybir.ActivationFunctionType
ALU = mybir.AluOpType

D = 1024
T = 4096
TC = 256
NT = T + TC
FF = 2816
NFF = FF // 128
PIN = 3088
EPS = 1e-6
ENG = ("pe", "act", "dve", "pool", "sp")
import os as _os
DMAQ = {"pool": _os.environ.get("POOLQ", "pool")}


class _Op:
    __slots__ = ("eng", "fn", "deps", "signal", "count", "is_dma", "key")

    def __init__(self, eng, fn, is_dma, key):
        self.eng = eng
        self.fn = fn
        self.deps = set()
        self.signal = False
        self.count = 0
        self.is_dma = is_dma
        self.key = key


class Prog:
    def __init__(self, nc):
        self.nc = nc
        self.ops = []
        self.last_w = {}
        self.readers = {}
        self.key_last = {}
        self.key_n = {}
        self.eng_last = {}

    def _add(self, eng, fn, reads, writes, is_dma=False, key=None):
        op = _Op(eng, fn, is_dma, key)
        deps = op.deps
        for r in reads:
            w = self.last_w.get(r)
            if w is not None:
                deps.add(w)
        for w_ in writes:
            w = self.last_w.get(w_)
            if w is not None:
                deps.add(w)
            for rd in self.readers.get(w_, ()):
                deps.add(rd)
        if is_dma:
            prev = self.key_last.get(key)
            if prev is not None:
                deps.add(prev)
            self.key_last[key] = op
            self.key_n[key] = self.key_n.get(key, 0) + 1
            op.count = 16 * self.key_n[key]
        else:
            self.eng_last[eng] = op
        deps.discard(op)
        for r in reads:
            self.readers.setdefault(r, []).append(op)
        for w_ in writes:
            self.last_w[w_] = op
            self.readers[w_] = []
        self.ops.append(op)
        return op

    def op(self, eng, fn, reads=(), writes=()):
        writes = tuple(writes) + tuple(r for r in reads if isinstance(r, str) and r.startswith("bank") and r not in writes)
        return self._add(eng, fn, tuple(reads), writes)

    def dma(self, q, key, out, in_, reads=(), writes=()):
        q = DMAQ.get(q, q)

        def fn(e):
            return e.dma_start(out=out, in_=in_)
        return self._add(q, fn, tuple(reads), tuple(writes), True, key)

    def barrier(self):
        deps = set(self.eng_last.values()) | set(self.key_last.values())
        for e in ENG:
            op = _Op(e, None, False, None)
            op.deps = set(deps)
            self.ops.append(op)
        self.last_w = {}
        self.readers = {}

    def emit(self):
        nc = self.nc
        ops = self.ops
        for o in ops:
            for d in o.deps:
                if d.is_dma:
                    continue
                if d.eng == "pe" and o.eng == "pe" and not o.is_dma and o.fn is not None:
                    continue
                d.signal = True
        cnt = {e: 0 for e in ENG}
        for o in ops:
            if o.is_dma or o.fn is None:
                continue
            if o.signal:
                cnt[o.eng] += 1
                o.count = cnt[o.eng]
        keys = list(self.key_n.keys())
        with contextlib.ExitStack() as es:
            esem = {e: es.enter_context(nc.semaphore("s_" + e)) for e in ENG}
            ksem = {k: es.enter_context(nc.semaphore("k%d" % i)) for i, k in enumerate(keys)}
            block = es.enter_context(nc.Block())
            streams = {e: [o for o in ops if o.eng == e] for e in ENG}

            def run(e, engobj):
                seen = {}
                for o in streams[e]:
                    need = {}
                    for d in o.deps:
                        if d.is_dma:
                            s = ksem[d.key]
                        else:
                            if d.eng == "pe" and e == "pe" and not o.is_dma and o.fn is not None:
                                continue
                            s = esem[d.eng]
                        v = d.count
                        if need.get(s, 0) < v:
                            need[s] = v
                    for s, v in need.items():
                        if seen.get(s, 0) >= v:
                            continue
                        seen[s] = v
                        engobj.wait_ge(s, v)
                    if o.fn is None:
                        continue
                    ins = o.fn(engobj)
                    if o.is_dma:
                        ins.then_inc(ksem[o.key], 16)
                    elif o.signal:
                        ins.then_inc(esem[e], 1)
                if e == "sp":
                    for k in keys:
                        v = 16 * self.key_n[k]
                        if seen.get(ksem[k], 0) < v:
                            engobj.wait_ge(ksem[k], v)

            @block.sync
            def _(eng):
                run("sp", eng)

            @block.scalar
            def _(eng):
                run("act", eng)

            @block.vector
            def _(eng):
                run("dve", eng)

            @block.gpsimd
            def _(eng):
                run("pool", eng)

            @block.tensor
            def _(eng):
                run("pe", eng)


SB_LO = 16512
SB_HI = 229376


class Builder:
    def __init__(self, taps=()):
        self.nc = bass.Bass("TRN2", target_bir_lowering=False)
        self.P = Prog(self.nc)
        self.off = SB_LO
        self.nalloc = 0
        self.taps = set(taps)
        self.tap_out = {}
        self.es = contextlib.ExitStack()
        self.rot = {}

    def sb(self, name, shape, dt):
        n = int(np.prod(shape[1:])) * (4 if dt == F32 else 2)
        n = (n + 63) // 64 * 64
        assert self.off + n <= SB_HI, ("SBUF overflow", name, self.off, n)
        self.nalloc += 1
        t = self.nc.alloc_sbuf_tensor_at("%s_%d" % (name, self.nalloc), list(shape), dt, offset=self.off)
        self.off += n
        return t

    def mark(self):
        return self.off

    def release(self, m):
        self.P.barrier()
        self.off = m

    def din(self, name, shape, dt=F32):
        return self.nc.dram_tensor(name, list(shape), dt, kind="ExternalInput").ap()

    def dscr(self, name, shape, dt=F32):
        if name in self.taps:
            t = self.nc.dram_tensor(name, list(shape), dt, kind="ExternalOutput").ap()
            self.tap_out[name] = t
            return t
        return self.nc.dram_tensor(name, list(shape), dt).ap()

    def tap(self, name, ap, shape, reads, dt=F32):
        if name not in self.taps:
            return
        t = self.nc.dram_tensor(name, list(shape), dt, kind="ExternalOutput").ap()
        self.tap_out[name] = t
        self.P.dma("sp", "tap_" + name, t, ap, reads=reads)

    def nxt(self, name, n):
        i = self.rot.get(name, 0)
        self.rot[name] = i + 1
        return i % n

    def build(self, stop_after=None):
        nc, P = self.nc, self.P
        es = self.es
        x_in = self.din("x", [T, D])
        ctx_in = self.din("ctx", [TC, D])
        ccT_in = self.din("ccT", [128, 8, 2])
        w_mod = self.din("w_mod", [D, 9 * D])
        b_modT = self.din("b_modT", [128, 72])
        gvec = self.din("gvec", [128, 4, 8])
        f1w1 = self.din("ffn1_w1", [D, FF]); f1w3 = self.din("ffn1_w3", [D, FF]); f1w2 = self.din("ffn1_w2", [FF, D])
        f2w1 = self.din("ffn2_w1", [D, FF]); f2w3 = self.din("ffn2_w3", [D, FF]); f2w2 = self.din("ffn2_w2", [FF, D])
        w_in = self.din("w_in", [D, PIN]); w_out = self.din("w_out", [D, D])
        consts = self.din("consts", [128, 7, 128]); rows = self.din("rows", [128, 400])
        self.convT_in = self.din("convT", [128, 12, 5])
        self.lvmask_in = self.din("lvmask", [128, 7, 2, 128])
        ropeC = self.din("ropeC", [128, T]); ropeS = self.din("ropeS", [128, T])
        out = self.nc.dram_tensor("out", [T, D], F32, kind="ExternalOutput").ap()
        xT = self.dscr("xT", [D, NT])
        self.xT = xT

        self.banks = [es.enter_context(nc.psum_tensor("bank%d" % i, [128, 512], F32))[:, :] for i in range(8)]
        banks = self.banks

        identf = self.sb("identf", [128, 128], F32)
        onesb = self.sb("onesb", [128, 128], BF16)
        epsc = self.sb("epsc", [128, 1], F32)
        self.identf, self.onesb, self.epsc = identf, onesb, epsc
        P.op("pool", lambda e: e.memset(identf[:], 0.0), writes=["identf"])
        P.op("pool", lambda e: e.affine_select(out=identf[:], in_=identf[:], pattern=[[-1, 128]],
                                               compare_op=ALU.not_equal, fill=1.0, base=0, channel_multiplier=1),
             reads=["identf"], writes=["identf"])
        P.op("pool", lambda e: e.memset(onesb[:], 1.0), writes=["onesb"])
        P.op("pool", lambda e: e.memset(epsc[:], EPS), writes=["epsc"])

        modT = self.sb("modT", [128, 72, 2], F32)
        gv = self.sb("gv", [128, 4, 8], F32)
        vec = self.sb("vec", [128, 3, 3, 2, 8], F32)
        self.vec = vec
        m0 = self.mark()
        ccT = self.sb("ccT", [128, 8, 2], F32)
        bmT = self.sb("bmT", [128, 72], F32)
        wm = [self.sb("wm%d" % i, [128, 4608], F32) for i in range(2)]
        P.dma("sp", "ccT", ccT[:], ccT_in[:, :, :], writes=["ccT"])
        P.dma("sp", "bmT", bmT[:], b_modT[:, :], writes=["bmT"])
        P.dma("sp", "gv", gv[:], gvec[:, :, :], writes=["gv"])
        P.op("act", lambda e: e.activation(out=ccT[:], in_=ccT[:], func=AF.Silu), reads=["ccT"], writes=["ccT"])
        mb = banks[0]
        for k in range(8):
            for h in range(2):
                s = self.nxt("wm", 2)
                P.dma("sp", "wm%d" % s, wm[s][:], w_mod[k * 128:(k + 1) * 128, h * 4608:(h + 1) * 4608],
                      writes=["wm%d" % s])
                for n in range(36):
                    col = (h * 36 + n) * 2
                    P.op("pe", lambda e, s=s, n=n, col=col, k=k, h=h: e.matmul(
                        out=mb[:, col:col + 2], lhsT=wm[s][:, n * 128:(n + 1) * 128], rhs=ccT[:, k, :],
                        start=(k == 0 and h == 0 and n == 0), stop=(k == 7), skip_group_check=True),
                        reads=["wm%d" % s, "ccT"], writes=["bank0"])
        P.op("dve", lambda e: e.tensor_tensor(out=modT[:], in0=mb[:, 0:144].rearrange("p (n r) -> p n r", r=2),
                                              in1=bmT[:].unsqueeze(2).to_broadcast([128, 72, 2]), op=ALU.add),
             reads=["bank0", "bmT"], writes=["modT"])
        for wi, (jshift, jscale, jgate, gsc) in enumerate([(0, 1, 2, 0.5), (3, 4, 5, 1.0), (6, 7, 8, 0.5)]):
            for r in range(2):
                P.op("dve", lambda e, wi=wi, r=r, jscale=jscale: e.scalar_tensor_tensor(
                    out=vec[:, wi, 0, r, :], in0=modT[:, jscale * 8:(jscale + 1) * 8, r], scalar=1.0,
                    in1=gv[:, wi, :], op0=ALU.add, op1=ALU.mult), reads=["modT", "gv"], writes=["vec"])
                P.op("dve", lambda e, wi=wi, r=r, jshift=jshift: e.tensor_copy(
                    out=vec[:, wi, 1, r, :], in_=modT[:, jshift * 8:(jshift + 1) * 8, r]),
                    reads=["modT"], writes=["vec"])
                P.op("dve", lambda e, wi=wi, r=r, jgate=jgate, gsc=gsc: e.tensor_scalar(
                    out=vec[:, wi, 2, r, :], in0=modT[:, jgate * 8:(jgate + 1) * 8, r], scalar1=gsc, scalar2=None,
                    op0=ALU.mult), reads=["modT"], writes=["vec"])
        if "vec" in self.taps:
            tv = self.nc.dram_tensor("vec", [128, 144], F32, kind="ExternalOutput").ap()
            self.tap_out["vec"] = tv
            P.dma("sp", "tapvec", tv[:, :], vec[:].rearrange("p a b c d -> p (a b c d)"), reads=["vec"])
        self.release(m0)

        m0 = self.mark()
        xin = [self.sb("xin%d" % i, [128, 4, D], F32) for i in range(2)]
        xst = [self.sb("xst%d" % i, [128, 8, 512], F32) for i in range(2)]
        for g in range(9):
            n = 512 if g < 8 else 256
            nt = n // 128
            s = self.nxt("xin", 2)
            src = x_in[g * 512:(g + 1) * 512, :] if g < 8 else ctx_in[:, :]
            P.dma("sp", "xin%d" % s, xin[s][:, 0:nt, :], src.rearrange("(a p) d -> p a d", p=128), writes=["xin%d" % s])
            so = self.nxt("xst", 2)
            for c in range(8):
                bi = self.nxt("t0bank", 4)
                bk = banks[bi]
                for a in range(nt):
                    P.op("pe", lambda e, s=s, a=a, c=c, bk=bk: e.transpose(
                        out=bk[:, a * 128:(a + 1) * 128], in_=xin[s][:, a, c * 128:(c + 1) * 128], identity=identf[:]),
                        reads=["xin%d" % s, "identf"], writes=["bank%d" % bi])
                eng = "act" if c % 2 == 0 else "dve"
                if eng == "act":
                    P.op("act", lambda e, so=so, c=c, bk=bk, n=n: e.copy(out=xst[so][:, c, 0:n], in_=bk[:, 0:n]),
                         reads=["bank%d" % bi], writes=["xst%d" % so])
                else:
                    P.op("dve", lambda e, so=so, c=c, bk=bk, n=n: e.tensor_copy(out=xst[so][:, c, 0:n], in_=bk[:, 0:n]),
                         reads=["bank%d" % bi], writes=["xst%d" % so])
            P.dma("pool", "xst%d" % so, xT[:, g * 512:g * 512 + n].rearrange("(c p) t -> p c t", p=128), xst[so][:, :, 0:n],
                  reads=["xst%d" % so], writes=[("xT", g)])
        self.release(m0)
        if stop_after == "t0":
            return self.finish()

        groups = [(g * 512, 512, 0, g) for g in range(8)] + [(T, TC, 1, 8)]
        self.ffn(f1w1, f1w3, f1w2, 0, groups)
        if stop_after == "ffn1":
            return self.finish()
        self.mixer(w_in, w_out, consts, rows, ropeC, ropeS, stop_after=stop_after)
        if stop_after in ("atproj", "dnproj", "attn", "dn", "mixer"):
            return self.finish()
        lat_groups = groups[:8]
        self.ffn(f2w1, f2w3, f2w2, 2, lat_groups)
        self.final_norm(out, gv)
        return self.finish()

    def build_attn_only(self):
        nc, P, es = self.nc, self.P, self.es
        self.banks = [es.enter_context(nc.psum_tensor("bank%d" % i, [128, 512], F32))[:, :] for i in range(8)]
        at_qT = self.din("at_qT", [4, 128, T], BF16)
        at_kT = self.din("at_kT", [2, 128, NT], BF16)
        at_v = self.din("at_v", [NT, 256], BF16)
        mixT = self.nc.dram_tensor("mixT", [D, T], BF16, kind="ExternalOutput").ap()
        self.onesb = self.sb("onesb", [128, 128], BF16)
        negm = self.sb("negm", [128, 1], F32)
        P.op("pool", lambda e: e.memset(self.onesb[:], 1.0), writes=["onesb"])
        P.op("pool", lambda e: e.memset(negm[:], -12.623587), writes=["negm"])
        import os
        if not os.environ.get("SKIP_ATT"):
            self.attention(at_qT, at_kT, at_v, mixT, negm)
        self.dn_main(mixT)
        if stop_after == "dn":
            return
        return self.finish()

    def finish(self):
        self.P.emit()
        self.es.close()
        return self.nc

    def rms_stats(self, xg, n, sq, bank_key, bk, rs):
        P = self.P
        onesb, epsc = self.onesb, self.epsc
        kx, ksq, krs = xg[1], sq[1], rs[1]
        xg, sq, rs = xg[0], sq[0], rs[0]
        P.op("act", lambda e: e.activation(out=sq[:, :, 0:n], in_=xg[:, :, 0:n], func=AF.Square), reads=[kx], writes=[ksq])
        for c in range(8):
            P.op("pe", lambda e, c=c: e.matmul(out=bk[:, 0:n], lhsT=onesb[:], rhs=sq[:, c, 0:n], start=(c == 0), stop=(c == 7)),
                 reads=[ksq, "onesb"], writes=[bank_key])
        P.op("act", lambda e: e.activation(out=rs[:, 0:n], in_=bk[:, 0:n], func=AF.Sqrt, bias=epsc[:, 0:1], scale=1.0 / D),
             reads=[bank_key, "epsc"], writes=[krs])
        P.op("dve", lambda e: e.reciprocal(out=rs[:, 0:n], in_=rs[:, 0:n]), reads=[krs], writes=[krs])


    def norm_mod(self, wi, groups, loc, hT, xg, sq, rs):
        P, banks, xT, vec = self.P, self.banks, self.xT, self.vec
        for gi, (c0, n, kind, gid) in enumerate(groups):
            s = self.nxt("xg", 2)
            P.dma("sp", "xg%d" % s, xg[s][:, :, 0:n], xT[:, c0:c0 + n].rearrange("(c p) t -> p c t", p=128),
                  reads=[("xT", gid)], writes=["xg%d" % s])
            bi = 4 + self.nxt("stbank", 2)
            r = self.nxt("rs", 2)
            self.rms_stats((xg[s], "xg%d" % s), n, (sq, "sq"), "bank%d" % bi, banks[bi], (rs[r], "rs%d" % r))
            P.op("dve", lambda e, s=s, r=r, n=n: e.tensor_tensor(
                out=xg[s][:, :, 0:n], in0=xg[s][:, :, 0:n], in1=rs[r][:, 0:n].unsqueeze(1).to_broadcast([128, 8, n]),
                op=ALU.mult), reads=["xg%d" % s, "rs%d" % r], writes=["xg%d" % s])
            for c in range(8):
                P.op("act", lambda e, s=s, c=c, n=n, kind=kind, lo=loc[gi]: e.activation(
                    out=hT[:, c, lo:lo + n], in_=xg[s][:, c, 0:n], func=AF.Identity,
                    scale=vec[:, wi, 0, kind, c:c + 1], bias=vec[:, wi, 1, kind, c:c + 1]),
                    reads=["xg%d" % s, "vec"], writes=[("hT", gi)])

    def ffn(self, w1, w3, w2, wi, groups):
        nc, P, banks, xT, vec = self.nc, self.P, self.banks, self.xT, self.vec
        m0 = self.mark()
        ntok = sum(g[1] for g in groups)
        hT = self.sb("hT", [128, 8, ntok], BF16)
        parts = [(0, 6), (6, 12), (12, 17), (17, 22)]
        maxp = 6
        gT = self.sb("gT", [128, maxp, ntok], BF16)
        w2b = self.sb("w2b", [128, maxp, D], BF16)
        wst = [self.sb("wst%d" % i, [128, 8, 128], F32) for i in range(4)]
        w13b = [self.sb("w13b%d" % i, [128, 8, 128], BF16) for i in range(4)]
        xg = [self.sb("xg%d" % i, [128, 8, 512], F32) for i in range(2)]
        sq = self.sb("sq", [128, 8, 512], BF16)
        rs = [self.sb("rs%d" % i, [128, 512], F32) for i in range(2)]
        sa = [self.sb("sa%d" % i, [128, 512], F32) for i in range(2)]
        loc = []
        o = 0
        for (c0, n, kind, gid) in groups:
            loc.append(o)
            o += n
        self.norm_mod(wi, groups, loc, hT, xg, sq, rs)
        for (fa, fb) in parts:
            for f in range(fa, fb):
                wb = []
                for wsrc in (w1, w3):
                    s = self.nxt("wst", 4)
                    P.dma("sp", "wst%d" % s, wst[s][:], wsrc[:, f * 128:(f + 1) * 128].rearrange("(c p) n -> p c n", p=128),
                          writes=["wst%d" % s])
                    b = self.nxt("w13b", 4)
                    P.op("pool", lambda e, s=s, b=b: e.tensor_copy(out=w13b[b][:], in_=wst[s][:]),
                         reads=["wst%d" % s], writes=["w13b%d" % b])
                    wb.append(b)
                for gi, (c0, n, kind, gid) in enumerate(groups):
                    lo = loc[gi]
                    bp = self.nxt("upbank", 2)
                    ba, bb = banks[2 * bp], banks[2 * bp + 1]
                    for (bk, bkey, b) in ((ba, "bank%d" % (2 * bp), wb[0]), (bb, "bank%d" % (2 * bp + 1), wb[1])):
                        for c in range(8):
                            P.op("pe", lambda e, bk=bk, b=b, c=c, lo=lo, n=n: e.matmul(
                                out=bk[:, 0:n], lhsT=w13b[b][:, c, :], rhs=hT[:, c, lo:lo + n], start=(c == 0), stop=(c == 7)),
                                reads=["w13b%d" % b, ("hT", gi)], writes=[bkey])
                    si = self.nxt("sa", 2)
                    P.op("act", lambda e, si=si, ba=ba, n=n: e.activation(out=sa[si][:, 0:n], in_=ba[:, 0:n], func=AF.Silu),
                         reads=["bank%d" % (2 * bp)], writes=["sa%d" % si])
                    P.op("dve", lambda e, si=si, bb=bb, n=n, lo=lo, f=f, fa=fa: e.tensor_tensor(
                        out=gT[:, f - fa, lo:lo + n], in0=sa[si][:, 0:n], in1=bb[:, 0:n], op=ALU.mult),
                        reads=["sa%d" % si, "bank%d" % (2 * bp + 1)], writes=[("gT", f - fa, gi)])
            for f in range(fa, fb):
                for hh in range(2):
                    s = self.nxt("wst", 4)
                    P.dma("sp", "wst%d" % s, wst[s][:].rearrange("p c n -> p (c n)")[:, 0:512],
                          w2[f * 128:(f + 1) * 128, hh * 512:(hh + 1) * 512], writes=["wst%d" % s])
                    P.op("pool", lambda e, s=s, f=f, fa=fa, hh=hh: e.tensor_copy(
                        out=w2b[:, f - fa, hh * 512:(hh + 1) * 512], in_=wst[s][:].rearrange("p c n -> p (c n)")[:, 0:512]),
                        reads=["wst%d" % s], writes=[("w2b", f - fa)])
            for gi, (c0, n, kind, gid) in enumerate(groups):
                lo = loc[gi]
                s = self.nxt("xg", 2)
                P.dma("sp", "xg%d" % s, xg[s][:, :, 0:n], xT[:, c0:c0 + n].rearrange("(c p) t -> p c t", p=128),
                      reads=[("xT", gid)], writes=["xg%d" % s])
                for dc in range(8):
                    bi = 4 + self.nxt("dnbank", 4)
                    bk = banks[bi]
                    for f in range(fa, fb):
                        P.op("pe", lambda e, bk=bk, f=f, fa=fa, dc=dc, lo=lo, n=n: e.matmul(
                            out=bk[:, 0:n], lhsT=w2b[:, f - fa, dc * 128:(dc + 1) * 128], rhs=gT[:, f - fa, lo:lo + n],
                            start=(f == fa), stop=(f == fb - 1)),
                            reads=[("w2b", f - fa), ("gT", f - fa, gi)], writes=["bank%d" % bi])
                    P.op("dve", lambda e, bk=bk, s=s, dc=dc, n=n, kind=kind: e.scalar_tensor_tensor(
                        out=xg[s][:, dc, 0:n], in0=bk[:, 0:n], scalar=vec[:, wi, 2, kind, dc:dc + 1], in1=xg[s][:, dc, 0:n],
                        op0=ALU.mult, op1=ALU.add), reads=["bank%d" % bi, "xg%d" % s, "vec"], writes=["xg%d" % s])
                P.dma("pool", "xgo%d" % s, xT[:, c0:c0 + n].rearrange("(c p) t -> p c t", p=128), xg[s][:, :, 0:n],
                      reads=["xg%d" % s], writes=[("xT", gid)])
        self.release(m0)


    def load_w_chunk(self, src_ap, wst, w13b, nst=4, nb=None, bkey="w13b", bidx=None):
        P = self.P
        s = self.nxt("wst", nst)
        P.dma("sp", "wst%d" % s, wst[s][:], src_ap.rearrange("(c p) n -> p c n", p=128), writes=["wst%d" % s])
        if bidx is None:
            bidx = self.nxt(bkey, nb)
        P.op("pool", lambda e, s=s, b=bidx: e.tensor_copy(out=w13b[b][:], in_=wst[s][:]),
             reads=["wst%d" % s], writes=["%s%d" % (bkey, bidx)])
        return bidx

    def mixer(self, w_in, w_out, consts, rows, ropeC, ropeS, stop_after=None):
        nc, P, banks, xT, vec = self.nc, self.P, self.banks, self.xT, self.vec
        groups = [(g * 512, 512, 0, g) for g in range(8)] + [(T, TC, 1, 8)]
        loc = [g[0] for g in groups]
        at_qT = self.dscr("at_qT", [4, 128, T], BF16)
        at_kT = self.dscr("at_kT", [2, 128, NT], BF16)
        at_v = self.dscr("at_v", [NT, 256], BF16)
        mixT = self.dscr("mixT", [D, T], BF16)
        self.mixT = mixT
        m_top = self.mark()
        cst = self.sb("cst", [128, 7, 128], F32)
        rws = self.sb("rws", [128, 400], F32)
        negm = self.sb("negm", [128, 1], F32)
        identb = self.sb("identb", [128, 128], BF16)
        self.cst, self.rws, self.identb = cst, rws, identb
        self.dnsc = {k_: self.sb("dn_" + k_, [128, 34, 8], F32) for k_ in ("beta", "gc", "ngc", "gcl", "et", "bet", "egl")}
        onesf = self.sb("onesf", [128, 128], F32)
        self.onesf = onesf
        P.op("pool", lambda e: e.memset(onesf[:], 1.0), writes=["onesf"])
        P.dma("sp", "cst", cst[:], consts[:, :, :], writes=["cst"])
        P.dma("sp", "rws", rws[:], rows[:, :], writes=["rws"])
        P.op("dve", lambda e: e.tensor_copy(out=identb[:], in_=self.identf[:]), reads=["identf"], writes=["identb"])
        m_all = self.mark()
        hT = self.sb("mhT", [128, 8, NT], BF16)
        m1 = self.mark()
        xg = [self.sb("mxg%d" % i, [128, 8, 512], F32) for i in range(2)]
        sq = self.sb("msq", [128, 8, 512], BF16)
        rs = [self.sb("mrs%d" % i, [128, 512], F32) for i in range(2)]
        self.norm_mod(1, groups, loc, hT, xg, sq, rs)
        self.release(m1)
        m1 = self.mark()
        wst = [self.sb("awst%d" % i, [128, 8, 128], F32) for i in range(2)]
        wq = [self.sb("awq%d" % i, [128, 8, 128], BF16) for i in range(6)]
        wvs = self.sb("awvs", [128, 8, 256], F32)
        wv = self.sb("awv", [128, 8, 256], BF16)
        gcol = self.sb("agcol", [128, 2], F32)
        tabC = [self.sb("atabC%d" % i, [128, 512], F32) for i in range(2)]
        tabS = [self.sb("atabS%d" % i, [128, 512], F32) for i in range(2)]
        sqb = [self.sb("asqb%d" % i, [128, 512], BF16) for i in range(2)]
        rsb = [self.sb("arsb%d" % i, [128, 512], F32) for i in range(2)]
        qn = [self.sb("aqn%d" % i, [128, 512], F32) for i in range(2)]
        t1 = [self.sb("at1%d" % i, [128, 512], F32) for i in range(2)]
        t2 = [self.sb("at2%d" % i, [128, 512], F32) for i in range(2)]
        qst = [self.sb("aqst%d" % i, [128, 512], BF16) for i in range(2)]
        vst = [self.sb("avst%d" % i, [128, 4, 256], BF16) for i in range(2)]
        for j in range(2):
            P.op("dve", lambda e, j=j: e.tensor_tensor(out=t1[0][:, 0:128], in0=rws[:, j * 128:(j + 1) * 128], in1=self.identf[:],
                                                       op=ALU.mult), reads=["rws", "identf", "at10"], writes=["at10"])
            P.op("dve", lambda e, j=j: e.tensor_reduce(out=gcol[:, j:j + 1], in_=t1[0][:, 0:128], axis=mybir.AxisListType.X,
                                                       op=ALU.add), reads=["at10"], writes=["agcol"])
        P.op("dve", lambda e: e.scalar_tensor_tensor(out=t1[1][:, 0:256], in0=rws[:, 0:256], scalar=-1.0, in1=rws[:, 0:256],
                                                     op0=ALU.mult, op1=ALU.max), reads=["rws", "at11"], writes=["at11"])
        P.op("dve", lambda e: e.tensor_reduce(out=t2[0][:, 0:2], in_=t1[1][:, 0:256].rearrange("p (a b) -> p a b", a=2),
                                              axis=mybir.AxisListType.X, op=ALU.max), reads=["at11", "at20"], writes=["at20"])
        P.op("dve", lambda e: e.scalar_tensor_tensor(out=negm[:], in0=t2[0][:, 0:1], scalar=-float(np.sqrt(128.0)),
                                                     in1=t2[0][:, 1:2], op0=ALU.mult, op1=ALU.mult),
             reads=["at20"], writes=["negm"])
        self.tap("negm", negm[:], [128, 1], ["negm"])
        self.tap("gcol", gcol[:], [128, 2], ["agcol"])
        colq = [2064 + h * 128 for h in range(4)] + [2576 + g * 128 for g in range(2)]
        for ch in range(6):
            self.load_w_chunk(w_in[:, colq[ch]:colq[ch] + 128], wst, wq, nst=2, bkey="awq", bidx=ch)
        P.dma("sp", "awvs", wvs[:], w_in[:, 2832:3088].rearrange("(c p) n -> p c n", p=128), writes=["awvs"])
        P.op("pool", lambda e: e.tensor_copy(out=wv[:], in_=wvs[:]), reads=["awvs"], writes=["awv"])
        for gi, (c0, n, kind, gid) in enumerate(groups):
            lat = kind == 0
            if lat:
                tb = self.nxt("atab", 2)
                P.dma("sp", "atabC%d" % tb, tabC[tb][:], ropeC[:, c0:c0 + n], writes=["atabC%d" % tb])
                P.dma("sp", "atabS%d" % tb, tabS[tb][:], ropeS[:, c0:c0 + n], writes=["atabS%d" % tb])
            for ch in range(6):
                if ch < 4 and not lat:
                    continue
                gj = 0 if ch < 4 else 1
                bi = self.nxt("apbank", 2)
                bk = banks[bi]
                for c in range(8):
                    P.op("pe", lambda e, bk=bk, ch=ch, c=c, c0=c0, n=n: e.matmul(
                        out=bk[:, 0:n], lhsT=wq[ch][:, c, :], rhs=hT[:, c, c0:c0 + n], start=(c == 0), stop=(c == 7)),
                        reads=["awq%d" % ch, ("hT", gi)], writes=["bank%d" % bi])
                i2 = self.nxt("asq", 2)
                P.op("act", lambda e, i2=i2, bk=bk, n=n: e.activation(out=sqb[i2][:, 0:n], in_=bk[:, 0:n], func=AF.Square),
                     reads=["bank%d" % bi], writes=["asqb%d" % i2])
                bs = 2 + self.nxt("asbank", 2)
                P.op("pe", lambda e, bs=bs, i2=i2, n=n: e.matmul(out=banks[bs][:, 0:n], lhsT=self.onesb[:], rhs=sqb[i2][:, 0:n],
                                                                 start=True, stop=True),
                     reads=["asqb%d" % i2, "onesb"], writes=["bank%d" % bs])
                P.op("act", lambda e, bs=bs, i2=i2, n=n: e.activation(out=rsb[i2][:, 0:n], in_=banks[bs][:, 0:n], func=AF.Sqrt,
                                                                      bias=self.epsc[:, 0:1], scale=1.0 / 128),
                     reads=["bank%d" % bs, "epsc"], writes=["arsb%d" % i2])
                P.op("dve", lambda e, i2=i2, n=n: e.reciprocal(out=rsb[i2][:, 0:n], in_=rsb[i2][:, 0:n]),
                     reads=["arsb%d" % i2], writes=["arsb%d" % i2])
                P.op("dve", lambda e, i2=i2, bk=bk, gj=gj, n=n: e.scalar_tensor_tensor(
                    out=qn[i2][:, 0:n], in0=bk[:, 0:n], scalar=gcol[:, gj:gj + 1], in1=rsb[i2][:, 0:n],
                    op0=ALU.mult, op1=ALU.mult), reads=["bank%d" % bi, "agcol", "arsb%d" % i2], writes=["aqn%d" % i2])
                so = self.nxt("aqst", 2)
                if lat:
                    br = 4 + self.nxt("arbank", 2)
                    P.op("pe", lambda e, br=br, i2=i2, n=n: e.matmul(out=banks[br][:, 0:n], lhsT=cst[:, 0, :], rhs=qn[i2][:, 0:n],
                                                                     start=True, stop=True),
                         reads=["cst", "aqn%d" % i2], writes=["bank%d" % br])
                    P.op("pool", lambda e, i2=i2, tb=tb, n=n: e.tensor_tensor(out=t1[i2][:, 0:n], in0=qn[i2][:, 0:n],
                                                                             in1=tabC[tb][:, 0:n], op=ALU.mult),
                         reads=["aqn%d" % i2, "atabC%d" % tb], writes=["at1%d" % i2])
                    P.op("dve", lambda e, i2=i2, tb=tb, br=br, n=n: e.tensor_tensor(out=t2[i2][:, 0:n], in0=banks[br][:, 0:n],
                                                                                    in1=tabS[tb][:, 0:n], op=ALU.mult),
                         reads=["bank%d" % br, "atabS%d" % tb], writes=["at2%d" % i2])
                    P.op("pool", lambda e, i2=i2, so=so, n=n: e.tensor_tensor(out=qst[so][:, 0:n], in0=t1[i2][:, 0:n],
                                                                             in1=t2[i2][:, 0:n], op=ALU.add),
                         reads=["at1%d" % i2, "at2%d" % i2], writes=["aqst%d" % so])
                else:
                    P.op("pool", lambda e, i2=i2, so=so, n=n: e.tensor_copy(out=qst[so][:, 0:n], in_=qn[i2][:, 0:n]),
                         reads=["aqn%d" % i2], writes=["aqst%d" % so])
                dst = at_qT[ch, :, c0:c0 + n] if ch < 4 else at_kT[ch - 4, :, c0:c0 + n]
                P.dma("pool", "aqst%d" % so, dst, qst[so][:, 0:n], reads=["aqst%d" % so],
                      writes=[("at_q", ch, gi)])
            nt = n // 128
            sv = self.nxt("avst", 2)
            for a in range(nt):
                bv = 6 + (a // 2) % 2
                for c in range(8):
                    P.op("pe", lambda e, bv=bv, a=a, c=c, c0=c0: e.matmul(
                        out=banks[bv][:, (a % 2) * 256:(a % 2) * 256 + 256], lhsT=hT[:, c, c0 + a * 128:c0 + (a + 1) * 128],
                        rhs=wv[:, c, :], start=(c == 0), stop=(c == 7)),
                        reads=["awv", ("hT", gi)], writes=["bank%d" % bv])
                if a % 2 == 1 or a == nt - 1:
                    a0 = a - (a % 2)
                    na = a - a0 + 1
                    P.op("act", lambda e, bv=bv, sv=sv, a0=a0, na=na: e.copy(
                        out=vst[sv][:, a0:a0 + na, :], in_=banks[bv][:, 0:na * 256].rearrange("p (a e) -> p a e", e=256)),
                        reads=["bank%d" % bv], writes=["avst%d" % sv])
            P.dma("pool", "avst%d" % sv, at_v[c0:c0 + n, :].rearrange("(a p) e -> p a e", p=128), vst[sv][:, 0:nt, :],
                  reads=["avst%d" % sv], writes=[("at_v", gi)])
        self.release(m1)
        if stop_after == "atproj":
            self.release(m_all)
            return
        self.dn_proj(w_in, hT, groups)
        self.release(m_all)
        if stop_after == "dnproj":
            return

        import os
        if not os.environ.get("SKIP_ATT"):
            self.attention(at_qT, at_kT, at_v, mixT, negm)
        self.dn_main(mixT)
        if stop_after == "dn":
            return
        if stop_after == "attn":
            return

        m1 = self.mark()
        wos = [self.sb("owos%d" % i, [128, D], F32) for i in range(2)]
        wo = self.sb("owo", [128, 8, D], BF16)
        mt = [self.sb("omt%d" % i, [128, 8, 512], BF16) for i in range(2)]
        xg = [self.sb("oxg%d" % i, [128, 8, 512], F32) for i in range(2)]
        for mc in range(8):
            s_ = self.nxt("owos", 2)
            P.dma("sp", "owos%d" % s_, wos[s_][:], w_out[mc * 128:(mc + 1) * 128, :], writes=["owos%d" % s_])
            P.op("pool", lambda e, s_=s_, mc=mc: e.tensor_copy(out=wo[:, mc, :], in_=wos[s_][:]),
                 reads=["owos%d" % s_], writes=["owo"])
        for g in range(8):
            im = self.nxt("omt", 2)
            P.dma("sp", "omt%d" % im, mt[im][:], mixT[:, g * 512:(g + 1) * 512].rearrange("(c p) t -> p c t", p=128),
                  reads=[("mixT", r_, g) for r_ in range(8)], writes=["omt%d" % im])
            ix = self.nxt("oxg", 2)
            P.dma("sp", "oxg%d" % ix, xg[ix][:], xT[:, g * 512:(g + 1) * 512].rearrange("(c p) t -> p c t", p=128),
                  reads=[("xT", g)], writes=["oxg%d" % ix])
            for dc in range(8):
                bi = self.nxt("obank", 4)
                for mc in range(8):
                    P.op("pe", lambda e, bi=bi, mc=mc, dc=dc, im=im: e.matmul(
                        out=banks[bi][:, :], lhsT=wo[:, mc, dc * 128:(dc + 1) * 128], rhs=mt[im][:, mc, :],
                        start=(mc == 0), stop=(mc == 7)), reads=["owo", "omt%d" % im], writes=["bank%d" % bi])
                P.op("dve", lambda e, bi=bi, ix=ix, dc=dc: e.scalar_tensor_tensor(
                    out=xg[ix][:, dc, :], in0=banks[bi][:, :], scalar=vec[:, 1, 2, 0, dc:dc + 1], in1=xg[ix][:, dc, :],
                    op0=ALU.mult, op1=ALU.add), reads=["bank%d" % bi, "oxg%d" % ix, "vec"], writes=["oxg%d" % ix])
            P.dma("pool", "oxgo%d" % ix, xT[:, g * 512:(g + 1) * 512].rearrange("(c p) t -> p c t", p=128), xg[ix][:],
                  reads=["oxg%d" % ix], writes=[("xT", g)])
        self.release(m1)
        self.release(m_top)


    def dn_proj(self, w_in, hT, groups):
        nc, P, banks = self.nc, self.P, self.banks
        cst, rws, identb, identf = self.cst, self.rws, self.identb, self.identf
        dn_T = self.dscr("dn_T", [12, 128, NT], BF16)
        dn_zs = self.dscr("dn_zs", [T, 512], F32)
        self.dn_T, self.dn_zs = dn_T, dn_zs
        m1 = self.mark()
        wst = [self.sb("dwst%d" % i, [128, 8, 128], F32) for i in range(2)]
        wq = [self.sb("dwq%d" % i, [128, 8, 128], BF16) for i in range(2)]
        cw = self.sb("dcw", [128, 12, 5], F32)
        dg = [self.sb("ddg%d" % i, [128, 5, 128], BF16) for i in range(2)]
        pre = [self.sb("dpre%d" % i, [128, NT + 6], BF16) for i in range(2)]
        yb = [self.sb("dyb%d" % i, [128, 512], F32) for i in range(2)]
        sqb = [self.sb("dsqb%d" % i, [128, 512], BF16) for i in range(2)]
        rsb = [self.sb("drsb%d" % i, [128, 512], F32) for i in range(2)]
        yst = [self.sb("dyst%d" % i, [128, 512], BF16) for i in range(2)]
        P.dma("sp", "dcw", cw[:], self.convT_in[:, :, :], writes=["dcw"])
        for i in range(2):
            P.op("pool", lambda e, i=i: e.memset(pre[i][:], 0.0), writes=["dpre%d" % i])
        pos = lambda c0: c0 + 2 if c0 < T else c0 + 4
        import os
        dnp = os.environ.get("DNP", "conv,z,sc")
        for j in range(12 if "conv" in dnp else 0):
            wb = self.load_w_chunk(w_in[:, j * 128:(j + 1) * 128], wst, wq, nst=2, nb=2, bkey="dwq")
            di = self.nxt("ddg", 2)
            for k in range(5):
                P.op("dve", lambda e, di=di, k=k, j=j: e.tensor_scalar(out=dg[di][:, k, :], in0=identf[:], scalar1=cw[:, j, k:k + 1],
                                                                     scalar2=None, op0=ALU.mult),
                     reads=["identf", "dcw"], writes=["ddg%d" % di])
            pi = self.nxt("dpre", 2)
            for gi, (c0, n, kind, gid) in enumerate(groups):
                bi = self.nxt("dpbank", 2)
                for c in range(8):
                    P.op("pe", lambda e, bi=bi, wb=wb, c=c, c0=c0, n=n: e.matmul(
                        out=banks[bi][:, 0:n], lhsT=wq[wb][:, c, :], rhs=hT[:, c, c0:c0 + n], start=(c == 0), stop=(c == 7)),
                        reads=["dwq%d" % wb, ("hT", gi)], writes=["bank%d" % bi])
                P.op("act", lambda e, bi=bi, pi=pi, c0=c0, n=n: e.copy(out=pre[pi][:, pos(c0):pos(c0) + n], in_=banks[bi][:, 0:n]),
                     reads=["bank%d" % bi], writes=["dpre%d" % pi])
            for gi, (c0, n, kind, gid) in enumerate(groups):
                bi = 2 + self.nxt("dcbank", 2)
                for k in range(5):
                    P.op("pe", lambda e, bi=bi, di=di, k=k, pi=pi, c0=c0, n=n: e.matmul(
                        out=banks[bi][:, 0:n], lhsT=dg[di][:, k, :], rhs=pre[pi][:, pos(c0) + k - 2:pos(c0) + k - 2 + n],
                        start=(k == 0), stop=(k == 4)), reads=["ddg%d" % di, "dpre%d" % pi], writes=["bank%d" % bi])
                yi = self.nxt("dyb", 2)
                P.op("act", lambda e, bi=bi, yi=yi, n=n: e.activation(out=yb[yi][:, 0:n], in_=banks[bi][:, 0:n], func=AF.Silu),
                     reads=["bank%d" % bi], writes=["dyb%d" % yi])
                so = self.nxt("dyst", 2)
                if j < 8:
                    P.op("pool", lambda e, yi=yi, n=n: e.tensor_tensor(out=sqb[yi][:, 0:n], in0=yb[yi][:, 0:n], in1=yb[yi][:, 0:n],
                                                                      op=ALU.mult), reads=["dyb%d" % yi], writes=["dsqb%d" % yi])
                    bs = 4 + self.nxt("dsbank", 2)
                    P.op("pe", lambda e, bs=bs, yi=yi, n=n: e.matmul(out=banks[bs][:, 0:n], lhsT=self.onesb[:], rhs=sqb[yi][:, 0:n],
                                                                     start=True, stop=True),
                         reads=["dsqb%d" % yi, "onesb"], writes=["bank%d" % bs])
                    P.op("act", lambda e, bs=bs, yi=yi, n=n: e.activation(out=rsb[yi][:, 0:n], in_=banks[bs][:, 0:n], func=AF.Sqrt,
                                                                          bias=self.epsc[:, 0:1], scale=1.0),
                         reads=["bank%d" % bs, "epsc"], writes=["drsb%d" % yi])
                    P.op("dve", lambda e, yi=yi, n=n: e.reciprocal(out=rsb[yi][:, 0:n], in_=rsb[yi][:, 0:n]),
                         reads=["drsb%d" % yi], writes=["drsb%d" % yi])
                    qsc = float(128 ** -0.5) if j < 4 else 1.0
                    P.op("dve", lambda e, yi=yi, so=so, n=n, qsc=qsc: e.scalar_tensor_tensor(
                        out=yst[so][:, 0:n], in0=yb[yi][:, 0:n], scalar=qsc, in1=rsb[yi][:, 0:n], op0=ALU.mult, op1=ALU.mult),
                        reads=["dyb%d" % yi, "drsb%d" % yi], writes=["dyst%d" % so])
                else:
                    P.op("pool", lambda e, yi=yi, so=so, n=n: e.tensor_copy(out=yst[so][:, 0:n], in_=yb[yi][:, 0:n]),
                         reads=["dyb%d" % yi], writes=["dyst%d" % so])
                P.dma("pool", "dyst%d" % so, dn_T[j, :, c0:c0 + n], yst[so][:, 0:n], reads=["dyst%d" % so], writes=[("dn_T", j, gi)])
        self.release(m1)
        m1 = self.mark()
        wzs = self.sb("dwzs", [128, 8, 512], F32)
        wz = self.sb("dwz", [128, 8, 512], BF16)
        wbas = self.sb("dwbas", [128, 8, 16], F32)
        wba = self.sb("dwba", [128, 8, 16], BF16)
        zst = [self.sb("dzst%d" % i, [128, 512], F32) for i in range(2)]
        ba = self.sb("dba", [128, 34, 16], F32)
        tmp = {k_: self.sb("dt_" + k_, [128, 34, 8], F32) for k_ in ("x", "ax", "e", "l", "g", "lnb", "gl")}
        for c in range(8):
            P.dma("sp", "dwzs", wzs[:, c, :], w_in[c * 128:(c + 1) * 128, 1536:2048], writes=["dwzs"])
        P.op("pool", lambda e: e.tensor_copy(out=wz[:], in_=wzs[:]), reads=["dwzs"], writes=["dwz"])
        P.dma("sp", "dwbas", wbas[:], w_in[:, 2048:2064].rearrange("(c p) n -> p c n", p=128), writes=["dwbas"])
        P.op("pool", lambda e: e.tensor_copy(out=wba[:], in_=wbas[:]), reads=["dwbas"], writes=["dwba"])
        for tt in range(34 if "z" in dnp else 0):
            c0 = tt * 128
            gi = min(tt // 4, 8)
            if tt < 32:
                bi = self.nxt("dzbank", 2)
                for c in range(8):
                    P.op("pe", lambda e, bi=bi, c=c, c0=c0: e.matmul(out=banks[bi][:, :], lhsT=hT[:, c, c0:c0 + 128], rhs=wz[:, c, :],
                                                                     start=(c == 0), stop=(c == 7)),
                         reads=["dwz", ("hT", gi)], writes=["bank%d" % bi])
                zi = self.nxt("dzst", 2)
                P.op("act", lambda e, bi=bi, zi=zi: e.activation(out=zst[zi][:], in_=banks[bi][:, :], func=AF.Silu),
                     reads=["bank%d" % bi], writes=["dzst%d" % zi])
                P.dma("pool", "dzst%d" % zi, dn_zs[c0:c0 + 128, :], zst[zi][:], reads=["dzst%d" % zi], writes=[("dn_zs", tt)])
            bb = 2 if tt < 32 else 3
            col = (tt % 32) * 16
            for c in range(8):
                P.op("pe", lambda e, bb=bb, col=col, c=c, c0=c0: e.matmul(out=banks[bb][:, col:col + 16], lhsT=hT[:, c, c0:c0 + 128],
                                                                        rhs=wba[:, c, :], start=(c == 0), stop=(c == 7)),
                     reads=["dwba", ("hT", gi)], writes=["bank%d" % bb])
        P.op("dve", lambda e: e.tensor_copy(out=ba[:, 0:32, :], in_=banks[2][:, :].rearrange("p (t n) -> p t n", n=16)),
             reads=["bank2"], writes=["dba"])
        P.op("dve", lambda e: e.tensor_copy(out=ba[:, 32:34, :], in_=banks[3][:, 0:32].rearrange("p (t n) -> p t n", n=16)),
             reads=["bank3"], writes=["dba"])
        if "sc" not in dnp:
            self.release(m1)
            return
        self.tap("ba", ba[:].rearrange("p t n -> p (t n)"), [128, 544], ["dba"])
        sc = self.dnsc
        rowb = lambda lo: rws[:, lo:lo + 8].unsqueeze(1).to_broadcast([128, 34, 8])
        P.op("act", lambda e: e.activation(out=sc["beta"][:], in_=ba[:, :, 0:8], func=AF.Sigmoid), reads=["dba"], writes=["s_beta"])
        P.op("act", lambda e: e.activation(out=tmp["lnb"][:], in_=sc["beta"][:], func=AF.Ln), reads=["s_beta"], writes=["t_lnb"])
        P.op("dve", lambda e: e.tensor_tensor(out=tmp["x"][:], in0=ba[:, :, 8:16], in1=rowb(264), op=ALU.add),
             reads=["dba", "rws"], writes=["t_x"])
        P.op("dve", lambda e: e.scalar_tensor_tensor(out=tmp["ax"][:], in0=tmp["x"][:], scalar=-1.0, in1=tmp["x"][:],
                                                     op0=ALU.mult, op1=ALU.max), reads=["t_x"], writes=["t_ax"])
        P.op("act", lambda e: e.activation(out=tmp["e"][:], in_=tmp["ax"][:], func=AF.Exp, scale=-1.0), reads=["t_ax"], writes=["t_e"])
        P.op("act", lambda e: e.activation(out=tmp["l"][:], in_=tmp["e"][:], func=AF.Ln, bias=self.onesf[:, 0:1], scale=1.0),
             reads=["t_e", "onesf"], writes=["t_l"])
        P.op("dve", lambda e: e.scalar_tensor_tensor(out=tmp["l"][:], in0=tmp["x"][:], scalar=0.0, in1=tmp["l"][:],
                                                     op0=ALU.max, op1=ALU.add), reads=["t_x", "t_l"], writes=["t_l"])
        P.op("act", lambda e: e.activation(out=tmp["e"][:, 0, :], in_=rws[:, 256:264], func=AF.Exp), reads=["rws", "t_e"], writes=["t_e"])
        P.op("dve", lambda e: e.scalar_tensor_tensor(out=tmp["g"][:], in0=tmp["l"][:], scalar=-1.0,
                                                     in1=tmp["e"][:, 0:1, :].to_broadcast([128, 34, 8]), op0=ALU.mult, op1=ALU.mult),
             reads=["t_l", "t_e"], writes=["t_g"])
        gsp = self.sb("dgsp", [128, 2, 34, 4], F32)
        for d_ in range(2):
            P.op("dve", lambda e, d_=d_: e.tensor_copy(out=gsp[:, d_, :, :], in_=tmp["g"][:, :, d_ * 4:(d_ + 1) * 4]),
                 reads=["t_g"], writes=["dgsp"])
        P.op("pe", lambda e: e.matmul(out=banks[4][:, 0:136], lhsT=cst[:, 1, :], rhs=gsp[:, 0, :, :].rearrange("p t n -> p (t n)"),
                                      start=True, stop=True), reads=["cst", "dgsp"], writes=["bank4"])
        P.op("pe", lambda e: e.matmul(out=banks[5][:, 0:136], lhsT=cst[:, 2, :], rhs=gsp[:, 1, :, :].rearrange("p t n -> p (t n)"),
                                      start=True, stop=True), reads=["cst", "dgsp"], writes=["bank5"])
        P.op("pe", lambda e: e.matmul(out=banks[6][:, 0:272], lhsT=self.onesf[:], rhs=tmp["g"][:].rearrange("p t n -> p (t n)"),
                                      start=True, stop=True), reads=["onesf", "t_g"], writes=["bank6"])
        P.op("dve", lambda e: e.tensor_copy(out=sc["gc"][:, :, 0:4], in_=banks[4][:, 0:136].rearrange("p (t n) -> p t n", n=4)),
             reads=["bank4"], writes=["s_gc"])
        P.op("dve", lambda e: e.tensor_copy(out=sc["gc"][:, :, 4:8], in_=banks[5][:, 0:136].rearrange("p (t n) -> p t n", n=4)),
             reads=["bank5"], writes=["s_gc"])
        P.op("dve", lambda e: e.tensor_copy(out=tmp["gl"][:], in_=banks[6][:, 0:272].rearrange("p (t n) -> p t n", n=8)),
             reads=["bank6"], writes=["t_gl"])
        P.op("dve", lambda e: e.tensor_scalar(out=sc["ngc"][:], in0=sc["gc"][:], scalar1=-1.0, scalar2=None, op0=ALU.mult),
             reads=["s_gc"], writes=["s_ngc"])
        P.op("dve", lambda e: e.tensor_tensor(out=sc["gcl"][:], in0=sc["gc"][:], in1=tmp["lnb"][:], op=ALU.add),
             reads=["s_gc", "t_lnb"], writes=["s_gcl"])
        P.op("act", lambda e: e.activation(out=tmp["e"][:], in_=sc["gc"][:], func=AF.Exp), reads=["s_gc", "t_e"], writes=["t_e"])
        P.op("dve", lambda e: e.tensor_tensor(out=sc["bet"][:], in0=sc["beta"][:], in1=tmp["e"][:], op=ALU.mult),
             reads=["s_beta", "t_e"], writes=["s_bet"])
        P.op("dve", lambda e: e.tensor_tensor(out=tmp["x"][:], in0=tmp["gl"][:], in1=sc["gc"][:], op=ALU.subtract),
             reads=["t_gl", "s_gc", "t_x"], writes=["t_x"])
        P.op("act", lambda e: e.activation(out=sc["et"][:], in_=tmp["x"][:], func=AF.Exp), reads=["t_x"], writes=["s_et"])
        P.op("act", lambda e: e.activation(out=sc["egl"][:], in_=tmp["gl"][:], func=AF.Exp), reads=["t_gl"], writes=["s_egl"])
        for k_ in ("beta", "gc", "et", "bet", "egl", "gcl"):
            self.tap("sc_" + k_, sc[k_][:].rearrange("p t n -> p (t n)"), [128, 272], ["s_" + k_])
        self.release(m1)


    def dn_main(self, mixT):
        nc, P, banks = self.nc, self.P, self.banks
        cst, rws, identb, identf, sc = self.cst, self.rws, self.identb, self.identf, self.dnsc
        dn_T, dn_zs = self.dn_T, self.dn_zs
        m1 = self.mark()
        o_sb = self.sb("n_o", [128, 32, 4, 128], F32)
        S = self.sb("n_S", [128, 8, 128], F32)
        Sb = self.sb("n_Sb", [128, 8, 128], BF16)
        dnt = [self.sb("n_dnt%d" % i, [128, 12, 128], BF16) for i in range(4)]
        lvm = self.sb("n_lvm", [128, 7, 2, 128], F32)
        identb2 = self.sb("n_idb2", [128, 2, 128], BF16)
        P.dma("sp", "lvm", lvm[:], self.lvmask_in[:, :, :, :], writes=["lvm"])
        for a_ in range(2):
            P.op("dve", lambda e, a_=a_: e.tensor_copy(out=identb2[:, a_, :], in_=identf[:]), reads=["identf"], writes=["identb2"])
        tl = []
        for sl in range(2):
            t = {}
            for nm in ("E", "ET", "ER", "Dm", "Ym", "u"):
                t[nm] = self.sb("n_%s%d" % (nm, sl), [128, 4, 128], F32)
            for nm in ("attnT", "qhT", "ktail", "kbe", "bv", "wTn", "vnew"):
                t[nm] = self.sb("n_%s%d" % (nm, sl), [128, 4, 128], BF16)
            for nm in ("XY", "W", "UD"):
                t[nm] = self.sb("n_%s%d" % (nm, sl), [128, 4, 2, 128], BF16)
            tl.append(t)
        P.op("pool", lambda e: e.memset(S[:], 0.0), writes=["n_S0", "n_S1"])
        P.op("pool", lambda e: e.memset(Sb[:], 0.0), writes=["n_Sb0", "n_Sb1"])
        order_f = [32, 33] + list(range(32))
        order_b = [33, 32] + list(range(31, -1, -1))
        touched = set()
        import os
        nsteps = int(os.environ.get("DN_STEPS", "34"))
        bc3 = lambda ap: ap.unsqueeze(1).to_broadcast([128, 4, 128])

        def group_stages(d, tt):
            lat = tt < 32
            t = tl[d]
            K_ = lambda nm: "g%d_%s" % (d, nm)
            bk = [banks[4 * d + i] for i in range(4)]
            kb = ["bank%d" % (4 * d + i) for i in range(4)]
            ib = self.nxt("n_dnt", 4)
            kd = "n_dnt%d" % ib
            qT = lambda h: dnt[ib][:, h, :]
            kT = lambda h: dnt[ib][:, 4 + h, :]
            vT = lambda h: dnt[ib][:, 8 + h, :]
            scol = lambda nm, h: sc[nm][:, tt, d * 4 + h:d * 4 + h + 1]
            scb = lambda nm: sc[nm][:, tt, d * 4:d * 4 + 4].unsqueeze(2).to_broadcast([128, 4, 128])
            maski, nmask = cst[:, 3 + d, :], cst[:, 5 + d, :]
            b4 = lambda i: bk[i].rearrange("p (h e) -> p h e", h=4)
            kS, kSb = "n_S%d" % d, "n_Sb%d" % d
            st = []

            def s_load():
                for j0 in range(0, 12, 4):
                    P.dma("sp", kd, dnt[ib][:, j0:j0 + 4, :], dn_T[j0:j0 + 4, :, tt * 128:(tt + 1) * 128].rearrange("j p t -> p j t"),
                          writes=[kd])
            st.append(s_load)

            def s_mm1():
                for h in range(4):
                    hs = slice(h * 128, (h + 1) * 128)
                    P.op("pe", lambda e, h=h, hs=hs: e.matmul(out=bk[0][:, hs], lhsT=kT(h), rhs=kT(h), start=True, stop=True), reads=[kd], writes=[kb[0]])
                    P.op("pe", lambda e, h=h, hs=hs: e.matmul(out=bk[1][:, hs], lhsT=kT(h), rhs=qT(h), start=True, stop=True), reads=[kd], writes=[kb[1]])
                    P.op("pe", lambda e, h=h, hs=hs: e.matmul(out=bk[2][:, hs], lhsT=kT(h), rhs=identb[:], start=True, stop=True),
                         reads=[kd, "identb"], writes=[kb[2]])
                    P.op("pe", lambda e, h=h, hs=hs: e.matmul(out=bk[3][:, hs], lhsT=scol("gc", h).to_broadcast([128, 128]), rhs=identf[:],
                                                              start=True, stop=True), reads=["s_gc", "identf"], writes=[kb[3]])
            st.append(s_mm1)

            def s_exp():
                for h in range(4):
                    hs = slice(h * 128, (h + 1) * 128)
                    P.op("act", lambda e, h=h, hs=hs: e.activation(out=t["E"][:, h, :], in_=bk[3][:, hs], func=AF.Exp, bias=scol("ngc", h), scale=1.0),
                         reads=[kb[3], "s_ngc"], writes=[K_("E")])
                    P.op("act", lambda e, h=h, hs=hs: e.activation(out=t["ET"][:, h, :], in_=bk[3][:, hs], func=AF.Exp, bias=scol("gcl", h), scale=-1.0),
                         reads=[kb[3], "s_gcl"], writes=[K_("ET")])
                P.op("act", lambda e: e.activation(out=t["ER"][:], in_=b4(3), func=AF.Exp), reads=[kb[3]], writes=[K_("ER")])
            st.append(s_exp)

            def s_masks():
                P.op("dve", lambda e: e.scalar_tensor_tensor(out=t["Dm"][:], in0=t["E"][:], scalar=1.0, in1=bc3(maski), op0=ALU.min, op1=ALU.mult),
                     reads=[K_("E"), "cst"], writes=[K_("Dm")])
                P.op("dve", lambda e: e.tensor_tensor(out=t["attnT"][:], in0=b4(1), in1=t["Dm"][:], op=ALU.mult),
                     reads=[kb[1], K_("Dm")], writes=[K_("attnT")])
                P.op("dve", lambda e: e.scalar_tensor_tensor(out=t["Ym"][:], in0=t["ET"][:], scalar=1.0, in1=bc3(nmask), op0=ALU.min, op1=ALU.mult),
                     reads=[K_("ET"), "cst"], writes=[K_("Ym")])
                P.op("dve", lambda e: e.tensor_tensor(out=t["XY"][:, :, 1, :], in0=b4(0), in1=t["Ym"][:], op=ALU.mult),
                     reads=[kb[0], K_("Ym")], writes=[K_("XY")])
                P.op("pool", lambda e: e.tensor_tensor(out=t["qhT"][:], in0=dnt[ib][:, 0:4, :], in1=t["ER"][:], op=ALU.mult),
                     reads=[kd, K_("ER")], writes=[K_("qhT")])
                P.op("dve", lambda e: e.tensor_tensor(out=t["ktail"][:], in0=b4(2), in1=scb("et"), op=ALU.mult),
                     reads=[kb[2], "s_et"], writes=[K_("ktail")])
                P.op("dve", lambda e: e.tensor_tensor(out=t["kbe"][:], in0=b4(2), in1=scb("bet"), op=ALU.mult),
                     reads=[kb[2], "s_bet"], writes=[K_("kbe")])
            st.append(s_masks)

            def s_x0():
                for h in range(4):
                    hs = slice(h * 128, (h + 1) * 128)
                    P.op("pe", lambda e, h=h, hs=hs: e.matmul(out=bk[0][:, hs], lhsT=t["XY"][:, h, 1, :], rhs=identb[:], start=True, stop=True),
                         reads=[K_("XY"), "identb"], writes=[kb[0]])
                    P.op("pe", lambda e, h=h, hs=hs: e.matmul(out=bk[3][:, hs], lhsT=vT(h), rhs=identb[:], start=True, stop=True),
                         reads=[kd, "identb"], writes=[kb[3]])
            st.append(s_x0)

            def s_x0e():
                P.op("act", lambda e: e.copy(out=t["XY"][:, :, 0, :], in_=b4(0)), reads=[kb[0]], writes=[K_("XY")])
                P.op("dve", lambda e: e.tensor_tensor(out=t["bv"][:], in0=b4(3), in1=scb("beta"), op=ALU.mult),
                     reads=[kb[3], "s_beta"], writes=[K_("bv")])
                P.op("pool", lambda e: e.tensor_tensor(out=t["W"][:, :, 0, :], in0=t["XY"][:, :, 0, :], in1=bc3(lvm[:, 0, d, :]), op=ALU.mult),
                     reads=[K_("XY"), "lvm"], writes=[K_("W")])
                P.op("pool", lambda e: e.tensor_tensor(out=t["W"][:, :, 1, :], in0=t["XY"][:, :, 1, :], in1=bc3(lvm[:, 0, 1 - d, :]), op=ALU.mult),
                     reads=[K_("XY"), "lvm"], writes=[K_("W")])
                P.op("pool", lambda e: e.tensor_tensor(out=t["UD"][:], in0=t["W"][:], in1=identb2[:].unsqueeze(1).to_broadcast([128, 4, 2, 128]),
                                                      op=ALU.add), reads=[K_("W"), "identb2"], writes=[K_("UD")])
            st.append(s_x0e)

            for l in range(1, 7):
                def s_l1(l=l):
                    for h in range(4):
                        hs = slice(h * 128, (h + 1) * 128)
                        P.op("pe", lambda e, h=h, hs=hs: e.matmul(out=bk[1][:, hs], lhsT=t["XY"][:, h, 1, :], rhs=t["UD"][:, h, 0, :], start=True, stop=True),
                             reads=[K_("XY"), K_("UD")], writes=[kb[1]])
                        P.op("pe", lambda e, h=h, hs=hs: e.matmul(out=bk[2][:, hs], lhsT=t["XY"][:, h, 0, :], rhs=t["UD"][:, h, 1, :], start=True, stop=True),
                             reads=[K_("XY"), K_("UD")], writes=[kb[2]])
                st.append(s_l1)

                def s_l2(l=l):
                    P.op("dve", lambda e: e.tensor_tensor(out=t["W"][:, :, 0, :], in0=b4(1), in1=bc3(lvm[:, l, d, :]), op=ALU.mult),
                         reads=[kb[1], "lvm"], writes=[K_("W")])
                    P.op("dve", lambda e: e.tensor_tensor(out=t["W"][:, :, 1, :], in0=b4(2), in1=bc3(lvm[:, l, 1 - d, :]), op=ALU.mult),
                         reads=[kb[2], "lvm"], writes=[K_("W")])
                st.append(s_l2)

                def s_l3(l=l):
                    for h in range(4):
                        hs = slice(h * 128, (h + 1) * 128)
                        P.op("pe", lambda e, h=h, hs=hs: e.matmul(out=bk[3][:, hs], lhsT=t["UD"][:, h, 1, :], rhs=t["W"][:, h, 0, :], start=True, stop=True),
                             reads=[K_("UD"), K_("W")], writes=[kb[3]])
                        P.op("pe", lambda e, h=h, hs=hs: e.matmul(out=bk[0][:, hs], lhsT=t["UD"][:, h, 0, :], rhs=t["W"][:, h, 1, :], start=True, stop=True),
                             reads=[K_("UD"), K_("W")], writes=[kb[0]])
                st.append(s_l3)

                def s_l4(l=l):
                    P.op("dve", lambda e: e.tensor_tensor(out=t["UD"][:, :, 0, :], in0=b4(3), in1=t["UD"][:, :, 0, :], op=ALU.add),
                         reads=[kb[3], K_("UD")], writes=[K_("UD")])
                    P.op("dve", lambda e: e.tensor_tensor(out=t["UD"][:, :, 1, :], in0=b4(0), in1=t["UD"][:, :, 1, :], op=ALU.add),
                         reads=[kb[0], K_("UD")], writes=[K_("UD")])
                st.append(s_l4)

            def s_uw():
                for h in range(4):
                    hs = slice(h * 128, (h + 1) * 128)
                    P.op("pe", lambda e, h=h, hs=hs: e.matmul(out=bk[1][:, hs], lhsT=t["UD"][:, h, 0, :], rhs=t["bv"][:, h, :], start=True, stop=True),
                         reads=[K_("UD"), K_("bv")], writes=[kb[1]])
                    P.op("pe", lambda e, h=h, hs=hs: e.matmul(out=bk[2][:, hs], lhsT=t["kbe"][:, h, :], rhs=t["UD"][:, h, 0, :], start=True, stop=True),
                         reads=[K_("UD"), K_("kbe")], writes=[kb[2]])
            st.append(s_uw)

            def s_uwe():
                P.op("act", lambda e: e.copy(out=t["u"][:], in_=b4(1)), reads=[kb[1]], writes=[K_("u")])
                P.op("act", lambda e: e.mul(out=t["wTn"][:], in_=b4(2), mul=-1.0), reads=[kb[2]], writes=[K_("wTn")])
            st.append(s_uwe)

            def s_v():
                for h in range(4):
                    hs = slice(h * 128, (h + 1) * 128)
                    P.op("pe", lambda e, h=h, hs=hs: e.matmul(out=bk[3][:, hs], lhsT=t["wTn"][:, h, :], rhs=Sb[:, d * 4 + h, :], start=True, stop=True),
                         reads=[K_("wTn"), kSb], writes=[kb[3]])
            st.append(s_v)

            def s_ve():
                P.op("dve", lambda e: e.tensor_tensor(out=t["vnew"][:], in0=b4(3), in1=t["u"][:], op=ALU.add),
                     reads=[kb[3], K_("u")], writes=[K_("vnew")])
            st.append(s_ve)

            def s_so():
                for h in range(4):
                    hs = slice(h * 128, (h + 1) * 128)
                    P.op("pe", lambda e, h=h, hs=hs: e.matmul(out=bk[0][:, hs], lhsT=t["ktail"][:, h, :], rhs=t["vnew"][:, h, :], start=True, stop=True),
                         reads=[K_("ktail"), K_("vnew")], writes=[kb[0]])
                if lat:
                    for h in range(4):
                        hs = slice(h * 128, (h + 1) * 128)
                        P.op("pe", lambda e, h=h, hs=hs: e.matmul(out=bk[1][:, hs], lhsT=t["qhT"][:, h, :], rhs=Sb[:, d * 4 + h, :], start=True, stop=False),
                             reads=[K_("qhT"), kSb], writes=[kb[1]])
                        P.op("pe", lambda e, h=h, hs=hs: e.matmul(out=bk[1][:, hs], lhsT=t["attnT"][:, h, :], rhs=t["vnew"][:, h, :], start=False, stop=True),
                             reads=[K_("attnT"), K_("vnew")], writes=[kb[1]])
            st.append(s_so)

            def s_upd():
                if lat:
                    ko = ("n_o", tt)
                    if ko not in touched:
                        touched.add(ko)
                        P.op("act", lambda e: e.copy(out=o_sb[:, tt, :, :], in_=b4(1)), reads=[kb[1]], writes=[ko])
                    else:
                        P.op("dve", lambda e: e.tensor_tensor(out=o_sb[:, tt, :, :], in0=b4(1), in1=o_sb[:, tt, :, :], op=ALU.add),
                             reads=[kb[1], ko], writes=[ko])
                P.op("pool", lambda e: e.tensor_tensor(out=S[:, d * 4:d * 4 + 4, :], in0=S[:, d * 4:d * 4 + 4, :], in1=scb("egl"), op=ALU.mult),
                     reads=[kS, "s_egl"], writes=[kS])
                P.op("dve", lambda e: e.tensor_tensor(out=S[:, d * 4:d * 4 + 4, :], in0=b4(0), in1=S[:, d * 4:d * 4 + 4, :], op=ALU.add),
                     reads=[kb[0], kS], writes=[kS])
                P.op("act", lambda e: e.copy(out=Sb[:, d * 4:d * 4 + 4, :], in_=S[:, d * 4:d * 4 + 4, :]), reads=[kS], writes=[kSb])
            st.append(s_upd)
            return st

        for m in range(nsteps):
            sf = group_stages(0, order_f[m])
            sbw = group_stages(1, order_b[m])
            nst_lim = int(os.environ.get("DN_NST", "999"))
            for i_, (f_, b_) in enumerate(zip(sf, sbw)):
                if i_ >= nst_lim:
                    break
                f_()
                b_()
        if "dn_o" in self.taps:
            t_o = self.nc.dram_tensor("dn_o", [128, 32 * 512], F32, kind="ExternalOutput").ap()
            for a0 in range(0, 32, 4):
                P.dma("sp", "tap_dn_o", t_o[:, a0 * 512:(a0 + 4) * 512], o_sb[:, a0:a0 + 4, :, :].rearrange("p a h e -> p (a h e)"),
                      reads=[("n_o", tt_) for tt_ in range(a0, a0 + 4)])
        self.tap("dn_S", S[:].rearrange("p c e -> p (c e)"), [128, 1024], ["n_S0", "n_S1"])
        ss = self.sb("n_ss", [128, 32, 4], F32)
        sqt = [self.sb("n_sqt%d" % i, [128, 4, 128], F32) for i in range(2)]
        zt = [self.sb("n_zt%d" % i, [128, 4, 128], F32) for i in range(2)]
        yb = [self.sb("n_yb%d" % i, [128, 4, 128], BF16) for i in range(2)]
        yo = [self.sb("n_yo%d" % i, [128, 4, 128], BF16) for i in range(2)]
        okeys = lambda tt: [("n_o", tt)]
        for tt in range(32):
            i2 = self.nxt("n_sqt", 2)
            P.op("pool", lambda e, i2=i2, tt=tt: e.tensor_tensor(out=sqt[i2][:], in0=o_sb[:, tt, :, :], in1=o_sb[:, tt, :, :], op=ALU.mult),
                 reads=okeys(tt), writes=["n_sqt%d" % i2])
            P.op("dve", lambda e, i2=i2, tt=tt: e.tensor_reduce(out=ss[:, tt, :], in_=sqt[i2][:], axis=mybir.AxisListType.X, op=ALU.add),
                 reads=["n_sqt%d" % i2], writes=["n_ss"])
        P.op("act", lambda e: e.activation(out=ss[:], in_=ss[:], func=AF.Sqrt, bias=self.epsc[:, 0:1], scale=1.0 / 128),
             reads=["n_ss", "epsc"], writes=["n_ss"])
        P.op("dve", lambda e: e.reciprocal(out=ss[:], in_=ss[:]), reads=["n_ss"], writes=["n_ss"])
        for tt in range(32):
            iz = self.nxt("n_zt", 2)
            P.dma("sp", "n_zt%d" % iz, zt[iz][:], dn_zs[tt * 128:(tt + 1) * 128, :].rearrange("p (h e) -> p h e", h=4), writes=["n_zt%d" % iz])
            i2 = self.nxt("n_sqt", 2)
            P.op("dve", lambda e, i2=i2, tt=tt: e.tensor_tensor(out=sqt[i2][:], in0=o_sb[:, tt, :, :],
                                                               in1=ss[:, tt, :].unsqueeze(2).to_broadcast([128, 4, 128]), op=ALU.mult),
                 reads=okeys(tt) + ["n_ss"], writes=["n_sqt%d" % i2])
            P.op("pool", lambda e, i2=i2: e.tensor_tensor(out=sqt[i2][:], in0=sqt[i2][:],
                                                         in1=rws[:, 272:400].unsqueeze(1).to_broadcast([128, 4, 128]), op=ALU.mult),
                 reads=["n_sqt%d" % i2, "rws"], writes=["n_sqt%d" % i2])
            iy = self.nxt("n_yb", 2)
            P.op("dve", lambda e, i2=i2, iz=iz, iy=iy: e.tensor_tensor(out=yb[iy][:], in0=sqt[i2][:], in1=zt[iz][:], op=ALU.mult),
                 reads=["n_sqt%d" % i2, "n_zt%d" % iz], writes=["n_yb%d" % iy])
            bi = self.nxt("n_tbank", 2)
            for h in range(4):
                P.op("pe", lambda e, bi=bi, iy=iy, h=h: e.matmul(out=banks[bi][:, h * 128:(h + 1) * 128], lhsT=yb[iy][:, h, :], rhs=identb[:],
                                                                 start=True, stop=True), reads=["n_yb%d" % iy, "identb"], writes=["bank%d" % bi])
            io = self.nxt("n_yo", 2)
            P.op("act", lambda e, bi=bi, io=io: e.copy(out=yo[io][:], in_=banks[bi][:, :].rearrange("p (h t) -> p h t", h=4)),
                 reads=["bank%d" % bi], writes=["n_yo%d" % io])
            P.dma("pool", "n_yo%d" % io, mixT[0:512, tt * 128:(tt + 1) * 128].rearrange("(h p) t -> p h t", p=128), yo[io][:],
                  reads=["n_yo%d" % io], writes=[("mixT", h_, tt // 4) for h_ in range(4)])
        self.release(m1)

    def attention(self, at_qT, at_kT, at_v, mixT, negm):
        nc, P, banks = self.nc, self.P, self.banks
        m1 = self.mark()
        kT = self.sb("kkT", [128, 2, NT], BF16)
        vt = self.sb("kvt", [128, 34, 256], BF16)
        qs = [self.sb("kqs%d" % i, [128, 512], BF16) for i in range(2)]
        pT = [self.sb("kpT%d" % i, [128, 512], BF16) for i in range(3)]
        rl = [self.sb("krl%d" % i, [128, 512], F32) for i in range(2)]
        ot = [self.sb("kot%d" % i, [128, 512], BF16) for i in range(2)]
        P.dma("sp", "kkT", kT[:], at_kT.rearrange("g p t -> p g t"), writes=["kkT"])
        for a0 in range(0, 34, 6):
            a1 = min(34, a0 + 6)
            P.dma("sp", "kvt", vt[:, a0:a1, :], at_v[a0 * 128:a1 * 128, :].rearrange("(a p) e -> p a e", p=128), writes=["kvt"])
        scale = float(128 ** -0.5)
        import os
        adbg = bool(os.environ.get("ATT_DEBUG"))
        aiters = int(os.environ.get("ATT_ITERS", "1000"))
        acount = 0
        for g in range(2):
            for hh in range(2):
                h = 2 * g + hh
                for qg in range(8):
                    acount += 1
                    if acount > aiters:
                        continue
                    iq = self.nxt("kqs", 2)
                    P.dma("sp", "kqs%d" % iq, qs[iq][:], at_qT[h, :, qg * 512:(qg + 1) * 512], writes=["kqs%d" % iq])
                    par = self.nxt("kacc", 2)
                    OA, LA = banks[6 + par], banks[4 + par]
                    ko, kl = "bank%d" % (6 + par), "bank%d" % (4 + par)

                    def smm(kt, iq=iq, g=g):
                        sbk = kt % 4
                        P.op("pe", lambda e: e.matmul(out=banks[sbk][:, :], lhsT=kT[:, g, kt * 128:(kt + 1) * 128], rhs=qs[iq][:],
                                                      start=True, stop=True),
                             reads=["kkT", "kqs%d" % iq], writes=["bank%d" % sbk])
                    smm(0)
                    for kt in range(34):
                        if kt + 1 < 34:
                            smm(kt + 1)
                        sbk = kt % 4
                        ip = self.nxt("kpT", 3)
                        P.op("act", lambda e, sbk=sbk, ip=ip: e.activation(out=pT[ip][:], in_=banks[sbk][:, :], func=AF.Exp,
                                                                           scale=scale, bias=negm[:, 0:1]),
                             reads=["bank%d" % sbk, "negm"], writes=["kpT%d" % ip])
                        if adbg and kt in (0, 5) and acount == 1:
                            self.tap("pT%d" % kt, pT[ip][:], [128, 512], ["kpT%d" % ip], BF16)
                        P.op("pe", lambda e, kt=kt, ip=ip, OA=OA, g=g: e.matmul(
                            out=OA[:, :], lhsT=vt[:, kt, g * 128:(g + 1) * 128], rhs=pT[ip][:], start=(kt == 0), stop=(kt == 33)),
                            reads=["kvt", "kpT%d" % ip], writes=[ko])
                        P.op("pe", lambda e, kt=kt, ip=ip, LA=LA: e.matmul(
                            out=LA[:, :], lhsT=self.onesb[:], rhs=pT[ip][:], start=(kt == 0), stop=(kt == 33)),
                            reads=["onesb", "kpT%d" % ip], writes=[kl])
                    ir = self.nxt("krl", 2)
                    P.op("dve", lambda e, ir=ir, LA=LA: e.reciprocal(out=rl[ir][:], in_=LA[:, :]), reads=[kl], writes=["krl%d" % ir])
                    if adbg and acount == 1:
                        self.tap("rl", rl[ir][:], [128, 512], ["krl%d" % ir])
                    P.op("dve", lambda e, ir=ir, OA=OA: e.tensor_tensor(out=ot[ir][:], in0=OA[:, :], in1=rl[ir][:], op=ALU.mult),
                         reads=[ko, "krl%d" % ir], writes=["kot%d" % ir])
                    P.dma("pool", "kot%d" % ir, mixT[(4 + h) * 128:(5 + h) * 128, qg * 512:(qg + 1) * 512], ot[ir][:],
                          reads=["kot%d" % ir], writes=[("mixT", 4 + h, qg)])
        self.release(m1)

    def final_norm(self, out, gv):
        nc, P, banks, xT = self.nc, self.P, self.banks, self.xT
        identf = self.identf
        m0 = self.mark()
        xg = [self.sb("fxg%d" % i, [128, 8, 512], F32) for i in range(2)]
        sq = self.sb("fsq", [128, 8, 512], BF16)
        rs = [self.sb("frs%d" % i, [128, 512], F32) for i in range(2)]
        yo = [self.sb("fyo%d" % i, [128, 4, D], F32) for i in range(2)]
        for g in range(8):
            n = 512
            s = self.nxt("fxg", 2)
            P.dma("sp", "fxg%d" % s, xg[s][:], xT[:, g * 512:(g + 1) * 512].rearrange("(c p) t -> p c t", p=128),
                  reads=[("xT", g)], writes=["fxg%d" % s])
            bi = self.nxt("fstbank", 2)
            r = self.nxt("frs", 2)
            self.rms_stats((xg[s], "fxg%d" % s), n, (sq, "fsq"), "bank%d" % bi, banks[bi], (rs[r], "frs%d" % r))
            for c in range(8):
                P.op("dve", lambda e, s=s, r=r, c=c: e.scalar_tensor_tensor(
                    out=xg[s][:, c, :], in0=xg[s][:, c, :], scalar=gv[:, 3, c:c + 1], in1=rs[r][:, :],
                    op0=ALU.mult, op1=ALU.mult), reads=["fxg%d" % s, "frs%d" % r, "gv"], writes=["fxg%d" % s])
            so = self.nxt("fyo", 2)
            for a in range(4):
                for half in range(2):
                    bi2 = 2 + self.nxt("fobank", 4)
                    bk = banks[bi2]
                    for cc in range(4):
                        c = half * 4 + cc
                        P.op("pe", lambda e, s=s, a=a, c=c, cc=cc, bk=bk: e.transpose(
                            out=bk[:, cc * 128:(cc + 1) * 128], in_=xg[s][:, c, a * 128:(a + 1) * 128], identity=identf[:]),
                            reads=["fxg%d" % s, "identf"], writes=["bank%d" % bi2])
                    if (a + half) % 2 == 0:
                        P.op("act", lambda e, so=so, a=a, half=half, bk=bk: e.copy(
                            out=yo[so][:, a, half * 512:(half + 1) * 512], in_=bk[:, :]),
                            reads=["bank%d" % bi2], writes=["fyo%d" % so])
                    else:
                        P.op("dve", lambda e, so=so, a=a, half=half, bk=bk: e.tensor_copy(
                            out=yo[so][:, a, half * 512:(half + 1) * 512], in_=bk[:, :]),
                            reads=["bank%d" % bi2], writes=["fyo%d" % so])
            P.dma("pool", "fyo%d" % so, out[g * 512:(g + 1) * 512, :].rearrange("(a p) d -> p a d", p=128), yo[so][:],
                  reads=["fyo%d" % so], writes=[("out", g)])
        self.release(m0)


def _col(v):
    v = np.asarray(v, np.float32).reshape(-1, 128)
    return np.ascontiguousarray(v.T)


def make_inputs(inputs, core):
    b = core % 4
    f = lambda a: np.ascontiguousarray(np.asarray(a, np.float32))
    cc = np.stack([np.asarray(inputs["c"])[b], np.asarray(inputs["c_ctx"])], axis=1).astype(np.float32)
    ccT = np.ascontiguousarray(cc.reshape(8, 128, 2).transpose(1, 0, 2))
    gvec = np.stack([_col(inputs["g_ffn1"][0]), _col(inputs["g_mix"][0]), _col(inputs["g_ffn2"][0]),
                     _col(inputs["g_final"])], axis=1)
    m = {
        "x": f(inputs["x"][b]),
        "ctx": f(inputs["ctx"][b]),
        "ccT": ccT,
        "w_mod": f(inputs["w_mod"][0]),
        "b_modT": _col(inputs["b_mod"][0]),
        "gvec": np.ascontiguousarray(gvec),
        "ffn1_w1": f(inputs["ffn1_w1"][0]), "ffn1_w3": f(inputs["ffn1_w3"][0]), "ffn1_w2": f(inputs["ffn1_w2"][0]),
        "w_in": f(inputs["w_in"][0]), "w_out": f(inputs["w_out"][0]),
        "consts": _CACHE.setdefault("consts", _consts()),
        "lvmask": _CACHE.setdefault("lvmask", _lvmask()),
        "rows": np.ascontiguousarray(np.tile(np.concatenate([
            np.asarray(inputs["q_norm"][0], np.float32), np.asarray(inputs["k_norm"][0], np.float32),
            np.asarray(inputs["dn_a_log"][0], np.float32).reshape(-1),
            np.asarray(inputs["dn_dt_bias"][0], np.float32).reshape(-1),
            np.asarray(inputs["dn_norm"][0], np.float32)])[None, :], (128, 1))),
        "convT": np.ascontiguousarray(np.asarray(inputs["dn_conv"][0], np.float32).reshape(5, 12, 128).transpose(2, 1, 0)),
        "ropeC": _CACHE.setdefault("rope", _rope_tables())[0], "ropeS": _CACHE.setdefault("rope", _rope_tables())[1],
        "ffn2_w1": f(inputs["ffn2_w1"][0]), "ffn2_w3": f(inputs["ffn2_w3"][0]), "ffn2_w2": f(inputs["ffn2_w2"][0]),
    }
    return m


def _consts():
    i = np.arange(128)
    pm = (i[:, None] == (i[None, :] ^ 1)).astype(np.float32)
    tri_le = (i[:, None] <= i[None, :]).astype(np.float32)
    tri_ge = (i[:, None] >= i[None, :]).astype(np.float32)
    maski_f = (i[None, :] >= i[:, None]).astype(np.float32)
    maski_b = (i[None, :] <= i[:, None]).astype(np.float32)
    nmasks_f = -(i[:, None] > i[None, :]).astype(np.float32)
    nmasks_b = -(i[:, None] < i[None, :]).astype(np.float32)
    return np.ascontiguousarray(np.stack([pm, tri_le, tri_ge, maski_f, maski_b, nmasks_f, nmasks_b], axis=1))


def _lvmask():
    i = np.arange(128)
    out = np.zeros((128, 7, 2, 128), np.float32)
    for l in range(7):
        sz = 1 << l
        same = (i[:, None] // (2 * sz)) == (i[None, :] // (2 * sz))
        r2 = ((i[:, None] // sz) % 2) == 1
        c1 = ((i[None, :] // sz) % 2) == 0
        mlow = (same & r2 & c1).astype(np.float32)
        out[:, l, 1, :] = mlow
        out[:, l, 0, :] = mlow.T
    return out


def _rope_tables():
    t = np.arange(T)
    row = (t // 64).astype(np.float32)
    col = (t % 64).astype(np.float32)
    freqs = (1.0 / (np.float32(10000.0) ** (np.arange(0, 64, 2, dtype=np.float32) / np.float32(64)))).astype(np.float32)
    ang = np.concatenate([row[:, None] * freqs[None, :], col[:, None] * freqs[None, :]], axis=1).astype(np.float32)
    cos = np.cos(ang).astype(np.float32)
    sin = np.sin(ang).astype(np.float32)
    C = np.repeat(cos, 2, axis=1).T
    S = np.repeat(sin, 2, axis=1).T.copy()
    S[0::2, :] *= -1.0
    return np.ascontiguousarray(C), np.ascontiguousarray(S)


_CACHE = {}


def kernel(**inputs):
    if "nc" not in _CACHE:
        _CACHE["nc"] = Builder().build()
    nc = _CACHE["nc"]
    in_maps = [make_inputs(inputs, c) for c in range(8)]
    res = run_bass_kernel_spmd(nc, in_maps, core_ids=list(range(8)))
    outs = [res.results[c]["out"] for c in range(4)]
    return np.stack(outs, axis=0).astype(np.float32)
```

```python
import contextlib
import numpy as np
import concourse.bass as bass
import concourse.mybir as mybir
from concourse.bass_utils import run_bass_kernel_spmd

F32 = mybir.dt.float32
BF16 = mybir.dt.bfloat16
AF = mybir.ActivationFunctionType
ALU = mybir.AluOpType

D = 1024
T = 4096
TC = 256
NT = T + TC
FF = 2816
NFF = FF // 128
PIN = 3088
EPS = 1e-6
ENG = ("pe", "act", "dve", "pool", "sp")
import os as _os
DMAQ = {"pool": _os.environ.get("POOLQ", "pool")}


class _Op:
    __slots__ = ("eng", "fn", "deps", "signal", "count", "is_dma", "key")

    def __init__(self, eng, fn, is_dma, key):
        self.eng = eng
        self.fn = fn
        self.deps = set()
        self.signal = False
        self.count = 0
        self.is_dma = is_dma
        self.key = key


class Prog:
    def __init__(self, nc):
        self.nc = nc
        self.ops = []
        self.last_w = {}
        self.readers = {}
        self.key_last = {}
        self.key_n = {}
        self.eng_last = {}

    def _add(self, eng, fn, reads, writes, is_dma=False, key=None):
        op = _Op(eng, fn, is_dma, key)
        deps = op.deps
        for r in reads:
            w = self.last_w.get(r)
            if w is not None:
                deps.add(w)
        for w_ in writes:
            w = self.last_w.get(w_)
            if w is not None:
                deps.add(w)
            for rd in self.readers.get(w_, ()):
                deps.add(rd)
        if is_dma:
            prev = self.key_last.get(key)
            if prev is not None:
                deps.add(prev)
            self.key_last[key] = op
            self.key_n[key] = self.key_n.get(key, 0) + 1
            op.count = 16 * self.key_n[key]
        else:
            self.eng_last[eng] = op
        deps.discard(op)
        for r in reads:
            self.readers.setdefault(r, []).append(op)
        for w_ in writes:
            self.last_w[w_] = op
            self.readers[w_] = []
        self.ops.append(op)
        return op

    def op(self, eng, fn, reads=(), writes=()):
        writes = tuple(writes) + tuple(r for r in reads if isinstance(r, str) and r.startswith("bank") and r not in writes)
        return self._add(eng, fn, tuple(reads), writes)

    def dma(self, q, key, out, in_, reads=(), writes=()):
        q = DMAQ.get(q, q)

        def fn(e):
            return e.dma_start(out=out, in_=in_)
        return self._add(q, fn, tuple(reads), tuple(writes), True, key)

    def barrier(self):
        deps = set(self.eng_last.values()) | set(self.key_last.values())
        for e in ENG:
            op = _Op(e, None, False, None)
            op.deps = set(deps)
            self.ops.append(op)
        self.last_w = {}
        self.readers = {}

    def emit(self):
        nc = self.nc
        ops = self.ops
        for o in ops:
            for d in o.deps:
                if d.is_dma:
                    continue
                if d.eng == "pe" and o.eng == "pe" and not o.is_dma and o.fn is not None:
                    continue
                d.signal = True
        cnt = {e: 0 for e in ENG}
        for o in ops:
            if o.is_dma or o.fn is None:
                continue
            if o.signal:
                cnt[o.eng] += 1
                o.count = cnt[o.eng]
        keys = list(self.key_n.keys())
        with contextlib.ExitStack() as es:
            esem = {e: es.enter_context(nc.semaphore("s_" + e)) for e in ENG}
            ksem = {k: es.enter_context(nc.semaphore("k%d" % i)) for i, k in enumerate(keys)}
            block = es.enter_context(nc.Block())
            streams = {e: [o for o in ops if o.eng == e] for e in ENG}

            def run(e, engobj):
                seen = {}
                for o in streams[e]:
                    need = {}
                    for d in o.deps:
                        if d.is_dma:
                            s = ksem[d.key]
                        else:
                            if d.eng == "pe" and e == "pe" and not o.is_dma and o.fn is not None:
                                continue
                            s = esem[d.eng]
                        v = d.count
                        if need.get(s, 0) < v:
                            need[s] = v
                    for s, v in need.items():
                        if seen.get(s, 0) >= v:
                            continue
                        seen[s] = v
                        engobj.wait_ge(s, v)
                    if o.fn is None:
                        continue
                    ins = o.fn(engobj)
                    if o.is_dma:
                        ins.then_inc(ksem[o.key], 16)
                    elif o.signal:
                        ins.then_inc(esem[e], 1)
                if e == "sp":
                    for k in keys:
                        v = 16 * self.key_n[k]
                        if seen.get(ksem[k], 0) < v:
                            engobj.wait_ge(ksem[k], v)

            @block.sync
            def _(eng):
                run("sp", eng)

            @block.scalar
            def _(eng):
                run("act", eng)

            @block.vector
            def _(eng):
                run("dve", eng)

            @block.gpsimd
            def _(eng):
                run("pool", eng)

            @block.tensor
            def _(eng):
                run("pe", eng)


SB_LO = 16512
SB_HI = 229376


class Builder:
    def __init__(self, taps=()):
        self.nc = bass.Bass("TRN2", target_bir_lowering=False)
        self.P = Prog(self.nc)
        self.off = SB_LO
        self.nalloc = 0
        self.taps = set(taps)
        self.tap_out = {}
        self.es = contextlib.ExitStack()
        self.rot = {}

    def sb(self, name, shape, dt):
        n = int(np.prod(shape[1:])) * (4 if dt == F32 else 2)
        n = (n + 63) // 64 * 64
        assert self.off + n <= SB_HI, ("SBUF overflow", name, self.off, n)
        self.nalloc += 1
        t = self.nc.alloc_sbuf_tensor_at("%s_%d" % (name, self.nalloc), list(shape), dt, offset=self.off)
        self.off += n
        return t

    def mark(self):
        return self.off

    def release(self, m):
        self.P.barrier()
        self.off = m

    def din(self, name, shape, dt=F32):
        return self.nc.dram_tensor(name, list(shape), dt, kind="ExternalInput").ap()

    def dscr(self, name, shape, dt=F32):
        if name in self.taps:
            t = self.nc.dram_tensor(name, list(shape), dt, kind="ExternalOutput").ap()
            self.tap_out[name] = t
            return t
        return self.nc.dram_tensor(name, list(shape), dt).ap()

    def tap(self, name, ap, shape, reads, dt=F32):
        if name not in self.taps:
            return
        t = self.nc.dram_tensor(name, list(shape), dt, kind="ExternalOutput").ap()
        self.tap_out[name] = t
        self.P.dma("sp", "tap_" + name, t, ap, reads=reads)

    def nxt(self, name, n):
        i = self.rot.get(name, 0)
        self.rot[name] = i + 1
        return i % n

    def build(self, stop_after=None):
        nc, P = self.nc, self.P
        es = self.es
        x_in = self.din("x", [T, D])
        ctx_in = self.din("ctx", [TC, D])
        ccT_in = self.din("ccT", [128, 8, 2])
        w_mod = self.din("w_mod", [D, 9 * D])
        b_modT = self.din("b_modT", [128, 72])
        gvec = self.din("gvec", [128, 4, 8])
        f1w1 = self.din("ffn1_w1", [D, FF]); f1w3 = self.din("ffn1_w3", [D, FF]); f1w2 = self.din("ffn1_w2", [FF, D])
        f2w1 = self.din("ffn2_w1", [D, FF]); f2w3 = self.din("ffn2_w3", [D, FF]); f2w2 = self.din("ffn2_w2", [FF, D])
        w_in = self.din("w_in", [D, PIN]); w_out = self.din("w_out", [D, D])
        consts = self.din("consts", [128, 7, 128]); rows = self.din("rows", [128, 400])
        self.convT_in = self.din("convT", [128, 12, 5])
        self.lvmask_in = self.din("lvmask", [128, 7, 2, 128])
        ropeC = self.din("ropeC", [128, T]); ropeS = self.din("ropeS", [128, T])
        hfsel_in = self.din("hfsel", [128, 2])
        out = self.nc.dram_tensor("out", [T // 2, D], F32, kind="ExternalOutput").ap()
        xT = self.dscr("xT", [D, NT])
        self.xT = xT

        self.banks = [es.enter_context(nc.psum_tensor("bank%d" % i, [128, 512], F32))[:, :] for i in range(8)]
        banks = self.banks

        identf = self.sb("identf", [128, 128], F32)
        onesb = self.sb("onesb", [128, 128], BF16)
        epsc = self.sb("epsc", [128, 1], F32)
        self.identf, self.onesb, self.epsc = identf, onesb, epsc
        P.op("pool", lambda e: e.memset(identf[:], 0.0), writes=["identf"])
        P.op("pool", lambda e: e.affine_select(out=identf[:], in_=identf[:], pattern=[[-1, 128]],
                                               compare_op=ALU.not_equal, fill=1.0, base=0, channel_multiplier=1),
             reads=["identf"], writes=["identf"])
        P.op("pool", lambda e: e.memset(onesb[:], 1.0), writes=["onesb"])
        P.op("pool", lambda e: e.memset(epsc[:], EPS), writes=["epsc"])

        hfs = self.sb("hfs", [128, 2], F32)
        self.hfs = hfs
        P.dma("sp", "hfs", hfs[:], hfsel_in[:, :], writes=["hfs"])
        modT = self.sb("modT", [128, 72, 2], F32)
        gv = self.sb("gv", [128, 4, 8], F32)
        vec = self.sb("vec", [128, 3, 3, 2, 8], F32)
        self.vec = vec
        m0 = self.mark()
        ccT = self.sb("ccT", [128, 8, 2], F32)
        bmT = self.sb("bmT", [128, 72], F32)
        wm = [self.sb("wm%d" % i, [128, 4608], F32) for i in range(2)]
        P.dma("sp", "ccT", ccT[:], ccT_in[:, :, :], writes=["ccT"])
        P.dma("sp", "bmT", bmT[:], b_modT[:, :], writes=["bmT"])
        P.dma("sp", "gv", gv[:], gvec[:, :, :], writes=["gv"])
        P.op("act", lambda e: e.activation(out=ccT[:], in_=ccT[:], func=AF.Silu), reads=["ccT"], writes=["ccT"])
        mb = banks[0]
        for k in range(8):
            for h in range(2):
                s = self.nxt("wm", 2)
                P.dma("sp", "wm%d" % s, wm[s][:], w_mod[k * 128:(k + 1) * 128, h * 4608:(h + 1) * 4608],
                      writes=["wm%d" % s])
                for n in range(36):
                    col = (h * 36 + n) * 2
                    P.op("pe", lambda e, s=s, n=n, col=col, k=k, h=h: e.matmul(
                        out=mb[:, col:col + 2], lhsT=wm[s][:, n * 128:(n + 1) * 128], rhs=ccT[:, k, :],
                        start=(k == 0 and h == 0 and n == 0), stop=(k == 7), skip_group_check=True),
                        reads=["wm%d" % s, "ccT"], writes=["bank0"])
        P.op("dve", lambda e: e.tensor_tensor(out=modT[:], in0=mb[:, 0:144].rearrange("p (n r) -> p n r", r=2),
                                              in1=bmT[:].unsqueeze(2).to_broadcast([128, 72, 2]), op=ALU.add),
             reads=["bank0", "bmT"], writes=["modT"])
        for wi, (jshift, jscale, jgate, gsc) in enumerate([(0, 1, 2, 0.5), (3, 4, 5, 1.0), (6, 7, 8, 0.5)]):
            for r in range(2):
                P.op("dve", lambda e, wi=wi, r=r, jscale=jscale: e.scalar_tensor_tensor(
                    out=vec[:, wi, 0, r, :], in0=modT[:, jscale * 8:(jscale + 1) * 8, r], scalar=1.0,
                    in1=gv[:, wi, :], op0=ALU.add, op1=ALU.mult), reads=["modT", "gv"], writes=["vec"])
                P.op("dve", lambda e, wi=wi, r=r, jshift=jshift: e.tensor_copy(
                    out=vec[:, wi, 1, r, :], in_=modT[:, jshift * 8:(jshift + 1) * 8, r]),
                    reads=["modT"], writes=["vec"])
                P.op("dve", lambda e, wi=wi, r=r, jgate=jgate, gsc=gsc: e.tensor_scalar(
                    out=vec[:, wi, 2, r, :], in0=modT[:, jgate * 8:(jgate + 1) * 8, r], scalar1=gsc, scalar2=None,
                    op0=ALU.mult), reads=["modT"], writes=["vec"])
        if "vec" in self.taps:
            tv = self.nc.dram_tensor("vec", [128, 144], F32, kind="ExternalOutput").ap()
            self.tap_out["vec"] = tv
            P.dma("sp", "tapvec", tv[:, :], vec[:].rearrange("p a b c d -> p (a b c d)"), reads=["vec"])
        self.release(m0)

        m0 = self.mark()
        xin = [self.sb("xin%d" % i, [128, 4, D], F32) for i in range(2)]
        xst = [self.sb("xst%d" % i, [128, 8, 512], F32) for i in range(2)]
        for g in range(9):
            n = 512 if g < 8 else 256
            nt = n // 128
            s = self.nxt("xin", 2)
            src = x_in[g * 512:(g + 1) * 512, :] if g < 8 else ctx_in[:, :]
            P.dma("sp", "xin%d" % s, xin[s][:, 0:nt, :], src.rearrange("(a p) d -> p a d", p=128), writes=["xin%d" % s])
            so = self.nxt("xst", 2)
            for c in range(8):
                bi = self.nxt("t0bank", 4)
                bk = banks[bi]
                for a in range(nt):
                    P.op("pe", lambda e, s=s, a=a, c=c, bk=bk: e.transpose(
                        out=bk[:, a * 128:(a + 1) * 128], in_=xin[s][:, a, c * 128:(c + 1) * 128], identity=identf[:]),
                        reads=["xin%d" % s, "identf"], writes=["bank%d" % bi])
                eng = "act" if c % 2 == 0 else "dve"
                if eng == "act":
                    P.op("act", lambda e, so=so, c=c, bk=bk, n=n: e.copy(out=xst[so][:, c, 0:n], in_=bk[:, 0:n]),
                         reads=["bank%d" % bi], writes=["xst%d" % so])
                else:
                    P.op("dve", lambda e, so=so, c=c, bk=bk, n=n: e.tensor_copy(out=xst[so][:, c, 0:n], in_=bk[:, 0:n]),
                         reads=["bank%d" % bi], writes=["xst%d" % so])
            P.dma("pool", "xst%d" % so, xT[:, g * 512:g * 512 + n].rearrange("(c p) t -> p c t", p=128), xst[so][:, :, 0:n],
                  reads=["xst%d" % so], writes=[("xT", g)])
        self.release(m0)
        if stop_after == "t0":
            return self.finish()

        groups = [(g * 512, 512, 0, g) for g in range(8)] + [(T, TC, 1, 8)]
        self.ffn(f1w1, f1w3, f1w2, 0, groups)
        if stop_after == "ffn1":
            return self.finish()
        self.mixer(w_in, w_out, consts, rows, ropeC, ropeS, stop_after=stop_after)
        if stop_after in ("atproj", "dnproj", "attn", "dn", "mixer"):
            return self.finish()
        own_groups = [(g * 512, 512, 0, 100 + g) for g in range(4)]
        self.ffn(f2w1, f2w3, f2w2, 2, own_groups)
        self.final_norm(out, gv)
        return self.finish()

    def build_attn_only(self):
        nc, P, es = self.nc, self.P, self.es
        self.banks = [es.enter_context(nc.psum_tensor("bank%d" % i, [128, 512], F32))[:, :] for i in range(8)]
        at_qT = self.din("at_qT", [4, 128, T], BF16)
        at_kT = self.din("at_kT", [2, 128, NT], BF16)
        at_v = self.din("at_v", [NT, 256], BF16)
        mixT = self.nc.dram_tensor("mixT", [D, T], BF16, kind="ExternalOutput").ap()
        self.onesb = self.sb("onesb", [128, 128], BF16)
        negm = self.sb("negm", [128, 1], F32)
        P.op("pool", lambda e: e.memset(self.onesb[:], 1.0), writes=["onesb"])
        P.op("pool", lambda e: e.memset(negm[:], -12.623587), writes=["negm"])
        import os
        if not os.environ.get("SKIP_ATT"):
            self.attention(at_qT, at_kT, at_v, mixT, negm)
        self.dn_main(mixT)
        if stop_after == "dn":
            return
        return self.finish()

    def finish(self):
        self.P.emit()
        self.es.close()
        return self.nc

    def rms_stats(self, xg, n, sq, bank_key, bk, rs):
        P = self.P
        onesb, epsc = self.onesb, self.epsc
        kx, ksq, krs = xg[1], sq[1], rs[1]
        xg, sq, rs = xg[0], sq[0], rs[0]
        P.op("act", lambda e: e.activation(out=sq[:, :, 0:n], in_=xg[:, :, 0:n], func=AF.Square), reads=[kx], writes=[ksq])
        for c in range(8):
            P.op("pe", lambda e, c=c: e.matmul(out=bk[:, 0:n], lhsT=onesb[:], rhs=sq[:, c, 0:n], start=(c == 0), stop=(c == 7)),
                 reads=[ksq, "onesb"], writes=[bank_key])
        P.op("act", lambda e: e.activation(out=rs[:, 0:n], in_=bk[:, 0:n], func=AF.Sqrt, bias=epsc[:, 0:1], scale=1.0 / D),
             reads=[bank_key, "epsc"], writes=[krs])
        P.op("dve", lambda e: e.reciprocal(out=rs[:, 0:n], in_=rs[:, 0:n]), reads=[krs], writes=[krs])


    def norm_mod(self, wi, groups, loc, hT, xg, sq, rs):
        P, banks, xT, vec = self.P, self.banks, self.xT, self.vec
        for gi, (c0, n, kind, gid) in enumerate(groups):
            s = self.nxt("xg", 2)
            P.dma("sp", "xg%d" % s, xg[s][:, :, 0:n], xT[:, c0:c0 + n].rearrange("(c p) t -> p c t", p=128),
                  reads=[("xT", gid)], writes=["xg%d" % s])
            bi = 4 + self.nxt("stbank", 2)
            r = self.nxt("rs", 2)
            self.rms_stats((xg[s], "xg%d" % s), n, (sq, "sq"), "bank%d" % bi, banks[bi], (rs[r], "rs%d" % r))
            P.op("dve", lambda e, s=s, r=r, n=n: e.tensor_tensor(
                out=xg[s][:, :, 0:n], in0=xg[s][:, :, 0:n], in1=rs[r][:, 0:n].unsqueeze(1).to_broadcast([128, 8, n]),
                op=ALU.mult), reads=["xg%d" % s, "rs%d" % r], writes=["xg%d" % s])
            for c in range(8):
                P.op("act", lambda e, s=s, c=c, n=n, kind=kind, lo=loc[gi]: e.activation(
                    out=hT[:, c, lo:lo + n], in_=xg[s][:, c, 0:n], func=AF.Identity,
                    scale=vec[:, wi, 0, kind, c:c + 1], bias=vec[:, wi, 1, kind, c:c + 1]),
                    reads=["xg%d" % s, "vec"], writes=[("hT", gi)])

    def ffn(self, w1, w3, w2, wi, groups):
        nc, P, banks, xT, vec = self.nc, self.P, self.banks, self.xT, self.vec
        m0 = self.mark()
        ntok = sum(g[1] for g in groups)
        hT = self.sb("hT", [128, 8, ntok], BF16)
        parts = [(0, 6), (6, 12), (12, 17), (17, 22)]
        maxp = 6
        gT = self.sb("gT", [128, maxp, ntok], BF16)
        w2b = self.sb("w2b", [128, maxp, D], BF16)
        wst = [self.sb("wst%d" % i, [128, 8, 128], F32) for i in range(4)]
        w13b = [self.sb("w13b%d" % i, [128, 8, 128], BF16) for i in range(4)]
        xg = [self.sb("xg%d" % i, [128, 8, 512], F32) for i in range(2)]
        sq = self.sb("sq", [128, 8, 512], BF16)
        rs = [self.sb("rs%d" % i, [128, 512], F32) for i in range(2)]
        sa = [self.sb("sa%d" % i, [128, 512], F32) for i in range(2)]
        loc = []
        o = 0
        for (c0, n, kind, gid) in groups:
            loc.append(o)
            o += n
        self.norm_mod(wi, groups, loc, hT, xg, sq, rs)
        for (fa, fb) in parts:
            for f in range(fa, fb):
                wb = []
                for wsrc in (w1, w3):
                    s = self.nxt("wst", 4)
                    P.dma("sp", "wst%d" % s, wst[s][:], wsrc[:, f * 128:(f + 1) * 128].rearrange("(c p) n -> p c n", p=128),
                          writes=["wst%d" % s])
                    b = self.nxt("w13b", 4)
                    P.op("pool", lambda e, s=s, b=b: e.tensor_copy(out=w13b[b][:], in_=wst[s][:]),
                         reads=["wst%d" % s], writes=["w13b%d" % b])
                    wb.append(b)
                for gi, (c0, n, kind, gid) in enumerate(groups):
                    lo = loc[gi]
                    bp = self.nxt("upbank", 2)
                    ba, bb = banks[2 * bp], banks[2 * bp + 1]
                    for (bk, bkey, b) in ((ba, "bank%d" % (2 * bp), wb[0]), (bb, "bank%d" % (2 * bp + 1), wb[1])):
                        for c in range(8):
                            P.op("pe", lambda e, bk=bk, b=b, c=c, lo=lo, n=n: e.matmul(
                                out=bk[:, 0:n], lhsT=w13b[b][:, c, :], rhs=hT[:, c, lo:lo + n], start=(c == 0), stop=(c == 7)),
                                reads=["w13b%d" % b, ("hT", gi)], writes=[bkey])
                    si = self.nxt("sa", 2)
                    P.op("act", lambda e, si=si, ba=ba, n=n: e.activation(out=sa[si][:, 0:n], in_=ba[:, 0:n], func=AF.Silu),
                         reads=["bank%d" % (2 * bp)], writes=["sa%d" % si])
                    P.op("dve", lambda e, si=si, bb=bb, n=n, lo=lo, f=f, fa=fa: e.tensor_tensor(
                        out=gT[:, f - fa, lo:lo + n], in0=sa[si][:, 0:n], in1=bb[:, 0:n], op=ALU.mult),
                        reads=["sa%d" % si, "bank%d" % (2 * bp + 1)], writes=[("gT", f - fa, gi)])
            for f in range(fa, fb):
                for hh in range(2):
                    s = self.nxt("wst", 4)
                    P.dma("sp", "wst%d" % s, wst[s][:].rearrange("p c n -> p (c n)")[:, 0:512],
                          w2[f * 128:(f + 1) * 128, hh * 512:(hh + 1) * 512], writes=["wst%d" % s])
                    P.op("pool", lambda e, s=s, f=f, fa=fa, hh=hh: e.tensor_copy(
                        out=w2b[:, f - fa, hh * 512:(hh + 1) * 512], in_=wst[s][:].rearrange("p c n -> p (c n)")[:, 0:512]),
                        reads=["wst%d" % s], writes=[("w2b", f - fa)])
            for gi, (c0, n, kind, gid) in enumerate(groups):
                lo = loc[gi]
                s = self.nxt("xg", 2)
                P.dma("sp", "xg%d" % s, xg[s][:, :, 0:n], xT[:, c0:c0 + n].rearrange("(c p) t -> p c t", p=128),
                      reads=[("xT", gid)], writes=["xg%d" % s])
                for dc in range(8):
                    bi = 4 + self.nxt("dnbank", 4)
                    bk = banks[bi]
                    for f in range(fa, fb):
                        P.op("pe", lambda e, bk=bk, f=f, fa=fa, dc=dc, lo=lo, n=n: e.matmul(
                            out=bk[:, 0:n], lhsT=w2b[:, f - fa, dc * 128:(dc + 1) * 128], rhs=gT[:, f - fa, lo:lo + n],
                            start=(f == fa), stop=(f == fb - 1)),
                            reads=[("w2b", f - fa), ("gT", f - fa, gi)], writes=["bank%d" % bi])
                    P.op("dve", lambda e, bk=bk, s=s, dc=dc, n=n, kind=kind: e.scalar_tensor_tensor(
                        out=xg[s][:, dc, 0:n], in0=bk[:, 0:n], scalar=vec[:, wi, 2, kind, dc:dc + 1], in1=xg[s][:, dc, 0:n],
                        op0=ALU.mult, op1=ALU.add), reads=["bank%d" % bi, "xg%d" % s, "vec"], writes=["xg%d" % s])
                P.dma("pool", "xgo%d" % s, xT[:, c0:c0 + n].rearrange("(c p) t -> p c t", p=128), xg[s][:, :, 0:n],
                      reads=["xg%d" % s], writes=[("xT", gid)])
        self.release(m0)


    def load_w_chunk(self, src_ap, wst, w13b, nst=4, nb=None, bkey="w13b", bidx=None):
        P = self.P
        s = self.nxt("wst", nst)
        P.dma("sp", "wst%d" % s, wst[s][:], src_ap.rearrange("(c p) n -> p c n", p=128), writes=["wst%d" % s])
        if bidx is None:
            bidx = self.nxt(bkey, nb)
        P.op("pool", lambda e, s=s, b=bidx: e.tensor_copy(out=w13b[b][:], in_=wst[s][:]),
             reads=["wst%d" % s], writes=["%s%d" % (bkey, bidx)])
        return bidx

    def mixer(self, w_in, w_out, consts, rows, ropeC, ropeS, stop_after=None):
        nc, P, banks, xT, vec = self.nc, self.P, self.banks, self.xT, self.vec
        groups = [(g * 512, 512, 0, g) for g in range(8)] + [(T, TC, 1, 8)]
        loc = [g[0] for g in groups]
        at_qT = self.dscr("at_qT", [4, 128, T], BF16)
        at_kT = self.dscr("at_kT", [2, 128, NT], BF16)
        at_v = self.dscr("at_v", [NT, 256], BF16)
        mixT = self.dscr("mixT", [D, T // 2], BF16)
        self.mixT = mixT
        m_top = self.mark()
        cst = self.sb("cst", [128, 7, 128], F32)
        rws = self.sb("rws", [128, 400], F32)
        negm = self.sb("negm", [128, 1], F32)
        identb = self.sb("identb", [128, 128], BF16)
        self.cst, self.rws, self.identb = cst, rws, identb
        self.dnsc = {k_: self.sb("dn_" + k_, [128, 34, 8], F32) for k_ in ("beta", "gc", "ngc", "gcl", "et", "bet", "egl")}
        onesf = self.sb("onesf", [128, 128], F32)
        self.onesf = onesf
        P.op("pool", lambda e: e.memset(onesf[:], 1.0), writes=["onesf"])
        P.dma("sp", "cst", cst[:], consts[:, :, :], writes=["cst"])
        P.dma("sp", "rws", rws[:], rows[:, :], writes=["rws"])
        P.op("dve", lambda e: e.tensor_copy(out=identb[:], in_=self.identf[:]), reads=["identf"], writes=["identb"])
        m_all = self.mark()
        hT = self.sb("mhT", [128, 8, NT], BF16)
        m1 = self.mark()
        xg = [self.sb("mxg%d" % i, [128, 8, 512], F32) for i in range(2)]
        sq = self.sb("msq", [128, 8, 512], BF16)
        rs = [self.sb("mrs%d" % i, [128, 512], F32) for i in range(2)]
        self.norm_mod(1, groups, loc, hT, xg, sq, rs)
        self.release(m1)
        m1 = self.mark()
        wst = [self.sb("awst%d" % i, [128, 8, 128], F32) for i in range(2)]
        wq = [self.sb("awq%d" % i, [128, 8, 128], BF16) for i in range(6)]
        wvs = self.sb("awvs", [128, 8, 256], F32)
        wv = self.sb("awv", [128, 8, 256], BF16)
        gcol = self.sb("agcol", [128, 2], F32)
        tabC = [self.sb("atabC%d" % i, [128, 512], F32) for i in range(2)]
        tabS = [self.sb("atabS%d" % i, [128, 512], F32) for i in range(2)]
        sqb = [self.sb("asqb%d" % i, [128, 512], BF16) for i in range(4)]
        rsb = [self.sb("arsb%d" % i, [128, 512], F32) for i in range(4)]
        qn = [self.sb("aqn%d" % i, [128, 512], F32) for i in range(4)]
        t1 = [self.sb("at1%d" % i, [128, 512], F32) for i in range(4)]
        t2 = [self.sb("at2%d" % i, [128, 512], F32) for i in range(4)]
        qst = [self.sb("aqst%d" % i, [128, 512], BF16) for i in range(4)]
        vst = [self.sb("avst%d" % i, [128, 4, 256], BF16) for i in range(2)]
        for j in range(2):
            P.op("dve", lambda e, j=j: e.tensor_tensor(out=t1[0][:, 0:128], in0=rws[:, j * 128:(j + 1) * 128], in1=self.identf[:],
                                                       op=ALU.mult), reads=["rws", "identf", "at10"], writes=["at10"])
            P.op("dve", lambda e, j=j: e.tensor_reduce(out=gcol[:, j:j + 1], in_=t1[0][:, 0:128], axis=mybir.AxisListType.X,
                                                       op=ALU.add), reads=["at10"], writes=["agcol"])
        P.op("dve", lambda e: e.scalar_tensor_tensor(out=t1[1][:, 0:256], in0=rws[:, 0:256], scalar=-1.0, in1=rws[:, 0:256],
                                                     op0=ALU.mult, op1=ALU.max), reads=["rws", "at11"], writes=["at11"])
        P.op("dve", lambda e: e.tensor_reduce(out=t2[0][:, 0:2], in_=t1[1][:, 0:256].rearrange("p (a b) -> p a b", a=2),
                                              axis=mybir.AxisListType.X, op=ALU.max), reads=["at11", "at20"], writes=["at20"])
        P.op("dve", lambda e: e.scalar_tensor_tensor(out=negm[:], in0=t2[0][:, 0:1], scalar=-float(np.sqrt(128.0)),
                                                     in1=t2[0][:, 1:2], op0=ALU.mult, op1=ALU.mult),
             reads=["at20"], writes=["negm"])
        self.tap("negm", negm[:], [128, 1], ["negm"])
        self.tap("gcol", gcol[:], [128, 2], ["agcol"])
        colq = [2064 + h * 128 for h in range(4)] + [2576 + g * 128 for g in range(2)]
        for ch in range(6):
            self.load_w_chunk(w_in[:, colq[ch]:colq[ch] + 128], wst, wq, nst=2, bkey="awq", bidx=ch)
        P.dma("sp", "awvs", wvs[:], w_in[:, 2832:3088].rearrange("(c p) n -> p c n", p=128), writes=["awvs"])
        P.op("pool", lambda e: e.tensor_copy(out=wv[:], in_=wvs[:]), reads=["awvs"], writes=["awv"])
        for gi, (c0, n, kind, gid) in enumerate(groups):
            lat = kind == 0
            if lat:
                tb = self.nxt("atab", 2)
                P.dma("sp", "atabC%d" % tb, tabC[tb][:], ropeC[:, c0:c0 + n], writes=["atabC%d" % tb])
                P.dma("sp", "atabS%d" % tb, tabS[tb][:], ropeS[:, c0:c0 + n], writes=["atabS%d" % tb])
            for ch in range(6):
                if ch < 4 and not lat:
                    continue
                gj = 0 if ch < 4 else 1
                bi = self.nxt("apbank", 3)
                bk = banks[bi]
                for c in range(8):
                    P.op("pe", lambda e, bk=bk, ch=ch, c=c, c0=c0, n=n: e.matmul(
                        out=bk[:, 0:n], lhsT=wq[ch][:, c, :], rhs=hT[:, c, c0:c0 + n], start=(c == 0), stop=(c == 7)),
                        reads=["awq%d" % ch, ("hT", gi)], writes=["bank%d" % bi])
                i2 = self.nxt("asq", 4)
                P.op("act", lambda e, i2=i2, bk=bk, n=n: e.activation(out=sqb[i2][:, 0:n], in_=bk[:, 0:n], func=AF.Square),
                     reads=["bank%d" % bi], writes=["asqb%d" % i2])
                bs = 3 + self.nxt("asbank", 2)
                P.op("pe", lambda e, bs=bs, i2=i2, n=n: e.matmul(out=banks[bs][:, 0:n], lhsT=self.onesb[:], rhs=sqb[i2][:, 0:n],
                                                                 start=True, stop=True),
                     reads=["asqb%d" % i2, "onesb"], writes=["bank%d" % bs])
                P.op("act", lambda e, bs=bs, i2=i2, n=n: e.activation(out=rsb[i2][:, 0:n], in_=banks[bs][:, 0:n], func=AF.Sqrt,
                                                                      bias=self.epsc[:, 0:1], scale=1.0 / 128),
                     reads=["bank%d" % bs, "epsc"], writes=["arsb%d" % i2])
                P.op("dve", lambda e, i2=i2, n=n: e.reciprocal(out=rsb[i2][:, 0:n], in_=rsb[i2][:, 0:n]),
                     reads=["arsb%d" % i2], writes=["arsb%d" % i2])
                P.op("dve", lambda e, i2=i2, bk=bk, gj=gj, n=n: e.scalar_tensor_tensor(
                    out=qn[i2][:, 0:n], in0=bk[:, 0:n], scalar=gcol[:, gj:gj + 1], in1=rsb[i2][:, 0:n],
                    op0=ALU.mult, op1=ALU.mult), reads=["bank%d" % bi, "agcol", "arsb%d" % i2], writes=["aqn%d" % i2])
                so = self.nxt("aqst", 4)
                if lat:
                    br = 5 + self.nxt("arbank", 2)
                    P.op("pe", lambda e, br=br, i2=i2, n=n: e.matmul(out=banks[br][:, 0:n], lhsT=cst[:, 0, :], rhs=qn[i2][:, 0:n],
                                                                     start=True, stop=True),
                         reads=["cst", "aqn%d" % i2], writes=["bank%d" % br])
                    P.op("pool", lambda e, i2=i2, tb=tb, n=n: e.tensor_tensor(out=t1[i2][:, 0:n], in0=qn[i2][:, 0:n],
                                                                             in1=tabC[tb][:, 0:n], op=ALU.mult),
                         reads=["aqn%d" % i2, "atabC%d" % tb], writes=["at1%d" % i2])
                    P.op("dve", lambda e, i2=i2, tb=tb, br=br, n=n: e.tensor_tensor(out=t2[i2][:, 0:n], in0=banks[br][:, 0:n],
                                                                                    in1=tabS[tb][:, 0:n], op=ALU.mult),
                         reads=["bank%d" % br, "atabS%d" % tb], writes=["at2%d" % i2])
                    P.op("pool", lambda e, i2=i2, so=so, n=n: e.tensor_tensor(out=qst[so][:, 0:n], in0=t1[i2][:, 0:n],
                                                                             in1=t2[i2][:, 0:n], op=ALU.add),
                         reads=["at1%d" % i2, "at2%d" % i2], writes=["aqst%d" % so])
                else:
                    P.op("pool", lambda e, i2=i2, so=so, n=n: e.tensor_copy(out=qst[so][:, 0:n], in_=qn[i2][:, 0:n]),
                         reads=["aqn%d" % i2], writes=["aqst%d" % so])
                dst = at_qT[ch, :, c0:c0 + n] if ch < 4 else at_kT[ch - 4, :, c0:c0 + n]
                P.dma("pool", "aqst%d" % so, dst, qst[so][:, 0:n], reads=["aqst%d" % so],
                      writes=[("at_q", ch, gi)])
            nt = n // 128
            sv = self.nxt("avst", 2)
            for a in range(nt):
                bv = 7
                for c in range(8):
                    P.op("pe", lambda e, bv=bv, a=a, c=c, c0=c0: e.matmul(
                        out=banks[bv][:, (a % 2) * 256:(a % 2) * 256 + 256], lhsT=hT[:, c, c0 + a * 128:c0 + (a + 1) * 128],
                        rhs=wv[:, c, :], start=(c == 0), stop=(c == 7)),
                        reads=["awv", ("hT", gi)], writes=["bank%d" % bv])
                if a % 2 == 1 or a == nt - 1:
                    a0 = a - (a % 2)
                    na = a - a0 + 1
                    P.op("act", lambda e, bv=bv, sv=sv, a0=a0, na=na: e.copy(
                        out=vst[sv][:, a0:a0 + na, :], in_=banks[bv][:, 0:na * 256].rearrange("p (a e) -> p a e", e=256)),
                        reads=["bank%d" % bv], writes=["avst%d" % sv])
            P.dma("pool", "avst%d" % sv, at_v[c0:c0 + n, :].rearrange("(a p) e -> p a e", p=128), vst[sv][:, 0:nt, :],
                  reads=["avst%d" % sv], writes=[("at_v", gi)])
        self.release(m1)
        if stop_after == "atproj":
            self.release(m_all)
            return
        self.dn_proj(w_in, hT, groups)
        self.release(m_all)
        if stop_after == "dnproj":
            return

        import os
        if not os.environ.get("SKIP_ATT"):
            self.attention(at_qT, at_kT, at_v, mixT, negm)
        self.dn_main(mixT)
        if stop_after == "dn":
            return
        if stop_after == "attn":
            return

        m1 = self.mark()
        wos = [self.sb("owos%d" % i, [128, D], F32) for i in range(2)]
        wo = self.sb("owo", [128, 8, D], BF16)
        mt = [self.sb("omt%d" % i, [128, 8, 512], BF16) for i in range(2)]
        xg = [self.sb("oxg%d" % i, [128, 8, 512], F32) for i in range(2)]
        for mc in range(8):
            s_ = self.nxt("owos", 2)
            P.dma("sp", "owos%d" % s_, wos[s_][:], w_out[mc * 128:(mc + 1) * 128, :], writes=["owos%d" % s_])
            P.op("pool", lambda e, s_=s_, mc=mc: e.tensor_copy(out=wo[:, mc, :], in_=wos[s_][:]),
                 reads=["owos%d" % s_], writes=["owo"])
        xTo = self.dscr("xTo", [D, T // 2])
        xb = [self.sb("oxb%d" % i, [128, 8, 512], F32) for i in range(2)]
        for g in range(4):
            im = self.nxt("omt", 2)
            P.dma("sp", "omt%d" % im, mt[im][:], mixT[:, g * 512:(g + 1) * 512].rearrange("(c p) t -> p c t", p=128),
                  reads=[("mixT", r_, g) for r_ in range(8)], writes=["omt%d" % im])
            ix = self.nxt("oxg", 2)
            P.dma("sp", "oxg%d" % ix, xg[ix][:], xT[:, g * 512:(g + 1) * 512].rearrange("(c p) t -> p c t", p=128),
                  reads=[("xT", g)], writes=["oxg%d" % ix])
            P.dma("sp", "oxb%d" % ix, xb[ix][:], xT[:, 2048 + g * 512:2048 + (g + 1) * 512].rearrange("(c p) t -> p c t", p=128),
                  reads=[("xT", 4 + g)], writes=["oxb%d" % ix])
            P.op("pool", lambda e, ix=ix: e.tensor_scalar(out=xg[ix][:], in0=xg[ix][:], scalar1=self.hfs[:, 0:1], scalar2=None, op0=ALU.mult),
                 reads=["oxg%d" % ix, "hfs"], writes=["oxg%d" % ix])
            P.op("dve", lambda e, ix=ix: e.scalar_tensor_tensor(out=xg[ix][:], in0=xb[ix][:], scalar=self.hfs[:, 1:2], in1=xg[ix][:],
                                                               op0=ALU.mult, op1=ALU.add),
                 reads=["oxg%d" % ix, "oxb%d" % ix, "hfs"], writes=["oxg%d" % ix])
            for dc in range(8):
                bi = self.nxt("obank", 4)
                for mc in range(8):
                    P.op("pe", lambda e, bi=bi, mc=mc, dc=dc, im=im: e.matmul(
                        out=banks[bi][:, :], lhsT=wo[:, mc, dc * 128:(dc + 1) * 128], rhs=mt[im][:, mc, :],
                        start=(mc == 0), stop=(mc == 7)), reads=["owo", "omt%d" % im], writes=["bank%d" % bi])
                P.op("dve", lambda e, bi=bi, ix=ix, dc=dc: e.scalar_tensor_tensor(
                    out=xg[ix][:, dc, :], in0=banks[bi][:, :], scalar=vec[:, 1, 2, 0, dc:dc + 1], in1=xg[ix][:, dc, :],
                    op0=ALU.mult, op1=ALU.add), reads=["bank%d" % bi, "oxg%d" % ix, "vec"], writes=["oxg%d" % ix])
            P.dma("pool", "oxgo%d" % ix, xTo[:, g * 512:(g + 1) * 512].rearrange("(c p) t -> p c t", p=128), xg[ix][:],
                  reads=["oxg%d" % ix], writes=[("xT", 100 + g)])
        self.xT = xTo
        self.release(m1)
        self.release(m_top)


    def dn_proj(self, w_in, hT, groups):
        nc, P, banks = self.nc, self.P, self.banks
        cst, rws, identb, identf = self.cst, self.rws, self.identb, self.identf
        dn_T = self.dscr("dn_T", [12, 128, NT], BF16)
        dn_zs = self.dscr("dn_zs", [T, 512], F32)
        self.dn_T, self.dn_zs = dn_T, dn_zs
        m1 = self.mark()
        wst = [self.sb("dwst%d" % i, [128, 8, 128], F32) for i in range(2)]
        wq = [self.sb("dwq%d" % i, [128, 8, 128], BF16) for i in range(2)]
        cw = self.sb("dcw", [128, 12, 5], F32)
        dg = [self.sb("ddg%d" % i, [128, 5, 128], BF16) for i in range(2)]
        pre = [self.sb("dpre%d" % i, [128, NT + 6], BF16) for i in range(2)]
        yb = [self.sb("dyb%d" % i, [128, 512], F32) for i in range(4)]
        sqb = [self.sb("dsqb%d" % i, [128, 512], BF16) for i in range(4)]
        rsb = [self.sb("drsb%d" % i, [128, 512], F32) for i in range(4)]
        yst = [self.sb("dyst%d" % i, [128, 512], BF16) for i in range(4)]
        P.dma("sp", "dcw", cw[:], self.convT_in[:, :, :], writes=["dcw"])
        for i in range(2):
            P.op("pool", lambda e, i=i: e.memset(pre[i][:], 0.0), writes=["dpre%d" % i])
        pos = lambda c0: c0 + 2 if c0 < T else c0 + 4
        import os
        dnp = os.environ.get("DNP", "conv,z,sc")
        for j in range(12 if "conv" in dnp else 0):
            wb = self.load_w_chunk(w_in[:, j * 128:(j + 1) * 128], wst, wq, nst=2, nb=2, bkey="dwq")
            di = self.nxt("ddg", 2)
            for k in range(5):
                P.op("dve", lambda e, di=di, k=k, j=j: e.tensor_scalar(out=dg[di][:, k, :], in0=identf[:], scalar1=cw[:, j, k:k + 1],
                                                                     scalar2=None, op0=ALU.mult),
                     reads=["identf", "dcw"], writes=["ddg%d" % di])
            pi = self.nxt("dpre", 2)
            for gi, (c0, n, kind, gid) in enumerate(groups):
                bi = self.nxt("dpbank", 2)
                for c in range(8):
                    P.op("pe", lambda e, bi=bi, wb=wb, c=c, c0=c0, n=n: e.matmul(
                        out=banks[bi][:, 0:n], lhsT=wq[wb][:, c, :], rhs=hT[:, c, c0:c0 + n], start=(c == 0), stop=(c == 7)),
                        reads=["dwq%d" % wb, ("hT", gi)], writes=["bank%d" % bi])
                P.op("act", lambda e, bi=bi, pi=pi, c0=c0, n=n: e.copy(out=pre[pi][:, pos(c0):pos(c0) + n], in_=banks[bi][:, 0:n]),
                     reads=["bank%d" % bi], writes=["dpre%d" % pi])
            for gi, (c0, n, kind, gid) in enumerate(groups):
                bi = (2, 3, 6, 7)[self.nxt("dcbank", 4)]
                for k in range(5):
                    P.op("pe", lambda e, bi=bi, di=di, k=k, pi=pi, c0=c0, n=n: e.matmul(
                        out=banks[bi][:, 0:n], lhsT=dg[di][:, k, :], rhs=pre[pi][:, pos(c0) + k - 2:pos(c0) + k - 2 + n],
                        start=(k == 0), stop=(k == 4)), reads=["ddg%d" % di, "dpre%d" % pi], writes=["bank%d" % bi])
                yi = self.nxt("dyb", 4)
                P.op("act", lambda e, bi=bi, yi=yi, n=n: e.activation(out=yb[yi][:, 0:n], in_=banks[bi][:, 0:n], func=AF.Silu),
                     reads=["bank%d" % bi], writes=["dyb%d" % yi])
                so = self.nxt("dyst", 4)
                if j < 8:
                    P.op("pool", lambda e, yi=yi, n=n: e.tensor_tensor(out=sqb[yi][:, 0:n], in0=yb[yi][:, 0:n], in1=yb[yi][:, 0:n],
                                                                      op=ALU.mult), reads=["dyb%d" % yi], writes=["dsqb%d" % yi])
                    bs = 4 + self.nxt("dsbank", 2)
                    P.op("pe", lambda e, bs=bs, yi=yi, n=n: e.matmul(out=banks[bs][:, 0:n], lhsT=self.onesb[:], rhs=sqb[yi][:, 0:n],
                                                                     start=True, stop=True),
                         reads=["dsqb%d" % yi, "onesb"], writes=["bank%d" % bs])
                    P.op("act", lambda e, bs=bs, yi=yi, n=n: e.activation(out=rsb[yi][:, 0:n], in_=banks[bs][:, 0:n], func=AF.Sqrt,
                                                                          bias=self.epsc[:, 0:1], scale=1.0),
                         reads=["bank%d" % bs, "epsc"], writes=["drsb%d" % yi])
                    P.op("dve", lambda e, yi=yi, n=n: e.reciprocal(out=rsb[yi][:, 0:n], in_=rsb[yi][:, 0:n]),
                         reads=["drsb%d" % yi], writes=["drsb%d" % yi])
                    qsc = float(128 ** -0.5) if j < 4 else 1.0
                    P.op("dve", lambda e, yi=yi, so=so, n=n, qsc=qsc: e.scalar_tensor_tensor(
                        out=yst[so][:, 0:n], in0=yb[yi][:, 0:n], scalar=qsc, in1=rsb[yi][:, 0:n], op0=ALU.mult, op1=ALU.mult),
                        reads=["dyb%d" % yi, "drsb%d" % yi], writes=["dyst%d" % so])
                else:
                    P.op("pool", lambda e, yi=yi, so=so, n=n: e.tensor_copy(out=yst[so][:, 0:n], in_=yb[yi][:, 0:n]),
                         reads=["dyb%d" % yi], writes=["dyst%d" % so])
                P.dma("pool", "dyst%d" % so, dn_T[j, :, c0:c0 + n], yst[so][:, 0:n], reads=["dyst%d" % so], writes=[("dn_T", j, gi)])
        self.release(m1)
        m1 = self.mark()
        wzs = self.sb("dwzs", [128, 8, 512], F32)
        wz = self.sb("dwz", [128, 8, 512], BF16)
        wbas = self.sb("dwbas", [128, 8, 16], F32)
        wba = self.sb("dwba", [128, 8, 16], BF16)
        zst = [self.sb("dzst%d" % i, [128, 512], F32) for i in range(2)]
        ba = self.sb("dba", [128, 34, 16], F32)
        tmp = {k_: self.sb("dt_" + k_, [128, 34, 8], F32) for k_ in ("x", "ax", "e", "l", "g", "lnb", "gl")}
        for c in range(8):
            P.dma("sp", "dwzs", wzs[:, c, :], w_in[c * 128:(c + 1) * 128, 1536:2048], writes=["dwzs"])
        P.op("pool", lambda e: e.tensor_copy(out=wz[:], in_=wzs[:]), reads=["dwzs"], writes=["dwz"])
        P.dma("sp", "dwbas", wbas[:], w_in[:, 2048:2064].rearrange("(c p) n -> p c n", p=128), writes=["dwbas"])
        P.op("pool", lambda e: e.tensor_copy(out=wba[:], in_=wbas[:]), reads=["dwbas"], writes=["dwba"])
        for tt in range(34 if "z" in dnp else 0):
            c0 = tt * 128
            gi = min(tt // 4, 8)
            if tt < 32:
                bi = self.nxt("dzbank", 2)
                for c in range(8):
                    P.op("pe", lambda e, bi=bi, c=c, c0=c0: e.matmul(out=banks[bi][:, :], lhsT=hT[:, c, c0:c0 + 128], rhs=wz[:, c, :],
                                                                     start=(c == 0), stop=(c == 7)),
                         reads=["dwz", ("hT", gi)], writes=["bank%d" % bi])
                zi = self.nxt("dzst", 2)
                P.op("act", lambda e, bi=bi, zi=zi: e.activation(out=zst[zi][:], in_=banks[bi][:, :], func=AF.Silu),
                     reads=["bank%d" % bi], writes=["dzst%d" % zi])
                P.dma("pool", "dzst%d" % zi, dn_zs[c0:c0 + 128, :], zst[zi][:], reads=["dzst%d" % zi], writes=[("dn_zs", tt)])
            bb = 2 if tt < 32 else 3
            col = (tt % 32) * 16
            for c in range(8):
                P.op("pe", lambda e, bb=bb, col=col, c=c, c0=c0: e.matmul(out=banks[bb][:, col:col + 16], lhsT=hT[:, c, c0:c0 + 128],
                                                                        rhs=wba[:, c, :], start=(c == 0), stop=(c == 7)),
                     reads=["dwba", ("hT", gi)], writes=["bank%d" % bb])
        P.op("dve", lambda e: e.tensor_copy(out=ba[:, 0:32, :], in_=banks[2][:, :].rearrange("p (t n) -> p t n", n=16)),
             reads=["bank2"], writes=["dba"])
        P.op("dve", lambda e: e.tensor_copy(out=ba[:, 32:34, :], in_=banks[3][:, 0:32].rearrange("p (t n) -> p t n", n=16)),
             reads=["bank3"], writes=["dba"])
        if "sc" not in dnp:
            self.release(m1)
            return
        self.tap("ba", ba[:].rearrange("p t n -> p (t n)"), [128, 544], ["dba"])
        sc = self.dnsc
        rowb = lambda lo: rws[:, lo:lo + 8].unsqueeze(1).to_broadcast([128, 34, 8])
        P.op("act", lambda e: e.activation(out=sc["beta"][:], in_=ba[:, :, 0:8], func=AF.Sigmoid), reads=["dba"], writes=["s_beta"])
        P.op("act", lambda e: e.activation(out=tmp["lnb"][:], in_=sc["beta"][:], func=AF.Ln), reads=["s_beta"], writes=["t_lnb"])
        P.op("dve", lambda e: e.tensor_tensor(out=tmp["x"][:], in0=ba[:, :, 8:16], in1=rowb(264), op=ALU.add),
             reads=["dba", "rws"], writes=["t_x"])
        P.op("dve", lambda e: e.scalar_tensor_tensor(out=tmp["ax"][:], in0=tmp["x"][:], scalar=-1.0, in1=tmp["x"][:],
                                                     op0=ALU.mult, op1=ALU.max), reads=["t_x"], writes=["t_ax"])
        P.op("act", lambda e: e.activation(out=tmp["e"][:], in_=tmp["ax"][:], func=AF.Exp, scale=-1.0), reads=["t_ax"], writes=["t_e"])
        P.op("act", lambda e: e.activation(out=tmp["l"][:], in_=tmp["e"][:], func=AF.Ln, bias=self.onesf[:, 0:1], scale=1.0),
             reads=["t_e", "onesf"], writes=["t_l"])
        P.op("dve", lambda e: e.scalar_tensor_tensor(out=tmp["l"][:], in0=tmp["x"][:], scalar=0.0, in1=tmp["l"][:],
                                                     op0=ALU.max, op1=ALU.add), reads=["t_x", "t_l"], writes=["t_l"])
        P.op("act", lambda e: e.activation(out=tmp["e"][:, 0, :], in_=rws[:, 256:264], func=AF.Exp), reads=["rws", "t_e"], writes=["t_e"])
        P.op("dve", lambda e: e.scalar_tensor_tensor(out=tmp["g"][:], in0=tmp["l"][:], scalar=-1.0,
                                                     in1=tmp["e"][:, 0:1, :].to_broadcast([128, 34, 8]), op0=ALU.mult, op1=ALU.mult),
             reads=["t_l", "t_e"], writes=["t_g"])
        gsp = self.sb("dgsp", [128, 2, 34, 4], F32)
        for d_ in range(2):
            P.op("dve", lambda e, d_=d_: e.tensor_copy(out=gsp[:, d_, :, :], in_=tmp["g"][:, :, d_ * 4:(d_ + 1) * 4]),
                 reads=["t_g"], writes=["dgsp"])
        P.op("pe", lambda e: e.matmul(out=banks[4][:, 0:136], lhsT=cst[:, 1, :], rhs=gsp[:, 0, :, :].rearrange("p t n -> p (t n)"),
                                      start=True, stop=True), reads=["cst", "dgsp"], writes=["bank4"])
        P.op("pe", lambda e: e.matmul(out=banks[5][:, 0:136], lhsT=cst[:, 2, :], rhs=gsp[:, 1, :, :].rearrange("p t n -> p (t n)"),
                                      start=True, stop=True), reads=["cst", "dgsp"], writes=["bank5"])
        P.op("pe", lambda e: e.matmul(out=banks[6][:, 0:272], lhsT=self.onesf[:], rhs=tmp["g"][:].rearrange("p t n -> p (t n)"),
                                      start=True, stop=True), reads=["onesf", "t_g"], writes=["bank6"])
        P.op("dve", lambda e: e.tensor_copy(out=sc["gc"][:, :, 0:4], in_=banks[4][:, 0:136].rearrange("p (t n) -> p t n", n=4)),
             reads=["bank4"], writes=["s_gc"])
        P.op("dve", lambda e: e.tensor_copy(out=sc["gc"][:, :, 4:8], in_=banks[5][:, 0:136].rearrange("p (t n) -> p t n", n=4)),
             reads=["bank5"], writes=["s_gc"])
        P.op("dve", lambda e: e.tensor_copy(out=tmp["gl"][:], in_=banks[6][:, 0:272].rearrange("p (t n) -> p t n", n=8)),
             reads=["bank6"], writes=["t_gl"])
        P.op("dve", lambda e: e.tensor_scalar(out=sc["ngc"][:], in0=sc["gc"][:], scalar1=-1.0, scalar2=None, op0=ALU.mult),
             reads=["s_gc"], writes=["s_ngc"])
        P.op("dve", lambda e: e.tensor_tensor(out=sc["gcl"][:], in0=sc["gc"][:], in1=tmp["lnb"][:], op=ALU.add),
             reads=["s_gc", "t_lnb"], writes=["s_gcl"])
        P.op("act", lambda e: e.activation(out=tmp["e"][:], in_=sc["gc"][:], func=AF.Exp), reads=["s_gc", "t_e"], writes=["t_e"])
        P.op("dve", lambda e: e.tensor_tensor(out=sc["bet"][:], in0=sc["beta"][:], in1=tmp["e"][:], op=ALU.mult),
             reads=["s_beta", "t_e"], writes=["s_bet"])
        P.op("dve", lambda e: e.tensor_tensor(out=tmp["x"][:], in0=tmp["gl"][:], in1=sc["gc"][:], op=ALU.subtract),
             reads=["t_gl", "s_gc", "t_x"], writes=["t_x"])
        P.op("act", lambda e: e.activation(out=sc["et"][:], in_=tmp["x"][:], func=AF.Exp), reads=["t_x"], writes=["s_et"])
        P.op("act", lambda e: e.activation(out=sc["egl"][:], in_=tmp["gl"][:], func=AF.Exp), reads=["t_gl"], writes=["s_egl"])
        for k_ in ("beta", "gc", "et", "bet", "egl", "gcl"):
            self.tap("sc_" + k_, sc[k_][:].rearrange("p t n -> p (t n)"), [128, 272], ["s_" + k_])
        self.release(m1)


    def dn_main(self, mixT):
        nc, P, banks = self.nc, self.P, self.banks
        cst, rws, identb, identf, sc = self.cst, self.rws, self.identb, self.identf, self.dnsc
        dn_T, dn_zs = self.dn_T, self.dn_zs
        m1 = self.mark()
        o_sb = self.sb("n_o", [128, 32, 4, 128], F32)
        S = self.sb("n_S", [128, 8, 128], F32)
        Sb = self.sb("n_Sb", [128, 8, 128], BF16)
        dnt = [self.sb("n_dnt%d" % i, [128, 12, 128], BF16) for i in range(4)]
        lvm = self.sb("n_lvm", [128, 7, 2, 128], F32)
        identb2 = self.sb("n_idb2", [128, 2, 128], BF16)
        P.dma("sp", "lvm", lvm[:], self.lvmask_in[:, :, :, :], writes=["lvm"])
        for a_ in range(2):
            P.op("dve", lambda e, a_=a_: e.tensor_copy(out=identb2[:, a_, :], in_=identf[:]), reads=["identf"], writes=["identb2"])
        tl = []
        for sl in range(2):
            t = {}
            for nm in ("E", "ET", "ER", "Dm", "Ym", "u"):
                t[nm] = self.sb("n_%s%d" % (nm, sl), [128, 4, 128], F32)
            for nm in ("attnT", "qhT", "ktail", "kbe", "bv", "wTn", "vnew"):
                t[nm] = self.sb("n_%s%d" % (nm, sl), [128, 4, 128], BF16)
            for nm in ("XY", "W", "UD"):
                t[nm] = self.sb("n_%s%d" % (nm, sl), [128, 4, 2, 128], BF16)
            tl.append(t)
        P.op("pool", lambda e: e.memset(S[:], 0.0), writes=["n_S0", "n_S1"])
        P.op("pool", lambda e: e.memset(Sb[:], 0.0), writes=["n_Sb0", "n_Sb1"])
        order_f = [32, 33] + list(range(32))
        order_b = [33, 32] + list(range(31, -1, -1))
        touched = set()
        import os
        nsteps = int(os.environ.get("DN_STEPS", "34"))
        bc3 = lambda ap: ap.unsqueeze(1).to_broadcast([128, 4, 128])

        def group_stages(d, tt):
            lat = tt < 32
            t = tl[d]
            K_ = lambda nm: "g%d_%s" % (d, nm)
            bk = [banks[4 * d + i] for i in range(4)]
            kb = ["bank%d" % (4 * d + i) for i in range(4)]
            ib = self.nxt("n_dnt", 4)
            kd = "n_dnt%d" % ib
            qT = lambda h: dnt[ib][:, h, :]
            kT = lambda h: dnt[ib][:, 4 + h, :]
            vT = lambda h: dnt[ib][:, 8 + h, :]
            scol = lambda nm, h: sc[nm][:, tt, d * 4 + h:d * 4 + h + 1]
            scb = lambda nm: sc[nm][:, tt, d * 4:d * 4 + 4].unsqueeze(2).to_broadcast([128, 4, 128])
            maski, nmask = cst[:, 3 + d, :], cst[:, 5 + d, :]
            b4 = lambda i: bk[i].rearrange("p (h e) -> p h e", h=4)
            kS, kSb = "n_S%d" % d, "n_Sb%d" % d
            st = []

            def s_load():
                for j0 in range(0, 12, 4):
                    P.dma("sp", kd, dnt[ib][:, j0:j0 + 4, :], dn_T[j0:j0 + 4, :, tt * 128:(tt + 1) * 128].rearrange("j p t -> p j t"),
                          writes=[kd])
            st.append(s_load)

            def s_mm1():
                for h in range(4):
                    hs = slice(h * 128, (h + 1) * 128)
                    P.op("pe", lambda e, h=h, hs=hs: e.matmul(out=bk[0][:, hs], lhsT=kT(h), rhs=kT(h), start=True, stop=True), reads=[kd], writes=[kb[0]])
                    P.op("pe", lambda e, h=h, hs=hs: e.matmul(out=bk[1][:, hs], lhsT=kT(h), rhs=qT(h), start=True, stop=True), reads=[kd], writes=[kb[1]])
                    P.op("pe", lambda e, h=h, hs=hs: e.matmul(out=bk[2][:, hs], lhsT=kT(h), rhs=identb[:], start=True, stop=True),
                         reads=[kd, "identb"], writes=[kb[2]])
                    P.op("pe", lambda e, h=h, hs=hs: e.matmul(out=bk[3][:, hs], lhsT=scol("gc", h).to_broadcast([128, 128]), rhs=identf[:],
                                                              start=True, stop=True), reads=["s_gc", "identf"], writes=[kb[3]])
            st.append(s_mm1)

            def s_exp():
                for h in range(4):
                    hs = slice(h * 128, (h + 1) * 128)
                    P.op("act", lambda e, h=h, hs=hs: e.activation(out=t["E"][:, h, :], in_=bk[3][:, hs], func=AF.Exp, bias=scol("ngc", h), scale=1.0),
                         reads=[kb[3], "s_ngc"], writes=[K_("E")])
                    P.op("act", lambda e, h=h, hs=hs: e.activation(out=t["ET"][:, h, :], in_=bk[3][:, hs], func=AF.Exp, bias=scol("gcl", h), scale=-1.0),
                         reads=[kb[3], "s_gcl"], writes=[K_("ET")])
                P.op("act", lambda e: e.activation(out=t["ER"][:], in_=b4(3), func=AF.Exp), reads=[kb[3]], writes=[K_("ER")])
            st.append(s_exp)

            def s_masks():
                P.op("dve", lambda e: e.scalar_tensor_tensor(out=t["Dm"][:], in0=t["E"][:], scalar=1.0, in1=bc3(maski), op0=ALU.min, op1=ALU.mult),
                     reads=[K_("E"), "cst"], writes=[K_("Dm")])
                P.op("dve", lambda e: e.tensor_tensor(out=t["attnT"][:], in0=b4(1), in1=t["Dm"][:], op=ALU.mult),
                     reads=[kb[1], K_("Dm")], writes=[K_("attnT")])
                P.op("dve", lambda e: e.scalar_tensor_tensor(out=t["Ym"][:], in0=t["ET"][:], scalar=1.0, in1=bc3(nmask), op0=ALU.min, op1=ALU.mult),
                     reads=[K_("ET"), "cst"], writes=[K_("Ym")])
                P.op("dve", lambda e: e.tensor_tensor(out=t["XY"][:, :, 1, :], in0=b4(0), in1=t["Ym"][:], op=ALU.mult),
                     reads=[kb[0], K_("Ym")], writes=[K_("XY")])
                P.op("pool", lambda e: e.tensor_tensor(out=t["qhT"][:], in0=dnt[ib][:, 0:4, :], in1=t["ER"][:], op=ALU.mult),
                     reads=[kd, K_("ER")], writes=[K_("qhT")])
                P.op("dve", lambda e: e.tensor_tensor(out=t["ktail"][:], in0=b4(2), in1=scb("et"), op=ALU.mult),
                     reads=[kb[2], "s_et"], writes=[K_("ktail")])
                P.op("dve", lambda e: e.tensor_tensor(out=t["kbe"][:], in0=b4(2), in1=scb("bet"), op=ALU.mult),
                     reads=[kb[2], "s_bet"], writes=[K_("kbe")])
            st.append(s_masks)

            def s_x0():
                for h in range(4):
                    hs = slice(h * 128, (h + 1) * 128)
                    P.op("pe", lambda e, h=h, hs=hs: e.matmul(out=bk[0][:, hs], lhsT=t["XY"][:, h, 1, :], rhs=identb[:], start=True, stop=True),
                         reads=[K_("XY"), "identb"], writes=[kb[0]])
                    P.op("pe", lambda e, h=h, hs=hs: e.matmul(out=bk[3][:, hs], lhsT=vT(h), rhs=identb[:], start=True, stop=True),
                         reads=[kd, "identb"], writes=[kb[3]])
            st.append(s_x0)

            def s_x0e():
                P.op("act", lambda e: e.copy(out=t["XY"][:, :, 0, :], in_=b4(0)), reads=[kb[0]], writes=[K_("XY")])
                P.op("dve", lambda e: e.tensor_tensor(out=t["bv"][:], in0=b4(3), in1=scb("beta"), op=ALU.mult),
                     reads=[kb[3], "s_beta"], writes=[K_("bv")])
                P.op("pool", lambda e: e.tensor_tensor(out=t["W"][:, :, 0, :], in0=t["XY"][:, :, 0, :], in1=bc3(lvm[:, 0, d, :]), op=ALU.mult),
                     reads=[K_("XY"), "lvm"], writes=[K_("W")])
                P.op("pool", lambda e: e.tensor_tensor(out=t["W"][:, :, 1, :], in0=t["XY"][:, :, 1, :], in1=bc3(lvm[:, 0, 1 - d, :]), op=ALU.mult),
                     reads=[K_("XY"), "lvm"], writes=[K_("W")])
                P.op("pool", lambda e: e.tensor_tensor(out=t["UD"][:], in0=t["W"][:], in1=identb2[:].unsqueeze(1).to_broadcast([128, 4, 2, 128]),
                                                      op=ALU.add), reads=[K_("W"), "identb2"], writes=[K_("UD")])
            st.append(s_x0e)

            for l in range(1, 7):
                def s_l1(l=l):
                    for h in range(4):
                        hs = slice(h * 128, (h + 1) * 128)
                        P.op("pe", lambda e, h=h, hs=hs: e.matmul(out=bk[1][:, hs], lhsT=t["XY"][:, h, 1, :], rhs=t["UD"][:, h, 0, :], start=True, stop=True),
                             reads=[K_("XY"), K_("UD")], writes=[kb[1]])
                        P.op("pe", lambda e, h=h, hs=hs: e.matmul(out=bk[2][:, hs], lhsT=t["XY"][:, h, 0, :], rhs=t["UD"][:, h, 1, :], start=True, stop=True),
                             reads=[K_("XY"), K_("UD")], writes=[kb[2]])
                st.append(s_l1)

                def s_l2(l=l):
                    P.op("dve", lambda e: e.tensor_tensor(out=t["W"][:, :, 0, :], in0=b4(1), in1=bc3(lvm[:, l, d, :]), op=ALU.mult),
                         reads=[kb[1], "lvm"], writes=[K_("W")])
                    P.op("dve", lambda e: e.tensor_tensor(out=t["W"][:, :, 1, :], in0=b4(2), in1=bc3(lvm[:, l, 1 - d, :]), op=ALU.mult),
                         reads=[kb[2], "lvm"], writes=[K_("W")])
                st.append(s_l2)

                def s_l3(l=l):
                    for h in range(4):
                        hs = slice(h * 128, (h + 1) * 128)
                        P.op("pe", lambda e, h=h, hs=hs: e.matmul(out=bk[3][:, hs], lhsT=t["UD"][:, h, 1, :], rhs=t["W"][:, h, 0, :], start=True, stop=True),
                             reads=[K_("UD"), K_("W")], writes=[kb[3]])
                        P.op("pe", lambda e, h=h, hs=hs: e.matmul(out=bk[0][:, hs], lhsT=t["UD"][:, h, 0, :], rhs=t["W"][:, h, 1, :], start=True, stop=True),
                             reads=[K_("UD"), K_("W")], writes=[kb[0]])
                st.append(s_l3)

                def s_l4(l=l):
                    P.op("dve", lambda e: e.tensor_tensor(out=t["UD"][:, :, 0, :], in0=b4(3), in1=t["UD"][:, :, 0, :], op=ALU.add),
                         reads=[kb[3], K_("UD")], writes=[K_("UD")])
                    P.op("dve", lambda e: e.tensor_tensor(out=t["UD"][:, :, 1, :], in0=b4(0), in1=t["UD"][:, :, 1, :], op=ALU.add),
                         reads=[kb[0], K_("UD")], writes=[K_("UD")])
                st.append(s_l4)

            def s_uw():
                for h in range(4):
                    hs = slice(h * 128, (h + 1) * 128)
                    P.op("pe", lambda e, h=h, hs=hs: e.matmul(out=bk[1][:, hs], lhsT=t["UD"][:, h, 0, :], rhs=t["bv"][:, h, :], start=True, stop=True),
                         reads=[K_("UD"), K_("bv")], writes=[kb[1]])
                    P.op("pe", lambda e, h=h, hs=hs: e.matmul(out=bk[2][:, hs], lhsT=t["kbe"][:, h, :], rhs=t["UD"][:, h, 0, :], start=True, stop=True),
                         reads=[K_("UD"), K_("kbe")], writes=[kb[2]])
            st.append(s_uw)

            def s_uwe():
                P.op("act", lambda e: e.copy(out=t["u"][:], in_=b4(1)), reads=[kb[1]], writes=[K_("u")])
                P.op("act", lambda e: e.mul(out=t["wTn"][:], in_=b4(2), mul=-1.0), reads=[kb[2]], writes=[K_("wTn")])
            st.append(s_uwe)

            def s_v():
                for h in range(4):
                    hs = slice(h * 128, (h + 1) * 128)
                    P.op("pe", lambda e, h=h, hs=hs: e.matmul(out=bk[3][:, hs], lhsT=t["wTn"][:, h, :], rhs=Sb[:, d * 4 + h, :], start=True, stop=True),
                         reads=[K_("wTn"), kSb], writes=[kb[3]])
            st.append(s_v)

            def s_ve():
                P.op("dve", lambda e: e.tensor_tensor(out=t["vnew"][:], in0=b4(3), in1=t["u"][:], op=ALU.add),
                     reads=[kb[3], K_("u")], writes=[K_("vnew")])
            st.append(s_ve)

            def s_so():
                for h in range(4):
                    hs = slice(h * 128, (h + 1) * 128)
                    P.op("pe", lambda e, h=h, hs=hs: e.matmul(out=bk[0][:, hs], lhsT=t["ktail"][:, h, :], rhs=t["vnew"][:, h, :], start=True, stop=True),
                         reads=[K_("ktail"), K_("vnew")], writes=[kb[0]])
                if lat:
                    for h in range(4):
                        hs = slice(h * 128, (h + 1) * 128)
                        P.op("pe", lambda e, h=h, hs=hs: e.matmul(out=bk[1][:, hs], lhsT=t["qhT"][:, h, :], rhs=Sb[:, d * 4 + h, :], start=True, stop=False),
                             reads=[K_("qhT"), kSb], writes=[kb[1]])
                        P.op("pe", lambda e, h=h, hs=hs: e.matmul(out=bk[1][:, hs], lhsT=t["attnT"][:, h, :], rhs=t["vnew"][:, h, :], start=False, stop=True),
                             reads=[K_("attnT"), K_("vnew")], writes=[kb[1]])
            st.append(s_so)

            def s_upd():
                if lat:
                    ko = ("n_o", tt)
                    if ko not in touched:
                        touched.add(ko)
                        P.op("act", lambda e: e.copy(out=o_sb[:, tt, :, :], in_=b4(1)), reads=[kb[1]], writes=[ko])
                    else:
                        P.op("dve", lambda e: e.tensor_tensor(out=o_sb[:, tt, :, :], in0=b4(1), in1=o_sb[:, tt, :, :], op=ALU.add),
                             reads=[kb[1], ko], writes=[ko])
                P.op("pool", lambda e: e.tensor_tensor(out=S[:, d * 4:d * 4 + 4, :], in0=S[:, d * 4:d * 4 + 4, :], in1=scb("egl"), op=ALU.mult),
                     reads=[kS, "s_egl"], writes=[kS])
                P.op("dve", lambda e: e.tensor_tensor(out=S[:, d * 4:d * 4 + 4, :], in0=b4(0), in1=S[:, d * 4:d * 4 + 4, :], op=ALU.add),
                     reads=[kb[0], kS], writes=[kS])
                P.op("act", lambda e: e.copy(out=Sb[:, d * 4:d * 4 + 4, :], in_=S[:, d * 4:d * 4 + 4, :]), reads=[kS], writes=[kSb])
            st.append(s_upd)
            return st

        for m in range(nsteps):
            sf = group_stages(0, order_f[m])
            sbw = group_stages(1, order_b[m])
            nst_lim = int(os.environ.get("DN_NST", "999"))
            for i_, (f_, b_) in enumerate(zip(sf, sbw)):
                if i_ >= nst_lim:
                    break
                f_()
                b_()
        if "dn_o" in self.taps:
            t_o = self.nc.dram_tensor("dn_o", [128, 32 * 512], F32, kind="ExternalOutput").ap()
            for a0 in range(0, 32, 4):
                P.dma("sp", "tap_dn_o", t_o[:, a0 * 512:(a0 + 4) * 512], o_sb[:, a0:a0 + 4, :, :].rearrange("p a h e -> p (a h e)"),
                      reads=[("n_o", tt_) for tt_ in range(a0, a0 + 4)])
        self.tap("dn_S", S[:].rearrange("p c e -> p (c e)"), [128, 1024], ["n_S0", "n_S1"])
        ss = self.sb("n_ss", [128, 32, 4], F32)
        sqt = [self.sb("n_sqt%d" % i, [128, 4, 128], F32) for i in range(2)]
        zt = [self.sb("n_zt%d" % i, [128, 4, 128], F32) for i in range(2)]
        yb = [self.sb("n_yb%d" % i, [128, 4, 128], BF16) for i in range(2)]
        yo = [self.sb("n_yo%d" % i, [128, 4, 128], BF16) for i in range(2)]
        okeys = lambda tt: [("n_o", tt)]
        for tt in range(32):
            i2 = self.nxt("n_sqt", 2)
            P.op("pool", lambda e, i2=i2, tt=tt: e.tensor_tensor(out=sqt[i2][:], in0=o_sb[:, tt, :, :], in1=o_sb[:, tt, :, :], op=ALU.mult),
                 reads=okeys(tt), writes=["n_sqt%d" % i2])
            P.op("dve", lambda e, i2=i2, tt=tt: e.tensor_reduce(out=ss[:, tt, :], in_=sqt[i2][:], axis=mybir.AxisListType.X, op=ALU.add),
                 reads=["n_sqt%d" % i2], writes=["n_ss"])
        P.op("act", lambda e: e.activation(out=ss[:], in_=ss[:], func=AF.Sqrt, bias=self.epsc[:, 0:1], scale=1.0 / 128),
             reads=["n_ss", "epsc"], writes=["n_ss"])
        P.op("dve", lambda e: e.reciprocal(out=ss[:], in_=ss[:]), reads=["n_ss"], writes=["n_ss"])
        ybp = [self.sb("n_ybp%d" % i, [128, 4, 128], BF16) for i in range(2)]
        for i in range(16):
            for k_, tt in enumerate((i, 16 + i)):
                iz = self.nxt("n_zt", 2)
                P.dma("sp", "n_zt%d" % iz, zt[iz][:], dn_zs[tt * 128:(tt + 1) * 128, :].rearrange("p (h e) -> p h e", h=4), writes=["n_zt%d" % iz])
                i2 = self.nxt("n_sqt", 2)
                P.op("dve", lambda e, i2=i2, tt=tt: e.tensor_tensor(out=sqt[i2][:], in0=o_sb[:, tt, :, :],
                                                                   in1=ss[:, tt, :].unsqueeze(2).to_broadcast([128, 4, 128]), op=ALU.mult),
                     reads=okeys(tt) + ["n_ss"], writes=["n_sqt%d" % i2])
                P.op("pool", lambda e, i2=i2: e.tensor_tensor(out=sqt[i2][:], in0=sqt[i2][:],
                                                             in1=rws[:, 272:400].unsqueeze(1).to_broadcast([128, 4, 128]), op=ALU.mult),
                     reads=["n_sqt%d" % i2, "rws"], writes=["n_sqt%d" % i2])
                P.op("dve", lambda e, i2=i2, iz=iz, k_=k_: e.tensor_tensor(out=ybp[k_][:], in0=sqt[i2][:], in1=zt[iz][:], op=ALU.mult),
                     reads=["n_sqt%d" % i2, "n_zt%d" % iz], writes=["n_ybp%d" % k_])
            iy = self.nxt("n_yb", 2)
            P.op("dve", lambda e: e.tensor_scalar(out=ybp[0][:], in0=ybp[0][:], scalar1=self.hfs[:, 0:1], scalar2=None, op0=ALU.mult),
                 reads=["n_ybp0", "hfs"], writes=["n_ybp0"])
            P.op("dve", lambda e, iy=iy: e.scalar_tensor_tensor(out=yb[iy][:], in0=ybp[1][:], scalar=self.hfs[:, 1:2], in1=ybp[0][:],
                                                               op0=ALU.mult, op1=ALU.add),
                 reads=["n_ybp0", "n_ybp1", "hfs"], writes=["n_yb%d" % iy])
            bi = self.nxt("n_tbank", 2)
            for h in range(4):
                P.op("pe", lambda e, bi=bi, iy=iy, h=h: e.matmul(out=banks[bi][:, h * 128:(h + 1) * 128], lhsT=yb[iy][:, h, :], rhs=identb[:],
                                                                 start=True, stop=True), reads=["n_yb%d" % iy, "identb"], writes=["bank%d" % bi])
            io = self.nxt("n_yo", 2)
            P.op("act", lambda e, bi=bi, io=io: e.copy(out=yo[io][:], in_=banks[bi][:, :].rearrange("p (h t) -> p h t", h=4)),
                 reads=["bank%d" % bi], writes=["n_yo%d" % io])
            P.dma("pool", "n_yo%d" % io, mixT[0:512, i * 128:(i + 1) * 128].rearrange("(h p) t -> p h t", p=128), yo[io][:],
                  reads=["n_yo%d" % io], writes=[("mixT", h_, i // 4) for h_ in range(4)])
        self.release(m1)

    def attention(self, at_qT, at_kT, at_v, mixT, negm):
        nc, P, banks = self.nc, self.P, self.banks
        m1 = self.mark()
        kT = self.sb("kkT", [128, 2, NT], BF16)
        vt = self.sb("kvt", [128, 34, 256], BF16)
        qs = [self.sb("kqs%d" % i, [128, 512], BF16) for i in range(2)]
        qa = [self.sb("kqa%d" % i, [128, 512], BF16) for i in range(2)]
        qb = [self.sb("kqb%d" % i, [128, 512], BF16) for i in range(2)]
        pT = [self.sb("kpT%d" % i, [128, 512], BF16) for i in range(3)]
        rl = [self.sb("krl%d" % i, [128, 512], F32) for i in range(2)]
        ot = [self.sb("kot%d" % i, [128, 512], BF16) for i in range(2)]
        P.dma("sp", "kkT", kT[:], at_kT.rearrange("g p t -> p g t"), writes=["kkT"])
        for a0 in range(0, 34, 6):
            a1 = min(34, a0 + 6)
            P.dma("sp", "kvt", vt[:, a0:a1, :], at_v[a0 * 128:a1 * 128, :].rearrange("(a p) e -> p a e", p=128), writes=["kvt"])
        scale = float(128 ** -0.5)
        import os
        adbg = bool(os.environ.get("ATT_DEBUG"))
        aiters = int(os.environ.get("ATT_ITERS", "1000"))
        acount = 0
        for g in range(2):
            for hh in range(2):
                h = 2 * g + hh
                for qg in range(4):
                    acount += 1
                    if acount > aiters:
                        continue
                    iq = self.nxt("kqs", 2)
                    P.dma("sp", "kqa%d" % iq, qa[iq][:], at_qT[h, :, qg * 512:(qg + 1) * 512], writes=["kqa%d" % iq])
                    P.dma("sp", "kqb%d" % iq, qb[iq][:], at_qT[h, :, 2048 + qg * 512:2048 + (qg + 1) * 512], writes=["kqb%d" % iq])
                    P.op("dve", lambda e, iq=iq: e.tensor_scalar(out=qa[iq][:], in0=qa[iq][:], scalar1=self.hfs[:, 0:1], scalar2=None, op0=ALU.mult),
                         reads=["kqa%d" % iq, "hfs"], writes=["kqa%d" % iq])
                    P.op("dve", lambda e, iq=iq: e.scalar_tensor_tensor(out=qs[iq][:], in0=qb[iq][:], scalar=self.hfs[:, 1:2], in1=qa[iq][:],
                                                                      op0=ALU.mult, op1=ALU.add),
                         reads=["kqa%d" % iq, "kqb%d" % iq, "hfs"], writes=["kqs%d" % iq])
                    par = self.nxt("kacc", 2)
                    OA, LA = banks[6 + par], banks[4 + par]
                    ko, kl = "bank%d" % (6 + par), "bank%d" % (4 + par)

                    def smm(kt, iq=iq, g=g):
                        sbk = kt % 4
                        P.op("pe", lambda e: e.matmul(out=banks[sbk][:, :], lhsT=kT[:, g, kt * 128:(kt + 1) * 128], rhs=qs[iq][:],
                                                      start=True, stop=True),
                             reads=["kkT", "kqs%d" % iq], writes=["bank%d" % sbk])
                    smm(0)
                    for kt in range(34):
                        if kt + 1 < 34:
                            smm(kt + 1)
                        sbk = kt % 4
                        ip = self.nxt("kpT", 3)
                        P.op("act", lambda e, sbk=sbk, ip=ip: e.activation(out=pT[ip][:], in_=banks[sbk][:, :], func=AF.Exp,
                                                                           scale=scale, bias=negm[:, 0:1]),
                             reads=["bank%d" % sbk, "negm"], writes=["kpT%d" % ip])
                        if adbg and kt in (0, 5) and acount == 1:
                            self.tap("pT%d" % kt, pT[ip][:], [128, 512], ["kpT%d" % ip], BF16)
                        P.op("pe", lambda e, kt=kt, ip=ip, OA=OA, g=g: e.matmul(
                            out=OA[:, :], lhsT=vt[:, kt, g * 128:(g + 1) * 128], rhs=pT[ip][:], start=(kt == 0), stop=(kt == 33)),
                            reads=["kvt", "kpT%d" % ip], writes=[ko])
                        P.op("pe", lambda e, kt=kt, ip=ip, LA=LA: e.matmul(
                            out=LA[:, :], lhsT=self.onesb[:], rhs=pT[ip][:], start=(kt == 0), stop=(kt == 33)),
                            reads=["onesb", "kpT%d" % ip], writes=[kl])
                    ir = self.nxt("krl", 2)
                    P.op("dve", lambda e, ir=ir, LA=LA: e.reciprocal(out=rl[ir][:], in_=LA[:, :]), reads=[kl], writes=["krl%d" % ir])
                    if adbg and acount == 1:
                        self.tap("rl", rl[ir][:], [128, 512], ["krl%d" % ir])
                    P.op("dve", lambda e, ir=ir, OA=OA: e.tensor_tensor(out=ot[ir][:], in0=OA[:, :], in1=rl[ir][:], op=ALU.mult),
                         reads=[ko, "krl%d" % ir], writes=["kot%d" % ir])
                    P.dma("pool", "kot%d" % ir, mixT[(4 + h) * 128:(5 + h) * 128, qg * 512:(qg + 1) * 512], ot[ir][:],
                          reads=["kot%d" % ir], writes=[("mixT", 4 + h, qg)])
        self.release(m1)

    def final_norm(self, out, gv):
        nc, P, banks, xT = self.nc, self.P, self.banks, self.xT
        identf = self.identf
        m0 = self.mark()
        xg = [self.sb("fxg%d" % i, [128, 8, 512], F32) for i in range(2)]
        sq = self.sb("fsq", [128, 8, 512], BF16)
        rs = [self.sb("frs%d" % i, [128, 512], F32) for i in range(2)]
        yo = [self.sb("fyo%d" % i, [128, 4, D], F32) for i in range(2)]
        for g in range(4):
            n = 512
            s = self.nxt("fxg", 2)
            P.dma("sp", "fxg%d" % s, xg[s][:], xT[:, g * 512:(g + 1) * 512].rearrange("(c p) t -> p c t", p=128),
                  reads=[("xT", 100 + g)], writes=["fxg%d" % s])
            bi = self.nxt("fstbank", 2)
            r = self.nxt("frs", 2)
            self.rms_stats((xg[s], "fxg%d" % s), n, (sq, "fsq"), "bank%d" % bi, banks[bi], (rs[r], "frs%d" % r))
            for c in range(8):
                P.op("dve", lambda e, s=s, r=r, c=c: e.scalar_tensor_tensor(
                    out=xg[s][:, c, :], in0=xg[s][:, c, :], scalar=gv[:, 3, c:c + 1], in1=rs[r][:, :],
                    op0=ALU.mult, op1=ALU.mult), reads=["fxg%d" % s, "frs%d" % r, "gv"], writes=["fxg%d" % s])
            so = self.nxt("fyo", 2)
            for a in range(4):
                for half in range(2):
                    bi2 = 2 + self.nxt("fobank", 4)
                    bk = banks[bi2]
                    for cc in range(4):
                        c = half * 4 + cc
                        P.op("pe", lambda e, s=s, a=a, c=c, cc=cc, bk=bk: e.transpose(
                            out=bk[:, cc * 128:(cc + 1) * 128], in_=xg[s][:, c, a * 128:(a + 1) * 128], identity=identf[:]),
                            reads=["fxg%d" % s, "identf"], writes=["bank%d" % bi2])
                    if (a + half) % 2 == 0:
                        P.op("act", lambda e, so=so, a=a, half=half, bk=bk: e.copy(
                            out=yo[so][:, a, half * 512:(half + 1) * 512], in_=bk[:, :]),
                            reads=["bank%d" % bi2], writes=["fyo%d" % so])
                    else:
                        P.op("dve", lambda e, so=so, a=a, half=half, bk=bk: e.tensor_copy(
                            out=yo[so][:, a, half * 512:(half + 1) * 512], in_=bk[:, :]),
                            reads=["bank%d" % bi2], writes=["fyo%d" % so])
            P.dma("pool", "fyo%d" % so, out[g * 512:(g + 1) * 512, :].rearrange("(a p) d -> p a d", p=128), yo[so][:],
                  reads=["fyo%d" % so], writes=[("out", g)])
        self.release(m0)


def _col(v):
    v = np.asarray(v, np.float32).reshape(-1, 128)
    return np.ascontiguousarray(v.T)


def make_inputs(inputs, core):
    b = core // 2
    hf = core % 2
    f = lambda a: np.ascontiguousarray(np.asarray(a, np.float32))
    cc = np.stack([np.asarray(inputs["c"])[b], np.asarray(inputs["c_ctx"])], axis=1).astype(np.float32)
    ccT = np.ascontiguousarray(cc.reshape(8, 128, 2).transpose(1, 0, 2))
    gvec = np.stack([_col(inputs["g_ffn1"][0]), _col(inputs["g_mix"][0]), _col(inputs["g_ffn2"][0]),
                     _col(inputs["g_final"])], axis=1)
    m = {
        "x": f(inputs["x"][b]),
        "hfsel": np.ascontiguousarray(np.tile(np.array([[1.0 - hf, float(hf)]], np.float32), (128, 1))),
        "ctx": f(inputs["ctx"][b]),
        "ccT": ccT,
        "w_mod": f(inputs["w_mod"][0]),
        "b_modT": _col(inputs["b_mod"][0]),
        "gvec": np.ascontiguousarray(gvec),
        "ffn1_w1": f(inputs["ffn1_w1"][0]), "ffn1_w3": f(inputs["ffn1_w3"][0]), "ffn1_w2": f(inputs["ffn1_w2"][0]),
        "w_in": f(inputs["w_in"][0]), "w_out": f(inputs["w_out"][0]),
        "consts": _CACHE.setdefault("consts", _consts()),
        "lvmask": _CACHE.setdefault("lvmask", _lvmask()),
        "rows": np.ascontiguousarray(np.tile(np.concatenate([
            np.asarray(inputs["q_norm"][0], np.float32), np.asarray(inputs["k_norm"][0], np.float32),
            np.asarray(inputs["dn_a_log"][0], np.float32).reshape(-1),
            np.asarray(inputs["dn_dt_bias"][0], np.float32).reshape(-1),
            np.asarray(inputs["dn_norm"][0], np.float32)])[None, :], (128, 1))),
        "convT": np.ascontiguousarray(np.asarray(inputs["dn_conv"][0], np.float32).reshape(5, 12, 128).transpose(2, 1, 0)),
        "ropeC": _CACHE.setdefault("rope", _rope_tables())[0], "ropeS": _CACHE.setdefault("rope", _rope_tables())[1],
        "ffn2_w1": f(inputs["ffn2_w1"][0]), "ffn2_w3": f(inputs["ffn2_w3"][0]), "ffn2_w2": f(inputs["ffn2_w2"][0]),
    }
    return m


def _consts():
    i = np.arange(128)
    pm = (i[:, None] == (i[None, :] ^ 1)).astype(np.float32)
    tri_le = (i[:, None] <= i[None, :]).astype(np.float32)
    tri_ge = (i[:, None] >= i[None, :]).astype(np.float32)
    maski_f = (i[None, :] >= i[:, None]).astype(np.float32)
    maski_b = (i[None, :] <= i[:, None]).astype(np.float32)
    nmasks_f = -(i[:, None] > i[None, :]).astype(np.float32)
    nmasks_b = -(i[:, None] < i[None, :]).astype(np.float32)
    return np.ascontiguousarray(np.stack([pm, tri_le, tri_ge, maski_f, maski_b, nmasks_f, nmasks_b], axis=1))


def _lvmask():
    i = np.arange(128)
    out = np.zeros((128, 7, 2, 128), np.float32)
    for l in range(7):
        sz = 1 << l
        same = (i[:, None] // (2 * sz)) == (i[None, :] // (2 * sz))
        r2 = ((i[:, None] // sz) % 2) == 1
        c1 = ((i[None, :] // sz) % 2) == 0
        mlow = (same & r2 & c1).astype(np.float32)
        out[:, l, 1, :] = mlow
        out[:, l, 0, :] = mlow.T
    return out


def _rope_tables():
    t = np.arange(T)
    row = (t // 64).astype(np.float32)
    col = (t % 64).astype(np.float32)
    freqs = (1.0 / (np.float32(10000.0) ** (np.arange(0, 64, 2, dtype=np.float32) / np.float32(64)))).astype(np.float32)
    ang = np.concatenate([row[:, None] * freqs[None, :], col[:, None] * freqs[None, :]], axis=1).astype(np.float32)
    cos = np.cos(ang).astype(np.float32)
    sin = np.sin(ang).astype(np.float32)
    C = np.repeat(cos, 2, axis=1).T
    S = np.repeat(sin, 2, axis=1).T.copy()
    S[0::2, :] *= -1.0
    return np.ascontiguousarray(C), np.ascontiguousarray(S)


_CACHE = {}


def kernel(**inputs):
    if "nc" not in _CACHE:
        _CACHE["nc"] = Builder().build()
    nc = _CACHE["nc"]
    in_maps = [make_inputs(inputs, c) for c in range(8)]
    res = run_bass_kernel_spmd(nc, in_maps, core_ids=list(range(8)))
    out = np.empty((4, T, D), np.float32)
    for c in range(8):
        b, hf = c // 2, c % 2
        out[b, hf * (T // 2):(hf + 1) * (T // 2)] = res.results[c]["out"]
    return out
```
